# Optimizing a Trainium2 kernel written in Bass

```python
import jax, jax.numpy as jnp
from jax import lax
import numpy as np

D_MODEL = 1024
BATCH = 8
SEQ = 2048
DEPTH = 1
DEC_BATCH = 128
DEC_SEQ = 4
PAST_LEN = 16384
PAGE_SIZE = 128

D_CONV = D_MODEL
CONV_K = 31
D_RNN = 5 * D_MODEL // 4
RNN_HEADS = 16
RNN_HEAD_DIM = D_RNN // RNN_HEADS
RNN_CONV_K = 4
RG_C = 8.0
D_FF = 4 * D_MODEL
D_PLE = 256
LN_EPS = 1e-5
DN_ALPHA = (2.0 * DEPTH) ** 0.25
DN_BETA = (8.0 * DEPTH) ** -0.25
OFF_A_VAL = 0
OFF_A_GATE = OFF_A_VAL + D_CONV
OFF_B_X = OFF_A_GATE + D_CONV
OFF_B_GATE = OFF_B_X + D_RNN
OFF_G_A = OFF_B_GATE + D_RNN
OFF_G_B = OFF_G_A + D_MODEL
D_IN = OFF_G_B + D_MODEL

kernel_name = 'conformer_rglru_hybrid_step'


def layer_norm(x, g, b):
    xf = x.astype(jnp.float32)
    mu = jnp.mean(xf, axis=-1, keepdims=True)
    var = jnp.mean(jnp.square(xf - mu), axis=-1, keepdims=True)
    y = (xf - mu) * lax.rsqrt(var + LN_EPS) * g.astype(jnp.float32) + b.astype(jnp.float32)
    return y.astype(x.dtype)


def causal_dwconv(ctx, w, b):
    c = ctx.shape[-1]
    out = lax.conv_general_dilated(ctx, w[:, None, :].astype(ctx.dtype), window_strides=(1,),
                                   padding='VALID', dimension_numbers=('NWC', 'WIO', 'NWC'),
                                   feature_group_count=c)
    return out + b


def rg_lru(xc, h0, w_a, b_a, w_x, b_x, lam, reset_first):
    bn, t = xc.shape[0], xc.shape[1]
    xf = xc.astype(jnp.float32)
    xh = xf.reshape(bn, t, RNN_HEADS, RNN_HEAD_DIM)
    r = jax.nn.sigmoid(jnp.einsum('bthi,hij->bthj', xh, w_a.astype(jnp.float32)) + b_a.astype(jnp.float32))
    ig = jax.nn.sigmoid(jnp.einsum('bthi,hij->bthj', xh, w_x.astype(jnp.float32)) + b_x.astype(jnp.float32))
    r = r.reshape(bn, t, D_RNN)
    ig = ig.reshape(bn, t, D_RNN)
    log_a = -RG_C * r * jax.nn.softplus(-lam.astype(jnp.float32))
    a = jnp.exp(log_a)
    mult = jnp.sqrt(-jnp.expm1(2.0 * log_a))
    if reset_first:
        mult = mult.at[:, 0].set(1.0)
        a0 = jnp.zeros_like(a[:, 0])
    else:
        a0 = a[:, 0]
    bterm = mult * ig * xf
    bterm = bterm.at[:, 0].add(a0 * h0.astype(jnp.float32))

    def combine(left, right):
        a1, b1 = left
        a2, b2 = right
        return a1 * a2, a2 * b1 + b2

    _, h = lax.associative_scan(combine, (a, bterm), axis=1)
    return h, h[:, -1]


def trunk_layer(x, p, ctx_a, ctx_b, h0, reset_first,
                w_in, w_dw_a, b_dw_a, ln_a_g, ln_a_b, w_proj_a,
                w_dw_b, b_dw_b, w_rg_a, b_rg_a, w_rg_x, b_rg_x, rg_lam, w_proj_b,
                w_out, ln1_g, ln1_b, w_ff1, w_ff2, w_ple_gate, w_ple_proj, ln2_g, ln2_b):
    z = x @ w_in
    a_val = z[..., OFF_A_VAL:OFF_A_GATE]
    a_gate = z[..., OFF_A_GATE:OFF_B_X]
    b_x = z[..., OFF_B_X:OFF_B_GATE]
    b_gate = z[..., OFF_B_GATE:OFF_G_A]
    g_a = z[..., OFF_G_A:OFF_G_B]
    g_b = z[..., OFF_G_B:]
    u = a_val * jax.nn.sigmoid(a_gate)
    ua = jnp.concatenate([ctx_a, u], axis=1)
    ca = jax.nn.silu(layer_norm(causal_dwconv(ua, w_dw_a, b_dw_a), ln_a_g, ln_a_b))
    y_a = ca @ w_proj_a
    ub = jnp.concatenate([ctx_b, b_x], axis=1)
    cb = causal_dwconv(ub, w_dw_b, b_dw_b)
    hs, h_last = rg_lru(cb, h0, w_rg_a, b_rg_a, w_rg_x, b_rg_x, rg_lam, reset_first)
    y_b = (hs.astype(x.dtype) * jax.nn.gelu(b_gate)) @ w_proj_b
    mix = (jax.nn.sigmoid(g_a) * y_a + jax.nn.sigmoid(g_b) * y_b) @ w_out
    x1 = layer_norm(DN_ALPHA * x + mix, ln1_g, ln1_b)
    ff = jnp.square(jax.nn.relu(x1 @ w_ff1)) @ w_ff2
    ple = jax.nn.sigmoid(x1 @ w_ple_gate) * (p @ w_ple_proj)
    y = layer_norm(DN_ALPHA * x1 + ff + ple, ln2_g, ln2_b)
    return y, ua[:, -(CONV_K - 1):], ub[:, -(RNN_CONV_K - 1):], h_last.astype(h0.dtype)


def setup_inputs(seed: int = 0) -> dict:
    key = jax.random.key(seed)
    ks = jax.random.split(key, 40)
    f32 = jnp.float32

    def nrm(k, shape, scale):
        return jax.random.normal(k, shape, f32) * scale

    u = jax.random.uniform(ks[0], (DEPTH, D_RNN), f32, minval=0.9, maxval=0.999)
    a_init = u ** (1.0 / RG_C)
    rg_lam = jnp.log(a_init) - jnp.log1p(-a_init)
    return {
        'x_prompt': nrm(ks[1], (BATCH, SEQ, D_MODEL), 1.0),
        'x_sample': nrm(ks[2], (DEC_BATCH, DEC_SEQ, D_MODEL), 1.0),
        'state_conv_a': nrm(ks[3], (DEPTH, DEC_BATCH, CONV_K - 1, D_CONV), 0.5),
        'state_conv_b': nrm(ks[4], (DEPTH, DEC_BATCH, RNN_CONV_K - 1, D_RNN), 1.0),
        'state_h': nrm(ks[5], (DEPTH, DEC_BATCH, D_RNN), 0.5),
        'p_prompt': nrm(ks[6], (DEPTH, BATCH, SEQ, D_PLE), 1.0),
        'p_sample': nrm(ks[7], (DEPTH, DEC_BATCH, DEC_SEQ, D_PLE), 1.0),
        'w_in': nrm(ks[8], (DEPTH, D_MODEL, D_IN), D_MODEL ** -0.5),
        'w_dw_a': nrm(ks[9], (DEPTH, CONV_K, D_CONV), CONV_K ** -0.5),
        'b_dw_a': nrm(ks[10], (DEPTH, D_CONV), 0.01),
        'ln_a_g': 1.0 + nrm(ks[11], (DEPTH, D_CONV), 0.02),
        'ln_a_b': nrm(ks[12], (DEPTH, D_CONV), 0.02),
        'w_proj_a': nrm(ks[13], (DEPTH, D_CONV, D_MODEL), DN_BETA * D_CONV ** -0.5),
        'w_dw_b': nrm(ks[14], (DEPTH, RNN_CONV_K, D_RNN), RNN_CONV_K ** -0.5),
        'b_dw_b': nrm(ks[15], (DEPTH, D_RNN), 0.01),
        'w_rg_a': nrm(ks[16], (DEPTH, RNN_HEADS, RNN_HEAD_DIM, RNN_HEAD_DIM), RNN_HEAD_DIM ** -0.5),
        'b_rg_a': nrm(ks[17], (DEPTH, RNN_HEADS, RNN_HEAD_DIM), 0.01),
        'w_rg_x': nrm(ks[18], (DEPTH, RNN_HEADS, RNN_HEAD_DIM, RNN_HEAD_DIM), RNN_HEAD_DIM ** -0.5),
        'b_rg_x': nrm(ks[19], (DEPTH, RNN_HEADS, RNN_HEAD_DIM), 0.01),
        'rg_lam': rg_lam,
        'w_proj_b': nrm(ks[20], (DEPTH, D_RNN, D_MODEL), DN_BETA * D_RNN ** -0.5),
        'w_out': nrm(ks[21], (DEPTH, D_MODEL, D_MODEL), DN_BETA * D_MODEL ** -0.5),
        'ln1_g': 1.0 + nrm(ks[22], (DEPTH, D_MODEL), 0.02),
        'ln1_b': nrm(ks[23], (DEPTH, D_MODEL), 0.02),
        'w_ff1': nrm(ks[24], (DEPTH, D_MODEL, D_FF), D_MODEL ** -0.5),
        'w_ff2': nrm(ks[25], (DEPTH, D_FF, D_MODEL), DN_BETA * D_FF ** -0.5),
        'w_ple_gate': nrm(ks[26], (DEPTH, D_MODEL, D_MODEL), D_MODEL ** -0.5),
        'w_ple_proj': nrm(ks[27], (DEPTH, D_PLE, D_MODEL), DN_BETA * D_PLE ** -0.5),
        'ln2_g': 1.0 + nrm(ks[28], (DEPTH, D_MODEL), 0.02),
        'ln2_b': nrm(ks[29], (DEPTH, D_MODEL), 0.02),
    }


def reference(x_prompt, x_sample, state_conv_a, state_conv_b, state_h, p_prompt, p_sample,
              w_in, w_dw_a, b_dw_a, ln_a_g, ln_a_b, w_proj_a,
              w_dw_b, b_dw_b, w_rg_a, b_rg_a, w_rg_x, b_rg_x, rg_lam, w_proj_b,
              w_out, ln1_g, ln1_b, w_ff1, w_ff2, w_ple_gate, w_ple_proj, ln2_g, ln2_b):
    yp, ys = x_prompt, x_sample
    bp = x_prompt.shape[0]
    ca_p, cb_p, h_p, ca_s, cb_s, h_s = [], [], [], [], [], []
    for i in range(DEPTH):
        lw = (w_in[i], w_dw_a[i], b_dw_a[i], ln_a_g[i], ln_a_b[i], w_proj_a[i],
              w_dw_b[i], b_dw_b[i], w_rg_a[i], b_rg_a[i], w_rg_x[i], b_rg_x[i], rg_lam[i], w_proj_b[i],
              w_out[i], ln1_g[i], ln1_b[i], w_ff1[i], w_ff2[i], w_ple_gate[i], w_ple_proj[i],
              ln2_g[i], ln2_b[i])
        zero_a = jnp.zeros((bp, CONV_K - 1, D_CONV), yp.dtype)
        zero_b = jnp.zeros((bp, RNN_CONV_K - 1, D_RNN), yp.dtype)
        zero_h = jnp.zeros((bp, D_RNN), state_h.dtype)
        yp, na_p, nb_p, nh_p = trunk_layer(yp, p_prompt[i], zero_a, zero_b, zero_h, True, *lw)
        ys, na_s, nb_s, nh_s = trunk_layer(ys, p_sample[i], state_conv_a[i], state_conv_b[i],
                                           state_h[i], False, *lw)
        ca_p.append(na_p)
        cb_p.append(nb_p)
        h_p.append(nh_p)
        ca_s.append(na_s)
        cb_s.append(nb_s)
        h_s.append(nh_s)
    return (yp, ys, jnp.stack(ca_p), jnp.stack(cb_p), jnp.stack(h_p),
            jnp.stack(ca_s), jnp.stack(cb_s), jnp.stack(h_s))
```

```python
import numpy as np
from contextlib import ExitStack
import concourse.bass as bass
import concourse.mybir as mybir
from concourse.bass_utils import run_bass_kernel_spmd

F32 = mybir.dt.float32
BF16 = mybir.dt.bfloat16
AF = mybir.ActivationFunctionType
ALU = mybir.AluOpType

NCORES = 8
D = 1024
DR = 1280
DFF = 4096
DPL = 256
DIN = 6656
SEQ = 2048
NT = 512
KA = 31
KB = 4
ALPHA = 2.0 ** 0.25
EPS = 1e-5
GK = 0.7978845608028654
C_BDA, C_LAG, C_LAB, C_BDB, C_BRA, C_BRX, C_LAM, C_L1G, C_L1B, C_WDA, C_WDB = 0, 8, 16, 24, 34, 44, 54, 64, 72, 80, 328
NCOL = 368
V_HBRA, V_HBRX, V_C, V_HC, V_HLAG, V_HLAB, V_E, V_SP = 0, 10, 20, 30, 40, 48, 56, 66
NDV = 80
RG_BLOCKS = {0: (0, 1), 1: (0, 1, 2), 2: (1, 2, 3), 3: (2, 3, 4), 4: (3, 4)}
NRGB = 13
DEBUG = False
BIS = set()
PREFETCH_X = True
HEAD_OVERLAP = True
W_SCRATCH = True
POOL_OFF = True
STOP = None
TILES = None


class Res:
    __slots__ = ("name", "w", "r", "excl")

    def __init__(self, name, excl=False):
        self.name = name
        self.w = None
        self.r = {}
        self.excl = excl


class Sched:
    ENGS = ("pe", "act", "dve", "pool", "sp")

    def __init__(self, nc, es):
        self.nc = nc
        self.es = es
        self.prog = {e: [] for e in self.ENGS}
        self.cnt = {e: 0 for e in self.ENGS}
        self.sem = {e: es.enter_context(nc.semaphore("s_" + e)) for e in ("pe", "act", "dve", "pool")}
        self.seen = {e: {} for e in self.ENGS}
        self.dsem = {}

    def dma_sem(self, name):
        if name not in self.dsem:
            self.dsem[name] = [self.es.enter_context(self.nc.semaphore("d_" + name)), 0]
        return self.dsem[name]

    def _need(self, eng, tok, waits, skip_same):
        if tok is None:
            return
        kind, key, val, sem = tok
        if kind == "eng" and key == eng and skip_same:
            return
        k = (kind, key)
        if self.seen[eng].get(k, 0) >= val:
            return
        if k not in waits or waits[k][1] < val:
            waits[k] = (sem, val)

    def _deps(self, eng, reads, writes, is_dma):
        waits = {}
        for r in reads:
            self._need(eng, r.w, waits, (eng == "pe") and not is_dma)
            if r.excl:
                for t in r.r.values():
                    self._need(eng, t, waits, True)
        for w in writes:
            self._need(eng, w.w, waits, (eng == "pe") and not is_dma)
            for t in w.r.values():
                self._need(eng, t, waits, (eng == "pe") and not is_dma)
        for k, (sem, val) in waits.items():
            self.seen[eng][k] = val
        return list(waits.values())

    def _commit(self, tok, reads, writes):
        for r in reads:
            k = (tok[0], tok[1])
            if k not in r.r or r.r[k][2] < tok[2]:
                r.r[k] = tok
        for w in writes:
            w.w = tok
            w.r = {}

    def op(self, eng, fn, reads=(), writes=(), signal=True):
        reads, writes = flat(reads), flat(writes)
        waits = self._deps(eng, reads, writes, False)
        if signal:
            self.cnt[eng] += 1
            tok = ("eng", eng, self.cnt[eng], self.sem[eng])
        else:
            tok = ("eng", eng, self.cnt[eng] + 1, self.sem[eng])
        self.prog[eng].append((waits, fn, self.sem[eng] if signal else None, 1))
        self._commit(tok, reads, writes)
        return tok

    def dma(self, q, fn, semname, reads=(), writes=()):
        reads, writes = flat(reads), flat(writes)
        waits = self._deps(q, reads, writes, True)
        ds = self.dma_sem(semname)
        ds[1] += 16
        tok = ("dma", semname, ds[1], ds[0])
        self.prog[q].append((waits, fn, ds[0], 16))
        self._commit(tok, reads, writes)
        return tok

    def wait_all(self, eng, toks):
        waits = {}
        for t in toks:
            self._need(eng, t, waits, False)
        for k, (sem, val) in waits.items():
            self.seen[eng][k] = val
        self.prog[eng].append((list(waits.values()), None, None, 0))

    def replay(self, eng, e):
        for waits, fn, sem, inc in self.prog[eng]:
            for s, v in waits:
                e.wait_ge(s, v)
            if fn is None:
                continue
            ins = fn(e)
            if sem is not None:
                ins.then_inc(sem, inc)


class Buf:
    def __init__(self, ap3, res_list):
        self.t = ap3
        self.res = res_list

    def r(self, c):
        return self.res[c]


BLK = 2048


class Arena:
    def __init__(self, nc, es, name, nbytes):
        self.nbytes = nbytes
        self.t = es.enter_context(nc.sbuf_tensor(name, [128, nbytes // 4], F32))
        self.blocks = [Res("%s_b%d" % (name, i)) for i in range((nbytes + BLK - 1) // BLK)]
        self.off = 0

    def reset(self, off=0):
        self.off = off

    def alloc(self, C, n, dtype):
        esz = 2 if dtype == BF16 else 4
        nb = C * n * esz
        nb_al = (nb + 63) // 64 * 64
        lo = self.off
        assert lo + nb_al <= self.nbytes, (lo, nb_al, self.nbytes)
        self.off += nb_al
        ap = self.t[:, lo // 4:(lo + nb) // 4]
        if dtype == BF16:
            ap = ap.bitcast(BF16)
        ap = ap.rearrange("p (c n) -> p c n", c=C)
        res = []
        for c in range(C):
            a = lo + c * n * esz
            b = a + n * esz
            res.append(MultiRes([self.blocks[i] for i in range(a // BLK, (b - 1) // BLK + 1)]))
        return Buf(ap, res)


class MultiRes:
    def __init__(self, blocks):
        self.blocks = blocks


def flat(rs):
    out = []
    for r in rs:
        if isinstance(r, MultiRes):
            out.extend(r.blocks)
        elif isinstance(r, (list, tuple)):
            out.extend(flat(r))
        else:
            out.append(r)
    seen = set()
    o2 = []
    for r in out:
        if id(r) not in seen:
            seen.add(id(r))
            o2.append(r)
    return o2


def build_nc():
    nc = bass.Bass("TRN2", target_bir_lowering=False)

    def din(name, shape):
        return nc.dram_tensor(name, list(shape), F32, kind="ExternalInput").ap()

    def dout(name, shape):
        return nc.dram_tensor(name, list(shape), F32, kind="ExternalOutput").ap()

    xp = din("xp", [SEQ, D]); xs = din("xs", [64, D])
    ppr = din("ppr", [SEQ, DPL]); psm = din("psm", [64, DPL])
    sca = din("sca", [480, D]); scb = din("scb", [48, DR]); sh = din("sh", [16, DR])
    w_in = din("w_in", [D, DIN]); w_pa = din("w_pa", [D, D]); w_pb = din("w_pb", [DR, D])
    w_out = din("w_out", [D, D]); w_ff1 = din("w_ff1", [D, DFF]); w_ff2 = din("w_ff2", [DFF, D])
    w_pg = din("w_pg", [D, D]); w_pp = din("w_pp", [DPL, D])
    rgw = din("rgw", [2, 128, 2 * NRGB * 128])
    colv = din("colv", [128, NCOL]); bcv = din("bcv", [128, 4 * D])
    yp = dout("yp", [SEQ, D]); ys = dout("ys", [64, D])
    ncap = dout("ncap", [30, D]); ncbp = dout("ncbp", [3, DR]); nhp = dout("nhp", [10, 128])
    ncas_new = dout("ncas_new", [64, D]); ncas_old = dout("ncas_old", [416, D])
    ncbs = dout("ncbs", [48, DR]); nhs = dout("nhs", [16, DR])
    dA = nc.dram_tensor("dA", [8, 128, KA * 128], BF16, kind="Internal").ap()
    dB = nc.dram_tensor("dB", [2, 128, 5 * KB * 128], BF16, kind="Internal").ap()
    dbg = {}
    if DEBUG:
        for nm, C in (("u2", 8), ("convo", 8), ("ca2", 8), ("cb", 10), ("hh", 10), ("aa", 10), ("bt", 10), ("hg2", 10), ("mixin", 8),
                      ("x1T", 8), ("hT", 32)):
            dbg[nm] = nc.dram_tensor("dbg_" + nm, [128, C, NT], F32, kind="ExternalOutput").ap()
        dbg["x1"] = nc.dram_tensor("dbg_x1", [128, 4, D], F32, kind="ExternalOutput").ap()

    es = ExitStack()
    S = Sched(nc, es)

    def sb(name, shape, dt):
        return es.enter_context(nc.sbuf_tensor(name, list(shape), dt))

    colv_t = sb("colv_t", [128, NCOL], F32); r_colv = Res("colv")
    dv = sb("dv", [128, NDV], F32); r_dv = Res("dv")
    wdah = sb("wdah", [128, 8 * KA], F32); r_wdah = Res("wdah")
    bc_t = sb("bc_t", [128, 4 * D], F32); r_bc = Res("bc")
    ident_f = sb("ident_f", [128, 128], F32); ident_b = sb("ident_b", [128, 128], BF16); r_id = Res("ident")
    ones_f = sb("ones_f", [128, 128], F32); r_ones = Res("ones")
    eps_t = sb("eps_t", [128, 1], F32); quarter_t = sb("quarter_t", [128, 1], F32); r_eps = Res("eps")
    cnh = sb("cnh", [128, 8], F32); r_cnh = Res("cnh")
    hcar = sb("hcar", [128, 10], F32); r_hcar = [Res("hcar%d" % i) for i in range(10)]
    ucar = sb("ucar", [128, 8, 30], BF16); r_ucar = Res("ucar")
    bcar = sb("bcar", [128, 10, 4], BF16); r_bcar = Res("bcar")
    NTMP = 12
    tmps = [sb("tmp%d" % i, [128, NT], F32) for i in range(NTMP)]
    r_tmps = [Res("tmp%d" % i) for i in range(NTMP)]
    tmp_ptr = [0]

    def tmp():
        i = tmp_ptr[0] % NTMP
        tmp_ptr[0] += 1
        return tmps[i], r_tmps[i]

    lnm = sb("lnm", [128, NT], F32); r_lnm = Res("lnm")
    lnr = sb("lnr", [128, NT], F32); r_lnr = Res("lnr")
    stat = sb("stat", [128, 64], F32)
    r_stat = [Res("stat%d" % i) for i in range(16)]
    stat_ptr = [0]

    def stat4():
        i = stat_ptr[0] % 16
        stat_ptr[0] += 1
        return stat[:, 4 * i:4 * i + 4], r_stat[i]

    ln2s = sb("ln2s", [128, 4, 4], F32); r_ln2s = Res("ln2s")
    bnst = sb("bnst", [128, 4, 12], F32)
    r_bnst = [Res("bnst%d" % i) for i in range(4)]
    bn_ptr = [0]

    WSLOT = 5120
    NW = 4
    wslots = [sb("wslot%d" % i, [128, WSLOT], BF16) for i in range(NW)]
    r_wslots = [Res("wslot%d" % i) for i in range(NW)]
    DSLOT = KA * 128
    ND = 2
    dslots = [sb("dslot%d" % i, [128, DSLOT], BF16) for i in range(ND)]
    r_dslots = [Res("dslot%d" % i) for i in range(ND)]

    pairs = [es.enter_context(nc.psum_tensor("pp%d" % i, [128, 2 * NT], F32)) for i in range(4)]
    r_banks = [Res("bank%d" % i, excl=True) for i in range(8)]
    bank_ptr = [0]

    def bank():
        i = bank_ptr[0] % 8
        bank_ptr[0] += 1
        return pairs[i // 2][:, (i % 2) * NT:(i % 2 + 1) * NT], r_banks[i]

    def bankpair():
        if bank_ptr[0] % 2:
            bank_ptr[0] += 1
        i = bank_ptr[0] % 8
        bank_ptr[0] += 2
        return pairs[i // 2], (r_banks[i], r_banks[i + 1])

    ARENA = 100 * 1024
    arena = Arena(nc, es, "arena", ARENA)

    def ACT(fn, reads, writes):
        return S.op("act", fn, flat(reads), flat(writes))

    def DVE(fn, reads, writes):
        return S.op("dve", fn, flat(reads), flat(writes))

    def POOL(fn, reads, writes):
        return S.op("pool", fn, flat(reads), flat(writes))

    def PE(fn, reads, writes, signal):
        return S.op("pe", fn, flat(reads), flat(writes), signal=signal)

    def act(out, in_, func, scale=1.0, bias=0.0):
        return lambda e: e.activation(out=out, in_=in_, func=func, scale=scale, bias=bias)

    def tt(out, in0, in1, op):
        return lambda e: e.tensor_tensor(out=out, in0=in0, in1=in1, op=op)

    def ts(out, in0, s1, s2, op0, op1):
        return lambda e: e.tensor_scalar(out=out, in0=in0, scalar1=s1, scalar2=s2, op0=op0, op1=op1)

    def ts1(out, in0, s1, op0):
        return lambda e: e.tensor_single_scalar(out=out, in_=in0, scalar=s1, op=op0)

    def stt(out, in0, scalar, in1, op0, op1):
        return lambda e: e.scalar_tensor_tensor(out=out, in0=in0, scalar=scalar, in1=in1, op0=op0, op1=op1)

    def mm(out, lhsT, rhs, start, stop):
        return lambda e: e.matmul(out, lhsT=lhsT, rhs=rhs, start=start, stop=stop)

    def tr(out, in_, ident):
        return lambda e: e.transpose(out, in_, ident)

    S.dma("sp", lambda e: e.dma_start(out=colv_t[:], in_=colv), "const0", [], [r_colv])
    S.dma("sp", lambda e: e.dma_start(out=bc_t[:], in_=bcv), "const1", [], [r_bc])
    POOL(lambda e: e.memset(ident_f[:], 0.0), [], [r_id])
    POOL(lambda e: e.affine_select(out=ident_f[:], in_=ident_f[:], pattern=[[-1, 128]], compare_op=ALU.not_equal,
                                   fill=1.0, base=0, channel_multiplier=1), [r_id], [r_id])
    POOL(lambda e: e.tensor_copy(out=ident_b[:], in_=ident_f[:]), [r_id], [r_id])
    POOL(lambda e: e.memset(ones_f[:], 1.0 / D), [], [r_ones])
    POOL(lambda e: e.memset(eps_t[:], EPS), [], [r_eps])
    POOL(lambda e: e.memset(quarter_t[:], 0.25), [r_eps], [r_eps])
    POOL(lambda e: e.memset(cnh[:], -0.5), [], [r_cnh])
    POOL(lambda e: e.memset(hcar[:], 0.0), [], r_hcar)
    ACT(act(dv[:, V_E:V_E + 10], colv_t[:, C_LAM:C_LAM + 10], AF.Exp, scale=-1.0), [r_colv], [r_dv])
    ACT(act(dv[:, V_SP:V_SP + 10], dv[:, V_E:V_E + 10], AF.Ln, bias=1.0), [r_dv], [r_dv])
    DVE(ts1(dv[:, V_C:V_C + 10], dv[:, V_SP:V_SP + 10], -8.0, ALU.mult), [r_dv], [r_dv])
    DVE(ts1(dv[:, V_HC:V_HC + 10], dv[:, V_SP:V_SP + 10], -4.0, ALU.mult), [r_dv], [r_dv])
    DVE(ts1(dv[:, V_HBRA:V_HBRA + 10], colv_t[:, C_BRA:C_BRA + 10], 0.5, ALU.mult), [r_colv], [r_dv])
    DVE(ts1(dv[:, V_HBRX:V_HBRX + 10], colv_t[:, C_BRX:C_BRX + 10], 0.5, ALU.mult), [r_colv], [r_dv])
    DVE(ts1(dv[:, V_HLAG:V_HLAG + 8], colv_t[:, C_LAG:C_LAG + 8], 0.5, ALU.mult), [r_colv], [r_dv])
    DVE(ts1(dv[:, V_HLAB:V_HLAB + 8], colv_t[:, C_LAB:C_LAB + 8], 0.5, ALU.mult), [r_colv], [r_dv])
    DVE(ts1(wdah[:], colv_t[:, C_WDA:C_WDA + 8 * KA], 0.5, ALU.mult), [r_colv], [r_wdah])
    r_dA = [Res("dA%d" % c) for c in range(8)]
    r_dB = [Res("dB%d" % h) for h in range(2)]

    def wsrc(w, K, c0, nc_):
        return w.rearrange("(k p) n -> p k n", p=128)[:, 0:K, c0:c0 + nc_]

    def tile_plan():
        pl = []
        for i in range(2):
            pl.append(("in_av", i, wsrc(w_in, 8, 512 * i, 512), 8, 512))
            pl.append(("in_ag", i, wsrc(w_in, 8, 1024 + 512 * i, 512), 8, 512))
        for h in range(2):
            pl.append(("in_bx", h, wsrc(w_in, 8, 2048 + 640 * h, 640), 8, 640))
        for h in range(2):
            pl.append(("rg", h, rgw[h].rearrange("p (k n) -> p k n", k=2 * NRGB), 2 * NRGB, 128))
            pl.append(("in_bg", h, wsrc(w_in, 8, 3328 + 640 * h, 640), 8, 640))
        for i in range(2):
            pl.append(("pb", i, wsrc(w_pb, 10, 512 * i, 512), 10, 512))
            pl.append(("in_gb", i, wsrc(w_in, 8, 5632 + 512 * i, 512), 8, 512))
        for i in range(2):
            pl.append(("pa", i, wsrc(w_pa, 8, 512 * i, 512), 8, 512))
            pl.append(("in_ga", i, wsrc(w_in, 8, 4608 + 512 * i, 512), 8, 512))
        for i in range(2):
            pl.append(("wo", i, wsrc(w_out, 8, 512 * i, 512), 8, 512))
        for i in range(8):
            pl.append(("ff1", i, wsrc(w_ff1, 8, 512 * i, 512), 8, 512))
        for i in range(2):
            pl.append(("pg", i, wsrc(w_pg, 8, 512 * i, 512), 8, 512))
        pl.append(("ppj", 0, wsrc(w_pp, 2, 0, 1024), 2, 1024))
        for hf in range(2):
            for kg in range(4):
                src = w_ff2.rearrange("(k p) n -> p k n", p=128)[:, 8 * kg:8 * kg + 8, 512 * hf:512 * hf + 512]
                pl.append(("ff2", hf * 4 + kg, src, 8, 512))
        return pl

    NTILES = 5
    wplan = []
    for t in range(NTILES):
        wplan.extend(tile_plan())
    wstate = {"loaded": 0, "cur": 0}

    NPIECE = len(tile_plan())
    wscr = nc.dram_tensor("wscr", [NPIECE, 128, WSLOT], BF16, kind="Internal").ap()
    r_wscr = [Res("wscr%d" % i) for i in range(NPIECE)]

    def w_advance():
        lim = min(len(wplan), wstate["cur"] + NW)
        while wstate["loaded"] < lim:
            j = wstate["loaded"]
            nm, idx, src, K, ncol = wplan[j]
            sl, rs = wslots[j % NW], r_wslots[j % NW]
            jl = j % NPIECE
            if (not W_SCRATCH) or j < NPIECE or (j < 2 * NPIECE and jl % 2 == 1):
                dst = sl[:, 0:K * ncol].rearrange("p (k n) -> p k n", k=K)
                S.dma("pool", (lambda e, dst=dst, src=src: e.dma_start(out=dst, in_=src)), "w%d" % (j % NW), [], [rs])
            else:
                S.dma("pool", (lambda e, sl=sl, jl=jl, n_=K * ncol: e.dma_start(out=sl[:, 0:n_], in_=wscr[jl][:, 0:n_])),
                      "w%d" % (j % NW), [r_wscr[jl]], [rs])
            wstate["loaded"] += 1

    def w_take(name, idx, n=1):
        w_advance()
        out = []
        for q in range(n):
            j = wstate["cur"] + q
            nm, ix, src, K, ncol = wplan[j]
            assert j < wstate["loaded"], (j, wstate)
            sl, rs = wslots[j % NW], r_wslots[j % NW]
            out.append((sl[:, 0:K * ncol].rearrange("p (k n) -> p k n", k=K), rs, nm, ix))
            jl_ = j % NPIECE
            if W_SCRATCH and ((j < NPIECE and jl_ % 2 == 0) or (NPIECE <= j < 2 * NPIECE and jl_ % 2 == 1)):
                S.dma("sp", (lambda e, sl=sl, jl=jl_, n_=K * ncol: e.dma_start(out=wscr[jl][:, 0:n_], in_=sl[:, 0:n_])),
                      "wb%d" % (j % NW), [rs], [r_wscr[jl_]])
        assert out[0][2] == name and out[0][3] == idx, (out[0][2:], name, idx)
        wstate["cur"] += n
        return out

    dstate = {"n": 0}

    def d_load(src, ncols, rsrc):
        j = dstate["n"]
        dstate["n"] += 1
        sl, rs = dslots[j % ND], r_dslots[j % ND]
        S.dma("sp", (lambda e, sl=sl, src=src, ncols=ncols: e.dma_start(out=sl[:, 0:ncols], in_=src)),
              "d%d" % (j % ND), [rsrc], [rs])
        return sl, rs

    out_toks = []

    def early_loads(kind, j):
        prompt = (kind == "p")
        N = NT if prompt else 64
        x_src = xp[j * NT:j * NT + N, :] if prompt else xs
        p_src = ppr[j * NT:j * NT + N, :] if prompt else psm
        arena.reset(0)
        mixin = arena.alloc(8, N, BF16)
        pT = arena.alloc(2, N, BF16)
        offAB = arena.off
        xbf = arena.alloc(4, D, BF16)
        pbf = arena.alloc(4, DPL, BF16)
        if prompt:
            S.dma("pool", lambda e: e.dma_start(out=xbf.t[:, :, :], in_=x_src.rearrange("(s p) d -> p s d", p=128)),
                  "xbf", [], flat(xbf.res))
            S.dma("pool", lambda e: e.dma_start(out=pbf.t[:, :, :], in_=p_src.rearrange("(s p) d -> p s d", p=128)),
                  "pbf", [], flat(pbf.res))
        else:
            S.dma("pool", lambda e: e.dma_start(out=xbf.t[0:64, 0, :], in_=x_src), "xbf", [], flat(xbf.res))
            S.dma("pool", lambda e: e.dma_start(out=pbf.t[0:64, 0, :], in_=p_src), "pbf", [], flat(pbf.res))
        return dict(mixin=mixin, pT=pT, offAB=offAB, xbf=xbf, pbf=pbf, off=arena.off)

    prefetched = {}
    dpre = {}

    def emit_tile(kind, j, nxt_tile=None):
        prompt = (kind == "p")
        N = NT if prompt else 64
        NS = 4 if prompt else 1
        P = 128 if prompt else 64
        first = prompt and j == 0
        last = prompt and j == 3
        tok0 = j * NT
        x_src = xp[tok0:tok0 + N, :] if prompt else xs
        p_src = ppr[tok0:tok0 + N, :] if prompt else psm
        y_dst = yp[tok0:tok0 + N, :] if prompt else ys
        LA = 30 + NT if prompt else 16 * 34
        LB = 3 + NT if prompt else 16 * 7

        pre = prefetched.pop((kind, j), None)
        if pre is None:
            pre = early_loads(kind, j)
        mixin, pT, offAB, xbf, pbf = pre["mixin"], pre["pT"], pre["offAB"], pre["xbf"], pre["pbf"]
        arena.reset(pre["off"])
        xT = arena.alloc(8, N, BF16)
        _off_u = arena.off
        uA = arena.alloc(8, 544, BF16)
        bxb = arena.alloc(10, 516 if prompt else 112, BF16)
        _save = arena.off
        arena.reset(_off_u)
        m_b = arena.alloc(8, N, F32)
        assert arena.off <= _save
        arena.reset(_save)
        if last or not prompt:
            ufp = arena.alloc(8, 64, F32)
            h0b = arena.alloc(10, 16, F32)
        if not prompt:
            sca_t = arena.alloc(4, D, BF16)
            scb_t = arena.alloc(1, DR, BF16)
            sh_t = arena.alloc(1, DR, F32)
        assert arena.off <= 50 * 1024, arena.off
        convo = arena.alloc(8, N, F32)
        _save = arena.off
        arena.reset(offAB)
        ca2 = arena.alloc(8, N, BF16)
        arena.reset(_save)
        cb = arena.alloc(5, N, F32)
        cbb = arena.alloc(5, N, BF16)
        hg2 = arena.alloc(10, N, BF16)
        if last or not prompt:
            bxf = arena.alloc(10, 64, F32)
            hl = arena.alloc(10, 16, F32)
            stg = arena.alloc(1, DR, F32)

        def uview(buf, c, L, ctx, a, b):
            if prompt:
                return buf.t[:, c, a:b]
            return buf.t[:, c, 16 * a:16 * b]

        def nview(ap2):
            return ap2

        def transpose_in(src_buf, nfc, dst_buf):
            for fc in range(nfc):
                bk, rb = bank()
                bkb = bk[:, 0:NT // 2].bitcast(BF16)
                for s in range(NS):
                    PE(tr(bkb[:, s * 128:s * 128 + P], src_buf.t[0:P, s, fc * 128:(fc + 1) * 128], ident_b[0:P, 0:P]),
                       [src_buf.res[s], r_id], [rb], signal=(s == NS - 1))
                ACT(act(dst_buf.t[:, fc, 0:N], bkb[:, 0:N], AF.Copy), [rb], [dst_buf.res[fc]])

        transpose_in(xbf, 8, xT)
        transpose_in(pbf, 2, pT)

        if STOP == 'T':
            return
        if prompt and first:
            DVE(lambda e: e.memset(uA.t[:, :, 0:30], 0.0), [], flat(uA.res))
            DVE(lambda e: e.memset(bxb.t[:, :, 0:3], 0.0), [], flat(bxb.res))
        elif prompt:
            DVE(lambda e: e.tensor_copy(out=uA.t[:, :, 0:30], in_=ucar[:, :, :]), [r_ucar], flat(uA.res))
            DVE(lambda e: e.tensor_copy(out=bxb.t[:, :, 0:3], in_=bcar[:, :, 0:3]), [r_bcar], flat(bxb.res))
        else:
            for s_ in range(4):
                S.dma("pool", (lambda e, s_=s_: e.dma_start(out=sca_t.t[0:120, s_, :], in_=sca[120 * s_:120 * s_ + 120, :])),
                      "sca%d" % s_, [], [sca_t.res[s_]])
            for fc in range(8):
                for s_ in range(4):
                    bk, rb = bank()
                    bkb = bk[:, 0:NT // 2].bitcast(BF16)
                    PE(tr(bkb[:, 0:120], sca_t.t[0:120, s_, fc * 128:(fc + 1) * 128], ident_b[0:120, 0:120]),
                       [sca_t.res[s_], r_id], [rb], signal=True)
                    ACT(act(uA.t[:, fc, 120 * s_:120 * s_ + 120], bkb[:, 0:120], AF.Copy, scale=2.0), [rb], [uA.res[fc]])
            S.dma("pool", lambda e: e.dma_start(out=scb_t.t[0:48, 0, :], in_=scb), "scb", [], flat(scb_t.res))
            S.dma("sp", lambda e: e.dma_start(out=sh_t.t[0:16, 0, :], in_=sh), "sh", [], flat(sh_t.res))
            for fc in range(10):
                bk, rb = bank()
                bkb = bk[:, 0:NT // 2].bitcast(BF16)
                PE(tr(bkb[:, 0:48], scb_t.t[0:48, 0, fc * 128:(fc + 1) * 128], ident_b[0:48, 0:48]),
                   [scb_t.res[0], r_id], [rb], signal=True)
                ACT(act(bxb.t[:, fc, 0:48], bkb[:, 0:48], AF.Copy), [rb], [bxb.res[fc]])
                bk, rb = bank()
                PE(tr(bk[:, 0:16], sh_t.t[0:16, 0, fc * 128:(fc + 1) * 128], ident_f[0:16, 0:16]),
                   [sh_t.res[0], r_id], [rb], signal=True)
                ACT(act(h0b.t[:, fc, :], bk[:, 0:16], AF.Copy), [rb], [h0b.res[fc]])

        need_state = last or not prompt

        for i in range(2):
            (wav, rav, _, _), (wag, rag, _, _) = w_take("in_av", i, 2)
            for m in range(4):
                c = 4 * i + m
                bv, rbv = bank()
                for kc in range(8):
                    PE(mm(bv[:, 0:N], wav[:, kc, m * 128:(m + 1) * 128], xT.t[:, kc, 0:N], kc == 0, kc == 7),
                       [rav, xT.res[kc]], [rbv], signal=(kc == 7))
                bg, rbg = bank()
                for kc in range(8):
                    PE(mm(bg[:, 0:N], wag[:, kc, m * 128:(m + 1) * 128], xT.t[:, kc, 0:N], kc == 0, kc == 7),
                       [rag, xT.res[kc]], [rbg], signal=(kc == 7))
                t1, rt1 = tmp()
                ACT(act(t1[:, 0:N], bg[:, 0:N], AF.Tanh, scale=0.5), [rbg], [rt1])
                DVE(stt(uview(uA, c, 34, 30, 30, 30 + N) if prompt else uview(uA, c, 34, 30, 30, 34),
                        nview(t1[:, 0:N]), 1.0, nview(bv[:, 0:N]), ALU.add, ALU.mult), [rt1, rbv], [uA.res[c]])
                if need_state:
                    n0 = N - 30 if prompt else 0
                    nn = 30 if prompt else 64
                    DVE(stt(ufp.t[:, c, 0:nn], t1[:, n0:n0 + nn], 1.0, bv[:, n0:n0 + nn], ALU.add, ALU.mult),
                        [rt1, rbv], [ufp.res[c]])
                    DVE(ts1(ufp.t[:, c, 0:nn], ufp.t[:, c, 0:nn], 0.5, ALU.mult), [ufp.res[c]], [ufp.res[c]])

        yield "head"
        if STOP == 'S1':
            return

        bx_state = {}

        def s2_group(c):
            h, m = divmod(c, 5)
            if m == 0:
                ((bx_state["w"], bx_state["r"], _, _),) = w_take("in_bx", h, 1)
            wbx, rbx = bx_state["w"], bx_state["r"]
            bk, rb = bank()
            for kc in range(8):
                PE(mm(bk[:, 0:N], wbx[:, kc, m * 128:(m + 1) * 128], xT.t[:, kc, 0:N], kc == 0, kc == 7),
                   [rbx, xT.res[kc]], [rb], signal=(kc == 7))
            ACT(act(uview(bxb, c, 7, 3, 3, 3 + N) if prompt else uview(bxb, c, 7, 3, 3, 7),
                    bk[:, 0:N], AF.Copy), [rb], [bxb.res[c]])
            if need_state:
                n0 = N - 3 if prompt else 0
                nn = 3 if prompt else 64
                ACT(act(bxf.t[:, c, 0:nn], bk[:, n0:n0 + nn], AF.Copy), [rb], [bxf.res[c]])

        def get_diag(src, ncols, rsrc, build_in1):
            if not first:
                pk = dpre.pop((kind, j, id(rsrc)), None)
                if pk is not None:
                    return pk
                return d_load(src, ncols, rsrc)
            jd = dstate["n"]
            dstate["n"] += 1
            sl, rs = dslots[jd % ND], r_dslots[jd % ND]
            K = ncols // 128
            o3 = sl[:, 0:ncols].rearrange("p (k n) -> p k n", k=K)
            i0 = ident_f[:].unsqueeze(1).to_broadcast([128, K, 128])
            i1 = build_in1.unsqueeze(2).to_broadcast([128, K, 128])
            DVE(tt(o3, i0, i1, ALU.mult), [r_id, r_wdah, r_colv], [rs])
            S.dma("sp", (lambda e, sl=sl, src=src, ncols=ncols: e.dma_start(out=src, in_=sl[:, 0:ncols])),
                  "db%d" % (jd % ND), [rs], [rsrc])
            return sl, rs

        def conva_chunk(c):
            sl, rs = get_diag(dA[c], KA * 128, r_dA[c], wdah[:, c * KA:(c + 1) * KA])
            dg = sl[:, 0:KA * 128].rearrange("p (k n) -> p k n", k=KA)
            bk, rb = bank()
            for k in range(KA):
                rhs = uview(uA, c, 34, 30, k, k + N) if prompt else uview(uA, c, 34, 30, k, k + 4)
                PE(mm(bk[:, 0:N], dg[:, k, :], rhs, k == 0, k == KA - 1), [rs, uA.res[c]], [rb],
                   signal=(k == KA - 1))
            ACT(act(convo.t[:, c, 0:N], bk[:, 0:N], AF.Identity, bias=colv_t[:, C_BDA + c:C_BDA + c + 1]),
                [rb, r_colv], [convo.res[c]])

        mean_t, rmean = lnm, r_lnm
        rstd_t, rrstd = lnr, r_lnr

        def lna_stats():
            bmean, rbmean = bank()
            for c in range(8):
                PE(mm(bmean[:, 0:N], ones_f[:], convo.t[:, c, 0:N], c == 0, c == 7), [r_ones, convo.res[c]], [rbmean],
                   signal=(c == 7))
            bex2, rbex2 = bank()
            for c in range(8):
                t1, rt1 = tmp()
                ACT(act(t1[:, 0:N], convo.t[:, c, 0:N], AF.Square), [convo.res[c]], [rt1])
                PE(mm(bex2[:, 0:N], ones_f[:], t1[:, 0:N], c == 0, c == 7), [r_ones, rt1], [rbex2], signal=(c == 7))
            ACT(act(mean_t[:, 0:N], bmean[:, 0:N], AF.Copy), [rbmean], [rmean])
            DVE(tt(rstd_t[:, 0:N], mean_t[:, 0:N], mean_t[:, 0:N], ALU.mult), [rmean], [rrstd])
            DVE(tt(rstd_t[:, 0:N], bex2[:, 0:N], rstd_t[:, 0:N], ALU.subtract), [rbex2, rrstd], [rrstd])
            ACT(act(rstd_t[:, 0:N], rstd_t[:, 0:N], AF.Sqrt, bias=eps_t[:, 0:1]), [rrstd, r_eps], [rrstd])
            DVE(lambda e: e.reciprocal(out=rstd_t[:, 0:N], in_=rstd_t[:, 0:N]), [rrstd], [rrstd])

        def lna_norm(c):
            d1, rd1 = tmp()
            EW = POOL if POOL_OFF else DVE
            EW(tt(d1[:, 0:N], convo.t[:, c, 0:N], mean_t[:, 0:N], ALU.subtract), [convo.res[c], rmean], [rd1])
            EW(tt(d1[:, 0:N], d1[:, 0:N], rstd_t[:, 0:N], ALU.mult), [rd1, rrstd], [rd1])
            t1, rt1 = tmp()
            ACT(act(t1[:, 0:N], d1[:, 0:N], AF.Tanh, scale=dv[:, V_HLAG + c:V_HLAG + c + 1],
                    bias=dv[:, V_HLAB + c:V_HLAB + c + 1]), [rd1, r_dv], [rt1])
            EW(ts(d1[:, 0:N], d1[:, 0:N], colv_t[:, C_LAG + c:C_LAG + c + 1], colv_t[:, C_LAB + c:C_LAB + c + 1],
                  ALU.mult, ALU.add), [rd1, r_colv], [rd1])
            DVE(stt(ca2.t[:, c, 0:N], t1[:, 0:N], 1.0, d1[:, 0:N], ALU.add, ALU.mult), [rt1, rd1], [ca2.res[c]])

        rg_state = {}

        def convb_half(h):
            sl, rs = get_diag(dB[h], 20 * 128, r_dB[h], colv_t[:, C_WDB + 20 * h:C_WDB + 20 * h + 20])
            dg = sl[:, 0:20 * 128].rearrange("p (k n) -> p k n", k=20)
            for m in range(5):
                c = 5 * h + m
                bk, rb = bank()
                for k in range(KB):
                    rhs = uview(bxb, c, 7, 3, k, k + N) if prompt else uview(bxb, c, 7, 3, k, k + 4)
                    PE(mm(bk[:, 0:N], dg[:, m * KB + k, :], rhs, k == 0, k == KB - 1), [rs, bxb.res[c]], [rb],
                       signal=(k == KB - 1))
                ACT(act(cbb.t[:, m, 0:N], bk[:, 0:N], AF.Identity, bias=colv_t[:, C_BDB + c:C_BDB + c + 1]),
                    [rb, r_colv], [cbb.res[m]])
                ACT(act(cb.t[:, m, 0:N], bk[:, 0:N], AF.Identity, bias=colv_t[:, C_BDB + c:C_BDB + c + 1]),
                    [rb, r_colv], [cb.res[m]])
            (wrg, rrg, _, _), (wbg, rbgw, _, _) = w_take("rg", h, 2)
            rg_state.update(wrg=wrg, rrg=rrg, wbg=wbg, rbgw=rbgw)

        blk_idx = {}
        blk = 0
        for g in range(2):
            for m in range(5):
                for kk in RG_BLOCKS[m]:
                    blk_idx[(g, m, kk)] = blk
                    blk += 1

        def rg_stage1(h, m):
            wrg, rrg, wbg, rbgw = rg_state["wrg"], rg_state["rrg"], rg_state["wbg"], rg_state["rbgw"]
            c = 5 * h + m
            kks = RG_BLOCKS[m]
            br, rbr = bank()
            for q, kk in enumerate(kks):
                PE(mm(br[:, 0:N], wrg[:, blk_idx[(0, m, kk)], :], cbb.t[:, kk, 0:N], q == 0, q == len(kks) - 1),
                   [rrg, cbb.res[kk]], [rbr], signal=(q == len(kks) - 1))
            bi, rbi = bank()
            for q, kk in enumerate(kks):
                PE(mm(bi[:, 0:N], wrg[:, blk_idx[(1, m, kk)], :], cbb.t[:, kk, 0:N], q == 0, q == len(kks) - 1),
                   [rrg, cbb.res[kk]], [rbi], signal=(q == len(kks) - 1))
            bgt, rbgt = bank()
            for kc in range(8):
                PE(mm(bgt[:, 0:N], wbg[:, kc, m * 128:(m + 1) * 128], xT.t[:, kc, 0:N], kc == 0, kc == 7),
                   [rbgw, xT.res[kc]], [rbgt], signal=(kc == 7))
            tr_, rtr = tmp()
            ACT(act(tr_[:, 0:N], br[:, 0:N], AF.Tanh, scale=0.5, bias=dv[:, V_HBRA + c:V_HBRA + c + 1]),
                [rbr, r_dv], [rtr])
            ti_, rti = tmp()
            ACT(act(ti_[:, 0:N], bi[:, 0:N], AF.Tanh, scale=0.5, bias=dv[:, V_HBRX + c:V_HBRX + c + 1]),
                [rbi, r_dv], [rti])
            a_, ra = tmp()
            ACT(act(a_[:, 0:N], tr_[:, 0:N], AF.Exp, scale=dv[:, V_HC + c:V_HC + c + 1],
                    bias=dv[:, V_HC + c:V_HC + c + 1]), [rtr, r_dv], [ra])
            a2_, ra2 = tr_, rtr
            if POOL_OFF:
                POOL(tt(a2_[:, 0:N], a_[:, 0:N], a_[:, 0:N], ALU.mult), [ra, rtr], [ra2])
            else:
                ACT(act(a2_[:, 0:N], tr_[:, 0:N], AF.Exp, scale=dv[:, V_C + c:V_C + c + 1],
                        bias=dv[:, V_C + c:V_C + c + 1]), [rtr, r_dv], [ra2])
            sq_, rsq = tmp()
            ACT(act(sq_[:, 0:N], bgt[:, 0:N], AF.Square), [rbgt], [rsq])
            DVE(stt(ti_[:, 0:N], ti_[:, 0:N], 1.0, cb.t[:, m, 0:N], ALU.add, ALU.mult), [rti, cb.res[m]], [rti])
            DVE(ts(sq_[:, 0:N], sq_[:, 0:N], 0.044715 * GK, GK, ALU.mult, ALU.add), [rsq], [rsq])
            DVE(tt(sq_[:, 0:N], sq_[:, 0:N], bgt[:, 0:N], ALU.mult), [rsq, rbgt], [rsq])
            ACT(act(sq_[:, 0:N], sq_[:, 0:N], AF.Tanh), [rsq], [rsq])
            DVE(stt(sq_[:, 0:N], sq_[:, 0:N], 1.0, bgt[:, 0:N], ALU.add, ALU.mult), [rsq, rbgt], [rsq])
            return dict(c=c, m=m, a_=a_, ra=ra, a2_=a2_, ra2=ra2, ti_=ti_, rti=rti, sq_=sq_, rsq=rsq)

        def rg_stage2(st):
            c, m, a_, ra, a2_, ra2, ti_, rti, sq_, rsq = (st[k] for k in ("c", "m", "a_", "ra", "a2_", "ra2", "ti_", "rti",
                                                                         "sq_", "rsq"))
            ACT(act(a2_[:, 0:N], a2_[:, 0:N], AF.Sqrt, scale=-0.25, bias=quarter_t[:, 0:1]), [ra2, r_eps], [ra2])
            DVE(tt(a2_[:, 0:N], a2_[:, 0:N], ti_[:, 0:N], ALU.mult), [ra2, rti], [ra2])
            hh, rhh = tmp()
            if DEBUG and first:
                dbg_dump1("aa", c, a_[:, 0:N], ra)
                dbg_dump1("bt", c, a2_[:, 0:N], ra2)
            if prompt:
                if first:
                    DVE(ts1(a2_[:, 0:1], ti_[:, 0:1], 0.5, ALU.mult), [rti, ra2], [ra2])
                DVE((lambda e, hh=hh, a_=a_, a2_=a2_, c=c: e.tensor_tensor_scan(
                    out=hh[:, 0:N], data0=a_[:, 0:N], data1=a2_[:, 0:N],
                    initial=(0.0 if first else hcar[:, c:c + 1]), op0=ALU.mult, op1=ALU.add)),
                    [ra, ra2, r_hcar[c]], [rhh])
                DVE(lambda e, hh=hh, c=c: e.tensor_copy(out=hcar[:, c:c + 1], in_=hh[:, N - 1:N]), [rhh], [r_hcar[c]])
            else:
                for t_ in range(4):
                    prev = h0b.t[:, c, :] if t_ == 0 else hh[:, 16 * (t_ - 1):16 * t_]
                    rprev = [h0b.res[c]] if t_ == 0 else [rhh]
                    DVE(tt(hh[:, 16 * t_:16 * t_ + 16], a_[:, 16 * t_:16 * t_ + 16], prev, ALU.mult), [ra] + rprev, [rhh])
                    DVE(tt(hh[:, 16 * t_:16 * t_ + 16], hh[:, 16 * t_:16 * t_ + 16], a2_[:, 16 * t_:16 * t_ + 16], ALU.add),
                        [rhh, ra2], [rhh])
                DVE(lambda e, hh=hh, c=c: e.tensor_copy(out=hl.t[:, c, :], in_=hh[:, 48:64]), [rhh], [hl.res[c]])
            DVE(tt(hg2.t[:, c, 0:N], sq_[:, 0:N], hh[:, 0:N], ALU.mult), [rsq, rhh], [hg2.res[c]])
            if DEBUG and first:
                dbg_dump1("cb", c, cb.t[:, m, 0:N], cb.res[m])
                dbg_dump1("hh", c, hh[:, 0:N], rhh)

        for c in range(10):
            s2_group(c)
        if prompt and not last:
            DVE(lambda e: e.tensor_copy(out=bcar[:, :, 0:3], in_=bxb.t[:, :, NT:NT + 3]), flat(bxb.res), [r_bcar])
        if STOP == 'S3':
            return
        ca_next = [0]

        def filler(n):
            for _ in range(n):
                if ca_next[0] < 8:
                    conva_chunk(ca_next[0])
                    ca_next[0] += 1

        def rg_half(h, fill):
            for grp, nf in zip(((0, 1), (2, 3), (4,)), fill):
                sts = [rg_stage1(h, m) for m in grp]
                filler(nf)
                for st in sts:
                    rg_stage2(st)

        convb_half(0)
        filler(1)
        rg_half(0, (2, 1, 1))
        convb_half(1)
        filler(1)
        rg_half(1, (1, 1, 0))
        filler(8)
        if prompt and not last:
            DVE(lambda e: e.tensor_copy(out=ucar[:, :, :], in_=uA.t[:, :, NT:NT + 30]), flat(uA.res), [r_ucar])
        lna_stats()
        if STOP == 'S4':
            return
        for i in range(2):
            (wpb, rpb, _, _), (wgb, rgb, _, _) = w_take("pb", i, 2)
            for m in range(4):
                c = 4 * i + m
                by, rby = bank()
                for kc in range(10):
                    PE(mm(by[:, 0:N], wpb[:, kc, m * 128:(m + 1) * 128], hg2.t[:, kc, 0:N], kc == 0, kc == 9),
                       [rpb, hg2.res[kc]], [rby], signal=(kc == 9))
                bg, rbg = bank()
                for kc in range(8):
                    PE(mm(bg[:, 0:N], wgb[:, kc, m * 128:(m + 1) * 128], xT.t[:, kc, 0:N], kc == 0, kc == 7),
                       [rgb, xT.res[kc]], [rbg], signal=(kc == 7))
                t1, rt1 = tmp()
                ACT(act(t1[:, 0:N], bg[:, 0:N], AF.Tanh, scale=0.5), [rbg], [rt1])
                DVE(stt(m_b.t[:, c, 0:N], t1[:, 0:N], 1.0, by[:, 0:N], ALU.add, ALU.mult), [rt1, rby], [m_b.res[c]])
                for cc in (2 * c, 2 * c + 1):
                    if cc < 8:
                        lna_norm(cc)
        for i in range(2):
            (wpa, rpa, _, _), (wga, rga, _, _) = w_take("pa", i, 2)
            for m in range(4):
                c = 4 * i + m
                by, rby = bank()
                for kc in range(8):
                    PE(mm(by[:, 0:N], wpa[:, kc, m * 128:(m + 1) * 128], ca2.t[:, kc, 0:N], kc == 0, kc == 7),
                       [rpa, ca2.res[kc]], [rby], signal=(kc == 7))
                bg, rbg = bank()
                for kc in range(8):
                    PE(mm(bg[:, 0:N], wga[:, kc, m * 128:(m + 1) * 128], xT.t[:, kc, 0:N], kc == 0, kc == 7),
                       [rga, xT.res[kc]], [rbg], signal=(kc == 7))
                t1, rt1 = tmp()
                ACT(act(t1[:, 0:N], bg[:, 0:N], AF.Tanh, scale=0.5), [rbg], [rt1])
                DVE(stt(t1[:, 0:N], t1[:, 0:N], 1.0, by[:, 0:N], ALU.add, ALU.mult), [rt1, rby], [rt1])
                DVE(tt(mixin.t[:, c, 0:N], t1[:, 0:N], m_b.t[:, c, 0:N], ALU.add), [rt1, m_b.res[c]], [mixin.res[c]])
        if DEBUG and first:
            dbg_dump("u2", uA, 8, 30, "bf16")
            dbg_dump("convo", convo, 8, 0, "f32")
            dbg_dump("ca2", ca2, 8, 0, "bf16")
            dbg_dump("hg2", hg2, 10, 0, "bf16")
            dbg_dump("mixin", mixin, 8, 0, "bf16")

        if STOP == 'S6':
            return
        def fm_to_rows(srcbuf, nchunks, ncols, dst_rows_fn):
            for c in range(nchunks):
                bk, rb = bank()
                PE(tr(bk[0:ncols, 0:128], srcbuf.t[:, c, 0:ncols], ident_f[:, :]), [srcbuf.res[c], r_id], [rb], signal=True)
                ACT(act(stg.t[0:ncols, 0, c * 128:(c + 1) * 128], bk[0:ncols, 0:128], AF.Copy), [rb], [stg.res[0]])
            dst_rows_fn()

        if last:
            fm_to_rows(ufp, 8, 30, lambda: out_toks.append(
                S.dma("sp", lambda e: e.dma_start(out=ncap, in_=stg.t[0:30, 0, 0:D]), "ost", flat(stg.res), [])))
            fm_to_rows(bxf, 10, 3, lambda: out_toks.append(
                S.dma("sp", lambda e: e.dma_start(out=ncbp, in_=stg.t[0:3, 0, 0:DR]), "ost", flat(stg.res), [])))
            bk, rb = bank()
            PE(tr(bk[0:10, 0:128], hcar[:, 0:10], ident_f[:, :]), r_hcar + [r_id], [rb], signal=True)
            ACT(act(stg.t[0:10, 0, 0:128], bk[0:10, 0:128], AF.Copy), [rb], [stg.res[0]])
            out_toks.append(S.dma("sp", lambda e: e.dma_start(out=nhp, in_=stg.t[0:10, 0, 0:128]), "ost", flat(stg.res), []))
        if not prompt:
            fm_to_rows(ufp, 8, 64, lambda: out_toks.append(
                S.dma("sp", lambda e: e.dma_start(out=ncas_new, in_=stg.t[0:64, 0, 0:D]), "ost", flat(stg.res), [])))
            fm_to_rows(bxf, 10, 64, lambda: out_toks.append(
                S.dma("sp", lambda e: e.dma_start(out=ncbs, in_=stg.t[16:64, 0, 0:DR]), "ost", flat(stg.res), [])))
            fm_to_rows(hl, 10, 16, lambda: out_toks.append(
                S.dma("sp", lambda e: e.dma_start(out=nhs, in_=stg.t[0:16, 0, 0:DR]), "ost", flat(stg.res), [])))
            out_toks.append(S.dma("sp", lambda e: e.dma_start(out=ncas_old, in_=sca[64:480, :]), "ost2", [], []))

        if STOP == 'state':
            return
        arena.reset(offAB)
        hT = arena.alloc(32, N, BF16)
        x1T = arena.alloc(8, N, BF16)
        x1 = arena.alloc(4, D, F32)
        xfp = arena.alloc(2, D, F32)
        tg = arena.alloc(2, D, F32)
        yb = arena.alloc(2, D, F32)

        (wo0, rwo0, _, _), (wo1, rwo1, _, _) = w_take("wo", 0, 2)
        wos = ((wo0, rwo0), (wo1, rwo1))
        def wo_mm(s):
            S.dma("sp", (lambda e, s=s: e.dma_start(out=xfp.t[0:P, s % 2, :], in_=x_src[s * 128:s * 128 + P, :])),
                  "xfp%d" % (s % 2), [], [xfp.res[s % 2]])
            pr, (rb0, rb1) = bankpair()
            rbs = (rb0, rb1)
            for hf in range(2):
                for kc in range(8):
                    PE(mm(pr[0:P, hf * NT:(hf + 1) * NT], mixin.t[:, kc, s * 128:s * 128 + P], wos[hf][0][:, kc, :],
                          kc == 0, kc == 7), [wos[hf][1], mixin.res[kc]], [rbs[hf]], signal=(kc == 7))
            b = s % 2
            DVE(stt(x1.t[0:P, s, :], xfp.t[0:P, b, :], 4.0 * ALPHA, pr[0:P, :], ALU.mult, ALU.add),
                [xfp.res[b], rb0, rb1], [x1.res[s]])
            bi_ = bn_ptr[0] % 4
            bn_ptr[0] += 1
            for hf in range(2):
                DVE((lambda e, s=s, hf=hf, bi_=bi_: e.bn_stats(out=bnst[0:P, bi_, 6 * hf:6 * hf + 6],
                                                               in_=x1.t[0:P, s, hf * NT:(hf + 1) * NT])),
                    [x1.res[s]], [r_bnst[bi_]])
            mv, rmv = stat4()
            DVE(lambda e, mv=mv, bi_=bi_: e.bn_aggr(out=mv[0:P, 0:2], in_=bnst[0:P, bi_, :]), [r_bnst[bi_]], [rmv])
            DVE(ts1(mv[0:P, 2:3], mv[0:P, 1:2], 16.0 * EPS, ALU.add), [rmv], [rmv])
            POOL(tt(mv[0:P, 2:3], mv[0:P, 2:3], cnh[0:P, 0:1], ALU.pow), [rmv, r_cnh], [rmv])
            DVE(ts(x1.t[0:P, s, :], x1.t[0:P, s, :], mv[0:P, 0:1], mv[0:P, 2:3], ALU.subtract, ALU.mult),
                [x1.res[s], rmv], [x1.res[s]])

        def ln1_tr(s):
            for fc in range(8):
                bk, rb = bank()
                PE(tr(bk[:, 0:P], x1.t[0:P, s, fc * 128:(fc + 1) * 128], ident_f[0:P, 0:P]), [x1.res[s], r_id], [rb],
                   signal=True)
                ACT(act(x1T.t[:, fc, s * 128:s * 128 + P], bk[:, 0:P], AF.Identity,
                        scale=colv_t[:, C_L1G + fc:C_L1G + fc + 1], bias=colv_t[:, C_L1B + fc:C_L1B + fc + 1]),
                    [rb, r_colv], [x1T.res[fc]])
            EW = POOL if POOL_OFF else DVE
            EW(tt(x1.t[0:P, s, :], x1.t[0:P, s, :], bc_t[0:P, 0:D], ALU.mult), [x1.res[s], r_bc], [x1.res[s]])
            EW(tt(x1.t[0:P, s, :], x1.t[0:P, s, :], bc_t[0:P, D:2 * D], ALU.add), [x1.res[s], r_bc], [x1.res[s]])

        wo_mm(0)
        for s in range(1, NS):
            wo_mm(s)
            ln1_tr(s - 1)
        ln1_tr(NS - 1)
        if DEBUG and first:
            dbg_dump("x1T", x1T, 8, 0, "bf16")
            S.dma("sp", lambda e: e.dma_start(out=dbg["x1"], in_=x1.t[:, :, :]), "odbgx", flat(x1.res), [])

        if STOP == 'S7':
            return
        for i in range(8):
            ((wf, rwf, _, _),) = w_take("ff1", i, 1)
            for m in range(4):
                c = 4 * i + m
                bk, rb = bank()
                for kc in range(8):
                    PE(mm(bk[:, 0:N], wf[:, kc, m * 128:(m + 1) * 128], x1T.t[:, kc, 0:N], kc == 0, kc == 7),
                       [rwf, x1T.res[kc]], [rb], signal=(kc == 7))
                t1, rt1 = tmp()
                ACT(act(t1[:, 0:N], bk[:, 0:N], AF.Relu), [rb], [rt1])
                if c % 2 == 0:
                    ACT(act(hT.t[:, c, 0:N], t1[:, 0:N], AF.Square), [rt1], [hT.res[c]])
                else:
                    DVE(tt(hT.t[:, c, 0:N], t1[:, 0:N], t1[:, 0:N], ALU.mult), [rt1], [hT.res[c]])
        if DEBUG and first:
            dbg_dump("hT", hT, 32, 0, "bf16")

        if STOP == 'S8':
            return
        (wg0, rwg0, _, _), (wg1, rwg1, _, _), (wpj, rwpj, _, _) = w_take("pg", 0, 3)
        wgs = ((wg0, rwg0), (wg1, rwg1))
        ple = []
        for s in range(NS):
            b = s % 2
            pr, (rb0, rb1) = bankpair()
            rbs = (rb0, rb1)
            for hf in range(2):
                for kc in range(8):
                    PE(mm(pr[0:P, hf * NT:(hf + 1) * NT], x1T.t[:, kc, s * 128:s * 128 + P], wgs[hf][0][:, kc, :],
                          kc == 0, kc == 7), [wgs[hf][1], x1T.res[kc]], [rbs[hf]], signal=(kc == 7))
            ACT(act(tg.t[0:P, b, :], pr[0:P, :], AF.Tanh, scale=0.5), [rb0, rb1], [tg.res[b]])
            pr2, (rc0, rc1) = bankpair()
            rcs = (rc0, rc1)
            for hf in range(2):
                for kc in range(2):
                    PE(mm(pr2[0:P, hf * NT:(hf + 1) * NT], pT.t[:, kc, s * 128:s * 128 + P],
                          wpj[:, kc, hf * NT:(hf + 1) * NT], kc == 0, kc == 1), [rwpj, pT.res[kc]], [rcs[hf]],
                       signal=(kc == 1))
            DVE(stt(tg.t[0:P, b, :], tg.t[0:P, b, :], 1.0, pr2[0:P, :], ALU.add, ALU.mult), [tg.res[b], rc0, rc1],
                [tg.res[b]])
            DVE(ts1(x1.t[0:P, s, :], x1.t[0:P, s, :], ALPHA, ALU.mult), [x1.res[s]], [x1.res[s]])
            DVE(stt(x1.t[0:P, s, :], tg.t[0:P, b, :], 0.5, x1.t[0:P, s, :], ALU.mult, ALU.add), [tg.res[b], x1.res[s]],
                [x1.res[s]])
        if STOP == 'S9':
            return
        for hf in range(2):
            prs = []
            for s in range(NS):
                bk, rb = bank()
                prs.append((bk, rb))
            for kg in range(4):
                ((wf2, rwf2, _, _),) = w_take("ff2", hf * 4 + kg, 1)
                for s in range(NS):
                    bk, rb = prs[s]
                    for kc in range(8):
                        kglob = 8 * kg + kc
                        PE(mm(bk[0:P, :], hT.t[:, kglob, s * 128:s * 128 + P], wf2[:, kc, :], kglob == 0, kglob == 31),
                           [rwf2, hT.res[kglob]], [rb], signal=(kc == 7))
                if hf == 1 and kg == 1 and nxt_tile is not None and PREFETCH_X:
                    prefetched[nxt_tile] = early_loads(*nxt_tile)
            for s in range(NS):
                bk, rb = prs[s]
                DVE(tt(x1.t[0:P, s, hf * NT:(hf + 1) * NT], x1.t[0:P, s, hf * NT:(hf + 1) * NT], bk[0:P, :], ALU.add),
                    [x1.res[s], rb], [x1.res[s]])
        yield "ff2"
        mv4 = ln2s
        for s in range(NS):
            bi_ = bn_ptr[0] % 4
            bn_ptr[0] += 1
            for hf in range(2):
                DVE((lambda e, s=s, hf=hf, bi_=bi_: e.bn_stats(out=bnst[0:P, bi_, 6 * hf:6 * hf + 6],
                                                               in_=x1.t[0:P, s, hf * NT:(hf + 1) * NT])),
                    [x1.res[s]], [r_bnst[bi_]])
            DVE(lambda e, s=s, bi_=bi_: e.bn_aggr(out=mv4[0:P, s, 0:2], in_=bnst[0:P, bi_, :]), [r_bnst[bi_]], [r_ln2s])
        ACT(act(mv4[0:P, 0:NS, 2], mv4[0:P, 0:NS, 1], AF.Sqrt, bias=eps_t[0:P, 0:1]), [r_ln2s, r_eps], [r_ln2s])
        DVE(lambda e: e.reciprocal(out=mv4[0:P, 0:NS, 2], in_=mv4[0:P, 0:NS, 2]), [r_ln2s], [r_ln2s])
        for s in range(NS):
            b = s % 2
            DVE(ts(yb.t[0:P, b, :], x1.t[0:P, s, :], mv4[0:P, s, 0:1], mv4[0:P, s, 2:3], ALU.subtract, ALU.mult),
                [x1.res[s], r_ln2s], [yb.res[b]])
            DVE(tt(yb.t[0:P, b, :], yb.t[0:P, b, :], bc_t[0:P, 2 * D:3 * D], ALU.mult), [yb.res[b], r_bc], [yb.res[b]])
            DVE(tt(yb.t[0:P, b, :], yb.t[0:P, b, :], bc_t[0:P, 3 * D:4 * D], ALU.add), [yb.res[b], r_bc], [yb.res[b]])
            out_toks.append(S.dma("sp", (lambda e, s=s, b=b: e.dma_start(out=y_dst[s * 128:s * 128 + P, :],
                                                                         in_=yb.t[0:P, b, :])),
                                  "oy%d" % b, [yb.res[b]], []))

    dbg_tmp = {}

    def dbg_dump(name, buf, C, off, kind):
        for c in range(C):
            dbg_dump1(name, c, buf.t[:, c, off:off + NT], buf.res[c])

    def dbg_dump1(name, c, ap, res):
        ti = tmp_ptr[0] % NTMP
        t1, rt1 = tmp()
        DVE(lambda e: e.tensor_copy(out=t1[:, 0:NT], in_=ap), [res], [rt1])
        S.dma("sp", lambda e: e.dma_start(out=dbg[name][:, c, :], in_=t1[:, 0:NT]), "odbg%d" % ti, [rt1], [])

    if STOP != "setup":
        tl = (TILES if TILES is not None else [("p", 0), ("p", 1), ("p", 2), ("p", 3), ("s", 0)])
        gens = [emit_tile(kind, j, tl[q + 1] if q + 1 < len(tl) else None) for q, (kind, j) in enumerate(tl)]

        def run_to(g, tag):
            for t_ in g:
                if t_ == tag:
                    return True
            return False

        run_to(gens[0], "head")
        for q in range(len(gens)):
            alive = run_to(gens[q], "ff2")
            if q + 1 < len(gens) and HEAD_OVERLAP:
                run_to(gens[q + 1], "head")
                nk, nj = tl[q + 1]
                dpre[(nk, nj, id(r_dB[0]))] = d_load(dB[0], 20 * 128, r_dB[0])
                dpre[(nk, nj, id(r_dA[0]))] = d_load(dA[0], KA * 128, r_dA[0])
            if alive:
                run_to(gens[q], None)
            if q + 1 < len(gens) and not HEAD_OVERLAP:
                run_to(gens[q + 1], "head")

    final = list(out_toks)
    for nm, (sem, cnt) in S.dsem.items():
        if nm.startswith("o"):
            final.append(("dma", nm, cnt, sem))
    S.wait_all("sp", final)

    with nc.Block() as block:
        @block.tensor
        def _(e):
            S.replay("pe", e)

        @block.scalar
        def _(e):
            S.replay("act", e)

        @block.vector
        def _(e):
            S.replay("dve", e)

        @block.gpsimd
        def _(e):
            S.replay("pool", e)

        @block.sync
        def _(e):
            S.replay("sp", e)
    es.close()
    return nc


def _host_prep(inp):
    f = lambda a: np.ascontiguousarray(np.asarray(a, dtype=np.float32))
    colv = np.zeros((128, NCOL), np.float32)

    def put(col0, vec):
        v = np.asarray(vec, np.float32).reshape(-1, 128)
        colv[:, col0:col0 + v.shape[0]] = v.T

    put(C_BDA, inp["b_dw_a"][0]); put(C_LAG, inp["ln_a_g"][0]); put(C_LAB, inp["ln_a_b"][0])
    put(C_BDB, inp["b_dw_b"][0]); put(C_BRA, inp["b_rg_a"][0].reshape(-1)); put(C_BRX, inp["b_rg_x"][0].reshape(-1))
    put(C_LAM, inp["rg_lam"][0]); put(C_L1G, inp["ln1_g"][0]); put(C_L1B, inp["ln1_b"][0])
    wda = np.asarray(inp["w_dw_a"][0], np.float32)
    for c in range(8):
        colv[:, C_WDA + c * KA:C_WDA + (c + 1) * KA] = wda[:, c * 128:(c + 1) * 128].T
    wdb = np.asarray(inp["w_dw_b"][0], np.float32)
    for c in range(10):
        colv[:, C_WDB + c * KB:C_WDB + (c + 1) * KB] = wdb[:, c * 128:(c + 1) * 128].T
    bcv = np.zeros((128, 4 * D), np.float32)
    for i, k in enumerate(("ln1_g", "ln1_b", "ln2_g", "ln2_b")):
        bcv[:, i * D:(i + 1) * D] = np.asarray(inp[k][0], np.float32)[None, :]
    rgw = np.zeros((2, 128, 2 * NRGB, 128), np.float32)
    for g, key in enumerate(("w_rg_a", "w_rg_x")):
        wfull = np.zeros((DR, DR), np.float32)
        w = np.asarray(inp[key][0], np.float32)
        for hd in range(16):
            wfull[80 * hd:80 * hd + 80, 80 * hd:80 * hd + 80] = w[hd]
        for h in range(2):
            blk = g * NRGB
            for m in range(5):
                for kk in RG_BLOCKS[m]:
                    r0 = 640 * h + 128 * kk
                    c0 = 640 * h + 128 * m
                    rgw[h, :, blk, :] = wfull[r0:r0 + 128, c0:c0 + 128]
                    blk += 1
    rgw = rgw.reshape(2, 128, 2 * NRGB * 128)
    common = {
        "w_in": f(inp["w_in"][0]), "w_pa": f(inp["w_proj_a"][0]), "w_pb": f(inp["w_proj_b"][0]),
        "w_out": f(inp["w_out"][0]), "w_ff1": f(inp["w_ff1"][0]), "w_ff2": f(inp["w_ff2"][0]),
        "w_pg": f(inp["w_ple_gate"][0]), "w_pp": f(inp["w_ple_proj"][0]),
        "rgw": np.ascontiguousarray(rgw), "colv": colv, "bcv": bcv,
    }
    maps = []
    for i in range(NCORES):
        m = dict(common)
        m["xp"] = f(inp["x_prompt"][i])
        m["xs"] = f(np.asarray(inp["x_sample"])[16 * i:16 * i + 16].transpose(1, 0, 2).reshape(64, D))
        m["ppr"] = f(inp["p_prompt"][0][i])
        m["psm"] = f(np.asarray(inp["p_sample"][0])[16 * i:16 * i + 16].transpose(1, 0, 2).reshape(64, DPL))
        m["sca"] = f(np.asarray(inp["state_conv_a"][0])[16 * i:16 * i + 16].transpose(1, 0, 2).reshape(480, D))
        m["scb"] = f(np.asarray(inp["state_conv_b"][0])[16 * i:16 * i + 16].transpose(1, 0, 2).reshape(48, DR))
        m["sh"] = f(np.asarray(inp["state_h"][0])[16 * i:16 * i + 16])
        maps.append(m)
    return maps


_NC_CACHE = {}


def kernel(**inputs):
    maps = _host_prep(inputs)
    if "nc" not in _NC_CACHE:
        _NC_CACHE["nc"] = build_nc()
    nc = _NC_CACHE["nc"]
    res = run_bass_kernel_spmd(nc, maps, core_ids=list(range(NCORES)))
    R = res.results
    y_p = np.stack([R[i]["yp"] for i in range(NCORES)], 0).astype(np.float32)
    y_s = np.concatenate([R[i]["ys"].reshape(4, 16, D).transpose(1, 0, 2) for i in range(NCORES)], 0).astype(np.float32)
    ca_p = np.stack([R[i]["ncap"] for i in range(NCORES)], 0)[None].astype(np.float32)
    cb_p = np.stack([R[i]["ncbp"] for i in range(NCORES)], 0)[None].astype(np.float32)
    h_p = np.stack([R[i]["nhp"].reshape(DR) for i in range(NCORES)], 0)[None].astype(np.float32)
    ca_s = np.concatenate([np.concatenate([R[i]["ncas_old"].reshape(26, 16, D), R[i]["ncas_new"].reshape(4, 16, D)], 0)
                           .transpose(1, 0, 2) for i in range(NCORES)], 0)[None].astype(np.float32)
    cb_s = np.concatenate([R[i]["ncbs"].reshape(3, 16, DR).transpose(1, 0, 2) for i in range(NCORES)], 0)[None].astype(np.float32)
    h_s = np.concatenate([R[i]["nhs"] for i in range(NCORES)], 0)[None].astype(np.float32)
    if DEBUG:
        kernel.debug = R
    return (y_p, y_s, ca_p, cb_p, h_p, ca_s, cb_s, h_s)
```

```python
import numpy as np
from contextlib import ExitStack
import concourse.bass as bass
import concourse.mybir as mybir
from concourse.bass_utils import run_bass_kernel_spmd

F32 = mybir.dt.float32
BF16 = mybir.dt.bfloat16
AF = mybir.ActivationFunctionType
ALU = mybir.AluOpType

NCORES = 8
D = 1024
DR = 1280
DFF = 4096
DPL = 256
DIN = 6656
SEQ = 2048
NT = 512
KA = 31
KB = 4
ALPHA = 2.0 ** 0.25
EPS = 1e-5
GK = 0.7978845608028654
C_BDA, C_LAG, C_LAB, C_BDB, C_BRA, C_BRX, C_LAM, C_L1G, C_L1B, C_WDA, C_WDB = 0, 8, 16, 24, 34, 44, 54, 64, 72, 80, 328
NCOL = 368
V_HBRA, V_HBRX, V_C, V_HC, V_HLAG, V_HLAB, V_E, V_SP = 0, 10, 20, 30, 40, 48, 56, 66
NDV = 80
RG_BLOCKS = {0: (0, 1), 1: (0, 1, 2), 2: (1, 2, 3), 3: (2, 3, 4), 4: (3, 4)}
NRGB = 13
DEBUG = False
BIS = set()
PREFETCH_X = True
HEAD_OVERLAP = True
W_SCRATCH = True
POOL_OFF = True
STOP = None
TILES = None


class Res:
    __slots__ = ("name", "w", "r", "excl")

    def __init__(self, name, excl=False):
        self.name = name
        self.w = None
        self.r = {}
        self.excl = excl


class Sched:
    ENGS = ("pe", "act", "dve", "pool", "sp")

    def __init__(self, nc, es):
        self.nc = nc
        self.es = es
        self.prog = {e: [] for e in self.ENGS}
        self.cnt = {e: 0 for e in self.ENGS}
        self.sem = {e: es.enter_context(nc.semaphore("s_" + e)) for e in ("pe", "act", "dve", "pool")}
        self.seen = {e: {} for e in self.ENGS}
        self.dsem = {}

    def dma_sem(self, name):
        if name not in self.dsem:
            self.dsem[name] = [self.es.enter_context(self.nc.semaphore("d_" + name)), 0]
        return self.dsem[name]

    def _need(self, eng, tok, waits, skip_same):
        if tok is None:
            return
        kind, key, val, sem = tok
        if kind == "eng" and key == eng and skip_same:
            return
        k = (kind, key)
        if self.seen[eng].get(k, 0) >= val:
            return
        if k not in waits or waits[k][1] < val:
            waits[k] = (sem, val)

    def _deps(self, eng, reads, writes, is_dma):
        waits = {}
        for r in reads:
            self._need(eng, r.w, waits, (eng == "pe") and not is_dma)
            if r.excl:
                for t in r.r.values():
                    self._need(eng, t, waits, True)
        for w in writes:
            self._need(eng, w.w, waits, (eng == "pe") and not is_dma)
            for t in w.r.values():
                self._need(eng, t, waits, (eng == "pe") and not is_dma)
        for k, (sem, val) in waits.items():
            self.seen[eng][k] = val
        return list(waits.values())

    def _commit(self, tok, reads, writes):
        for r in reads:
            k = (tok[0], tok[1])
            if k not in r.r or r.r[k][2] < tok[2]:
                r.r[k] = tok
        for w in writes:
            w.w = tok
            w.r = {}

    def op(self, eng, fn, reads=(), writes=(), signal=True):
        reads, writes = flat(reads), flat(writes)
        waits = self._deps(eng, reads, writes, False)
        if signal:
            self.cnt[eng] += 1
            tok = ("eng", eng, self.cnt[eng], self.sem[eng])
        else:
            tok = ("eng", eng, self.cnt[eng] + 1, self.sem[eng])
        self.prog[eng].append((waits, fn, self.sem[eng] if signal else None, 1))
        self._commit(tok, reads, writes)
        return tok

    def dma(self, q, fn, semname, reads=(), writes=()):
        reads, writes = flat(reads), flat(writes)
        waits = self._deps(q, reads, writes, True)
        ds = self.dma_sem(semname)
        ds[1] += 16
        tok = ("dma", semname, ds[1], ds[0])
        self.prog[q].append((waits, fn, ds[0], 16))
        self._commit(tok, reads, writes)
        return tok

    def wait_all(self, eng, toks):
        waits = {}
        for t in toks:
            self._need(eng, t, waits, False)
        for k, (sem, val) in waits.items():
            self.seen[eng][k] = val
        self.prog[eng].append((list(waits.values()), None, None, 0))

    def replay(self, eng, e):
        for waits, fn, sem, inc in self.prog[eng]:
            for s, v in waits:
                e.wait_ge(s, v)
            if fn is None:
                continue
            ins = fn(e)
            if sem is not None:
                ins.then_inc(sem, inc)


class Buf:
    def __init__(self, ap3, res_list):
        self.t = ap3
        self.res = res_list

    def r(self, c):
        return self.res[c]


BLK = 2048


class Arena:
    def __init__(self, nc, es, name, nbytes):
        self.nbytes = nbytes
        self.t = es.enter_context(nc.sbuf_tensor(name, [128, nbytes // 4], F32))
        self.blocks = [Res("%s_b%d" % (name, i)) for i in range((nbytes + BLK - 1) // BLK)]
        self.off = 0

    def reset(self, off=0):
        self.off = off

    def alloc(self, C, n, dtype):
        esz = 2 if dtype == BF16 else 4
        nb = C * n * esz
        nb_al = (nb + 63) // 64 * 64
        lo = self.off
        assert lo + nb_al <= self.nbytes, (lo, nb_al, self.nbytes)
        self.off += nb_al
        ap = self.t[:, lo // 4:(lo + nb) // 4]
        if dtype == BF16:
            ap = ap.bitcast(BF16)
        ap = ap.rearrange("p (c n) -> p c n", c=C)
        res = []
        for c in range(C):
            a = lo + c * n * esz
            b = a + n * esz
            res.append(MultiRes([self.blocks[i] for i in range(a // BLK, (b - 1) // BLK + 1)]))
        return Buf(ap, res)


class MultiRes:
    def __init__(self, blocks):
        self.blocks = blocks


def flat(rs):
    out = []
    for r in rs:
        if isinstance(r, MultiRes):
            out.extend(r.blocks)
        elif isinstance(r, (list, tuple)):
            out.extend(flat(r))
        else:
            out.append(r)
    seen = set()
    o2 = []
    for r in out:
        if id(r) not in seen:
            seen.add(id(r))
            o2.append(r)
    return o2


def build_nc():
    nc = bass.Bass("TRN2", target_bir_lowering=False)

    def din(name, shape):
        return nc.dram_tensor(name, list(shape), F32, kind="ExternalInput").ap()

    def dout(name, shape):
        return nc.dram_tensor(name, list(shape), F32, kind="ExternalOutput").ap()

    xp = din("xp", [SEQ, D]); xs = din("xs", [64, D])
    ppr = din("ppr", [SEQ, DPL]); psm = din("psm", [64, DPL])
    sca = din("sca", [480, D]); scb = din("scb", [48, DR]); sh = din("sh", [16, DR])
    w_in = din("w_in", [D, DIN]); w_pa = din("w_pa", [D, D]); w_pb = din("w_pb", [DR, D])
    w_out = din("w_out", [D, D]); w_ff1 = din("w_ff1", [D, DFF]); w_ff2 = din("w_ff2", [DFF, D])
    w_pg = din("w_pg", [D, D]); w_pp = din("w_pp", [DPL, D])
    rgw = din("rgw", [2, 128, 2 * NRGB * 128])
    colv = din("colv", [128, NCOL]); bcv = din("bcv", [128, 4 * D])
    yp = dout("yp", [SEQ, D]); ys = dout("ys", [64, D])
    ncap = dout("ncap", [30, D]); ncbp = dout("ncbp", [3, DR]); nhp = dout("nhp", [10, 128])
    ncas_new = dout("ncas_new", [64, D]); ncas_old = dout("ncas_old", [416, D])
    ncbs = dout("ncbs", [48, DR]); nhs = dout("nhs", [16, DR])
    dA = nc.dram_tensor("dA", [8, 128, KA * 128], BF16, kind="Internal").ap()
    dB = nc.dram_tensor("dB", [2, 128, 5 * KB * 128], BF16, kind="Internal").ap()
    dbg = {}
    if DEBUG:
        for nm, C in (("u2", 8), ("convo", 8), ("ca2", 8), ("cb", 10), ("hh", 10), ("aa", 10), ("bt", 10), ("hg2", 10), ("mixin", 8),
                      ("x1T", 8), ("hT", 32)):
            dbg[nm] = nc.dram_tensor("dbg_" + nm, [128, C, NT], F32, kind="ExternalOutput").ap()
        dbg["x1"] = nc.dram_tensor("dbg_x1", [128, 4, D], F32, kind="ExternalOutput").ap()

    es = ExitStack()
    S = Sched(nc, es)

    def sb(name, shape, dt):
        return es.enter_context(nc.sbuf_tensor(name, list(shape), dt))

    colv_t = sb("colv_t", [128, NCOL], F32); r_colv = Res("colv")
    dv = sb("dv", [128, NDV], F32); r_dv = Res("dv")
    wdah = sb("wdah", [128, 8 * KA], F32); r_wdah = Res("wdah")
    bc_t = sb("bc_t", [128, 4 * D], F32); r_bc = Res("bc")
    ident_f = sb("ident_f", [128, 128], F32); ident_b = sb("ident_b", [128, 128], BF16); r_id = Res("ident")
    ones_f = sb("ones_f", [128, 128], F32); r_ones = Res("ones")
    eps_t = sb("eps_t", [128, 1], F32); quarter_t = sb("quarter_t", [128, 1], F32); r_eps = Res("eps")
    cnh = sb("cnh", [128, 8], F32); r_cnh = Res("cnh")
    hcar = sb("hcar", [128, 10], F32); r_hcar = [Res("hcar%d" % i) for i in range(10)]
    ucar = sb("ucar", [128, 8, 30], BF16); r_ucar = Res("ucar")
    bcar = sb("bcar", [128, 10, 4], BF16); r_bcar = Res("bcar")
    NTMP = 12
    tmps = [sb("tmp%d" % i, [128, NT], F32) for i in range(NTMP)]
    r_tmps = [Res("tmp%d" % i) for i in range(NTMP)]
    tmp_ptr = [0]

    def tmp():
        i = tmp_ptr[0] % NTMP
        tmp_ptr[0] += 1
        return tmps[i], r_tmps[i]

    lnm = sb("lnm", [128, NT], F32); r_lnm = Res("lnm")
    lnr = sb("lnr", [128, NT], F32); r_lnr = Res("lnr")
    stat = sb("stat", [128, 64], F32)
    r_stat = [Res("stat%d" % i) for i in range(16)]
    stat_ptr = [0]

    def stat4():
        i = stat_ptr[0] % 16
        stat_ptr[0] += 1
        return stat[:, 4 * i:4 * i + 4], r_stat[i]

    ln2s = sb("ln2s", [128, 4, 4], F32); r_ln2s = Res("ln2s")
    bnst = sb("bnst", [128, 4, 12], F32)
    r_bnst = [Res("bnst%d" % i) for i in range(4)]
    bn_ptr = [0]

    WSLOT = 5120
    NW = 4
    wslots = [sb("wslot%d" % i, [128, WSLOT], BF16) for i in range(NW)]
    r_wslots = [Res("wslot%d" % i) for i in range(NW)]
    DSLOT = KA * 128
    ND = 2
    dslots = [sb("dslot%d" % i, [128, DSLOT], BF16) for i in range(ND)]
    r_dslots = [Res("dslot%d" % i) for i in range(ND)]

    pairs = [es.enter_context(nc.psum_tensor("pp%d" % i, [128, 2 * NT], F32)) for i in range(4)]
    r_banks = [Res("bank%d" % i, excl=True) for i in range(8)]
    bank_ptr = [0]

    def bank():
        i = bank_ptr[0] % 8
        bank_ptr[0] += 1
        return pairs[i // 2][:, (i % 2) * NT:(i % 2 + 1) * NT], r_banks[i]

    def bankpair():
        if bank_ptr[0] % 2:
            bank_ptr[0] += 1
        i = bank_ptr[0] % 8
        bank_ptr[0] += 2
        return pairs[i // 2], (r_banks[i], r_banks[i + 1])

    ARENA = 101 * 1024
    arena = Arena(nc, es, "arena", ARENA)

    def ACT(fn, reads, writes):
        return S.op("act", fn, flat(reads), flat(writes))

    def DVE(fn, reads, writes):
        return S.op("dve", fn, flat(reads), flat(writes))

    def POOL(fn, reads, writes):
        return S.op("pool", fn, flat(reads), flat(writes))

    def PE(fn, reads, writes, signal):
        return S.op("pe", fn, flat(reads), flat(writes), signal=signal)

    def act(out, in_, func, scale=1.0, bias=0.0):
        return lambda e: e.activation(out=out, in_=in_, func=func, scale=scale, bias=bias)

    def tt(out, in0, in1, op):
        return lambda e: e.tensor_tensor(out=out, in0=in0, in1=in1, op=op)

    def ts(out, in0, s1, s2, op0, op1):
        return lambda e: e.tensor_scalar(out=out, in0=in0, scalar1=s1, scalar2=s2, op0=op0, op1=op1)

    def ts1(out, in0, s1, op0):
        return lambda e: e.tensor_single_scalar(out=out, in_=in0, scalar=s1, op=op0)

    def stt(out, in0, scalar, in1, op0, op1):
        return lambda e: e.scalar_tensor_tensor(out=out, in0=in0, scalar=scalar, in1=in1, op0=op0, op1=op1)

    def mm(out, lhsT, rhs, start, stop):
        return lambda e: e.matmul(out, lhsT=lhsT, rhs=rhs, start=start, stop=stop)

    def tr(out, in_, ident):
        return lambda e: e.transpose(out, in_, ident)

    S.dma("sp", lambda e: e.dma_start(out=colv_t[:], in_=colv), "const0", [], [r_colv])
    S.dma("sp", lambda e: e.dma_start(out=bc_t[:], in_=bcv), "const1", [], [r_bc])
    POOL(lambda e: e.memset(ident_f[:], 0.0), [], [r_id])
    POOL(lambda e: e.affine_select(out=ident_f[:], in_=ident_f[:], pattern=[[-1, 128]], compare_op=ALU.not_equal,
                                   fill=1.0, base=0, channel_multiplier=1), [r_id], [r_id])
    POOL(lambda e: e.tensor_copy(out=ident_b[:], in_=ident_f[:]), [r_id], [r_id])
    POOL(lambda e: e.memset(ones_f[:], 1.0 / D), [], [r_ones])
    POOL(lambda e: e.memset(eps_t[:], EPS), [], [r_eps])
    POOL(lambda e: e.memset(quarter_t[:], 0.25), [r_eps], [r_eps])
    POOL(lambda e: e.memset(cnh[:], -0.5), [], [r_cnh])
    POOL(lambda e: e.memset(hcar[:], 0.0), [], r_hcar)
    ACT(act(dv[:, V_E:V_E + 10], colv_t[:, C_LAM:C_LAM + 10], AF.Exp, scale=-1.0), [r_colv], [r_dv])
    ACT(act(dv[:, V_SP:V_SP + 10], dv[:, V_E:V_E + 10], AF.Ln, bias=1.0), [r_dv], [r_dv])
    DVE(ts1(dv[:, V_C:V_C + 10], dv[:, V_SP:V_SP + 10], -8.0, ALU.mult), [r_dv], [r_dv])
    DVE(ts1(dv[:, V_HC:V_HC + 10], dv[:, V_SP:V_SP + 10], -4.0, ALU.mult), [r_dv], [r_dv])
    DVE(ts1(dv[:, V_HBRA:V_HBRA + 10], colv_t[:, C_BRA:C_BRA + 10], 0.5, ALU.mult), [r_colv], [r_dv])
    DVE(ts1(dv[:, V_HBRX:V_HBRX + 10], colv_t[:, C_BRX:C_BRX + 10], 0.5, ALU.mult), [r_colv], [r_dv])
    DVE(ts1(dv[:, V_HLAG:V_HLAG + 8], colv_t[:, C_LAG:C_LAG + 8], 0.5, ALU.mult), [r_colv], [r_dv])
    DVE(ts1(dv[:, V_HLAB:V_HLAB + 8], colv_t[:, C_LAB:C_LAB + 8], 0.5, ALU.mult), [r_colv], [r_dv])
    DVE(ts1(wdah[:], colv_t[:, C_WDA:C_WDA + 8 * KA], 0.5, ALU.mult), [r_colv], [r_wdah])
    r_dA = [Res("dA%d" % c) for c in range(8)]
    r_dB = [Res("dB%d" % h) for h in range(2)]

    def wsrc(w, K, c0, nc_):
        return w.rearrange("(k p) n -> p k n", p=128)[:, 0:K, c0:c0 + nc_]

    def tile_plan():
        pl = []
        for i in range(2):
            pl.append(("in_av", i, wsrc(w_in, 8, 512 * i, 512), 8, 512))
            pl.append(("in_ag", i, wsrc(w_in, 8, 1024 + 512 * i, 512), 8, 512))
        for h in range(2):
            pl.append(("in_bx", h, wsrc(w_in, 8, 2048 + 640 * h, 640), 8, 640))
        for h in range(2):
            pl.append(("rg", h, rgw[h].rearrange("p (k n) -> p k n", k=2 * NRGB), 2 * NRGB, 128))
            pl.append(("in_bg", h, wsrc(w_in, 8, 3328 + 640 * h, 640), 8, 640))
        for i in range(2):
            pl.append(("pb", i, wsrc(w_pb, 10, 512 * i, 512), 10, 512))
            pl.append(("in_gb", i, wsrc(w_in, 8, 5632 + 512 * i, 512), 8, 512))
        for i in range(2):
            pl.append(("pa", i, wsrc(w_pa, 8, 512 * i, 512), 8, 512))
            pl.append(("in_ga", i, wsrc(w_in, 8, 4608 + 512 * i, 512), 8, 512))
        for i in range(2):
            pl.append(("wo", i, wsrc(w_out, 8, 512 * i, 512), 8, 512))
        for i in range(8):
            pl.append(("ff1", i, wsrc(w_ff1, 8, 512 * i, 512), 8, 512))
        for i in range(2):
            pl.append(("pg", i, wsrc(w_pg, 8, 512 * i, 512), 8, 512))
        pl.append(("ppj", 0, wsrc(w_pp, 2, 0, 1024), 2, 1024))
        for hf in range(2):
            for kg in range(4):
                src = w_ff2.rearrange("(k p) n -> p k n", p=128)[:, 8 * kg:8 * kg + 8, 512 * hf:512 * hf + 512]
                pl.append(("ff2", hf * 4 + kg, src, 8, 512))
        return pl

    NTILES = 5
    wplan = []
    for t in range(NTILES):
        wplan.extend(tile_plan())
    wstate = {"loaded": 0, "cur": 0}

    NPIECE = len(tile_plan())
    wscr = nc.dram_tensor("wscr", [NPIECE, 128, WSLOT], BF16, kind="Internal").ap()
    r_wscr = [Res("wscr%d" % i) for i in range(NPIECE)]

    def w_advance():
        lim = min(len(wplan), wstate["cur"] + NW)
        while wstate["loaded"] < lim:
            j = wstate["loaded"]
            nm, idx, src, K, ncol = wplan[j]
            sl, rs = wslots[j % NW], r_wslots[j % NW]
            jl = j % NPIECE
            if (not W_SCRATCH) or j < NPIECE or (j < 2 * NPIECE and jl % 2 == 1):
                dst = sl[:, 0:K * ncol].rearrange("p (k n) -> p k n", k=K)
                S.dma("pool", (lambda e, dst=dst, src=src: e.dma_start(out=dst, in_=src)), "w%d" % (j % NW), [], [rs])
            else:
                S.dma("pool", (lambda e, sl=sl, jl=jl, n_=K * ncol: e.dma_start(out=sl[:, 0:n_], in_=wscr[jl][:, 0:n_])),
                      "w%d" % (j % NW), [r_wscr[jl]], [rs])
            wstate["loaded"] += 1

    def w_take(name, idx, n=1):
        w_advance()
        out = []
        for q in range(n):
            j = wstate["cur"] + q
            nm, ix, src, K, ncol = wplan[j]
            assert j < wstate["loaded"], (j, wstate)
            sl, rs = wslots[j % NW], r_wslots[j % NW]
            out.append((sl[:, 0:K * ncol].rearrange("p (k n) -> p k n", k=K), rs, nm, ix))
            jl_ = j % NPIECE
            if W_SCRATCH and ((j < NPIECE and jl_ % 2 == 0) or (NPIECE <= j < 2 * NPIECE and jl_ % 2 == 1)):
                S.dma("sp", (lambda e, sl=sl, jl=jl_, n_=K * ncol: e.dma_start(out=wscr[jl][:, 0:n_], in_=sl[:, 0:n_])),
                      "wb%d" % (j % NW), [rs], [r_wscr[jl_]])
        assert out[0][2] == name and out[0][3] == idx, (out[0][2:], name, idx)
        wstate["cur"] += n
        return out

    dstate = {"n": 0}

    def d_load(src, ncols, rsrc):
        j = dstate["n"]
        dstate["n"] += 1
        sl, rs = dslots[j % ND], r_dslots[j % ND]
        S.dma("sp", (lambda e, sl=sl, src=src, ncols=ncols: e.dma_start(out=sl[:, 0:ncols], in_=src)),
              "d%d" % (j % ND), [rsrc], [rs])
        return sl, rs

    out_toks = []

    def early_loads(kind, j):
        prompt = (kind == "p")
        N = NT if prompt else 64
        x_src = xp[j * NT:j * NT + N, :] if prompt else xs
        p_src = ppr[j * NT:j * NT + N, :] if prompt else psm
        arena.reset(0)
        mixin = arena.alloc(8, N, BF16)
        pT = arena.alloc(2, N, BF16)
        offAB = arena.off
        xbf = arena.alloc(4, D, BF16)
        pbf = arena.alloc(4, DPL, BF16)
        if prompt:
            S.dma("pool", lambda e: e.dma_start(out=xbf.t[:, :, :], in_=x_src.rearrange("(s p) d -> p s d", p=128)),
                  "xbf", [], flat(xbf.res))
            S.dma("pool", lambda e: e.dma_start(out=pbf.t[:, :, :], in_=p_src.rearrange("(s p) d -> p s d", p=128)),
                  "pbf", [], flat(pbf.res))
        else:
            S.dma("pool", lambda e: e.dma_start(out=xbf.t[0:64, 0, :], in_=x_src), "xbf", [], flat(xbf.res))
            S.dma("pool", lambda e: e.dma_start(out=pbf.t[0:64, 0, :], in_=p_src), "pbf", [], flat(pbf.res))
        return dict(mixin=mixin, pT=pT, offAB=offAB, xbf=xbf, pbf=pbf, off=arena.off)

    prefetched = {}
    dpre = {}

    def emit_tile(kind, j, nxt_tile=None):
        prompt = (kind == "p")
        N = NT if prompt else 64
        NS = 4 if prompt else 1
        P = 128 if prompt else 64
        first = prompt and j == 0
        last = prompt and j == 3
        tok0 = j * NT
        x_src = xp[tok0:tok0 + N, :] if prompt else xs
        p_src = ppr[tok0:tok0 + N, :] if prompt else psm
        y_dst = yp[tok0:tok0 + N, :] if prompt else ys
        LA = 30 + NT if prompt else 16 * 34
        LB = 3 + NT if prompt else 16 * 7

        pre = prefetched.pop((kind, j), None)
        if pre is None:
            pre = early_loads(kind, j)
        mixin, pT, offAB, xbf, pbf = pre["mixin"], pre["pT"], pre["offAB"], pre["xbf"], pre["pbf"]
        arena.reset(pre["off"])
        xT = arena.alloc(8, N, BF16)
        _off_u = arena.off
        uA = arena.alloc(8, 544, BF16)
        bxb = arena.alloc(10, 516 if prompt else 112, BF16)
        _save = arena.off
        arena.reset(_off_u)
        m_b = arena.alloc(8, N, F32)
        assert arena.off <= _save
        arena.reset(_save)
        if not prompt:
            h0b = arena.alloc(10, 16, F32)
        if not prompt:
            sca_t = arena.alloc(4, D, BF16)
            scb_t = arena.alloc(1, DR, BF16)
            sh_t = arena.alloc(1, DR, F32)
        assert arena.off <= 50 * 1024, arena.off
        convo = arena.alloc(8, N, F32)
        _save = arena.off
        arena.reset(offAB)
        ca2 = arena.alloc(8, N, BF16)
        arena.reset(_save)
        cb = arena.alloc(5, N, F32)
        cbb = arena.alloc(5, N, BF16)
        hg2 = arena.alloc(10, N, BF16)
        assert arena.off <= 90 * 1024, arena.off
        if last or not prompt:
            arena.reset(90 * 1024)
            ufp = arena.alloc(8, 64, F32)
            bxf = arena.alloc(10, 64, F32)
            hl = arena.alloc(10, 16, F32)
            stg = arena.alloc(1, DR, F32)

        def uview(buf, c, L, ctx, a, b):
            if prompt:
                return buf.t[:, c, a:b]
            return buf.t[:, c, 16 * a:16 * b]

        def nview(ap2):
            return ap2

        def transpose_in(src_buf, nfc, dst_buf):
            for fc in range(nfc):
                bk, rb = bank()
                bkb = bk[:, 0:NT // 2].bitcast(BF16)
                for s in range(NS):
                    PE(tr(bkb[:, s * 128:s * 128 + P], src_buf.t[0:P, s, fc * 128:(fc + 1) * 128], ident_b[0:P, 0:P]),
                       [src_buf.res[s], r_id], [rb], signal=(s == NS - 1))
                ACT(act(dst_buf.t[:, fc, 0:N], bkb[:, 0:N], AF.Copy), [rb], [dst_buf.res[fc]])

        transpose_in(xbf, 8, xT)
        transpose_in(pbf, 2, pT)

        if STOP == 'T':
            return
        if prompt and first:
            DVE(lambda e: e.memset(uA.t[:, :, 0:30], 0.0), [], flat(uA.res))
            DVE(lambda e: e.memset(bxb.t[:, :, 0:3], 0.0), [], flat(bxb.res))
        elif prompt:
            DVE(lambda e: e.tensor_copy(out=uA.t[:, :, 0:30], in_=ucar[:, :, :]), [r_ucar], flat(uA.res))
            DVE(lambda e: e.tensor_copy(out=bxb.t[:, :, 0:3], in_=bcar[:, :, 0:3]), [r_bcar], flat(bxb.res))
        else:
            for s_ in range(4):
                S.dma("pool", (lambda e, s_=s_: e.dma_start(out=sca_t.t[0:120, s_, :], in_=sca[120 * s_:120 * s_ + 120, :])),
                      "sca%d" % s_, [], [sca_t.res[s_]])
            for fc in range(8):
                for s_ in range(4):
                    bk, rb = bank()
                    bkb = bk[:, 0:NT // 2].bitcast(BF16)
                    PE(tr(bkb[:, 0:120], sca_t.t[0:120, s_, fc * 128:(fc + 1) * 128], ident_b[0:120, 0:120]),
                       [sca_t.res[s_], r_id], [rb], signal=True)
                    ACT(act(uA.t[:, fc, 120 * s_:120 * s_ + 120], bkb[:, 0:120], AF.Copy, scale=2.0), [rb], [uA.res[fc]])
            S.dma("pool", lambda e: e.dma_start(out=scb_t.t[0:48, 0, :], in_=scb), "scb", [], flat(scb_t.res))
            S.dma("sp", lambda e: e.dma_start(out=sh_t.t[0:16, 0, :], in_=sh), "sh", [], flat(sh_t.res))
            for fc in range(10):
                bk, rb = bank()
                bkb = bk[:, 0:NT // 2].bitcast(BF16)
                PE(tr(bkb[:, 0:48], scb_t.t[0:48, 0, fc * 128:(fc + 1) * 128], ident_b[0:48, 0:48]),
                   [scb_t.res[0], r_id], [rb], signal=True)
                ACT(act(bxb.t[:, fc, 0:48], bkb[:, 0:48], AF.Copy), [rb], [bxb.res[fc]])
                bk, rb = bank()
                PE(tr(bk[:, 0:16], sh_t.t[0:16, 0, fc * 128:(fc + 1) * 128], ident_f[0:16, 0:16]),
                   [sh_t.res[0], r_id], [rb], signal=True)
                ACT(act(h0b.t[:, fc, :], bk[:, 0:16], AF.Copy), [rb], [h0b.res[fc]])

        need_state = last or not prompt

        for i in range(2):
            (wav, rav, _, _), (wag, rag, _, _) = w_take("in_av", i, 2)
            for m in range(4):
                c = 4 * i + m
                bv, rbv = bank()
                for kc in range(8):
                    PE(mm(bv[:, 0:N], wav[:, kc, m * 128:(m + 1) * 128], xT.t[:, kc, 0:N], kc == 0, kc == 7),
                       [rav, xT.res[kc]], [rbv], signal=(kc == 7))
                bg, rbg = bank()
                for kc in range(8):
                    PE(mm(bg[:, 0:N], wag[:, kc, m * 128:(m + 1) * 128], xT.t[:, kc, 0:N], kc == 0, kc == 7),
                       [rag, xT.res[kc]], [rbg], signal=(kc == 7))
                t1, rt1 = tmp()
                ACT(act(t1[:, 0:N], bg[:, 0:N], AF.Tanh, scale=0.5), [rbg], [rt1])
                DVE(stt(uview(uA, c, 34, 30, 30, 30 + N) if prompt else uview(uA, c, 34, 30, 30, 34),
                        nview(t1[:, 0:N]), 1.0, nview(bv[:, 0:N]), ALU.add, ALU.mult), [rt1, rbv], [uA.res[c]])
                if need_state:
                    n0 = N - 30 if prompt else 0
                    nn = 30 if prompt else 64
                    DVE(stt(ufp.t[:, c, 0:nn], t1[:, n0:n0 + nn], 1.0, bv[:, n0:n0 + nn], ALU.add, ALU.mult),
                        [rt1, rbv], [ufp.res[c]])
                    DVE(ts1(ufp.t[:, c, 0:nn], ufp.t[:, c, 0:nn], 0.5, ALU.mult), [ufp.res[c]], [ufp.res[c]])

        yield "head"
        if STOP == 'S1':
            return

        bx_state = {}

        def s2_group(c):
            h, m = divmod(c, 5)
            if m == 0:
                ((bx_state["w"], bx_state["r"], _, _),) = w_take("in_bx", h, 1)
            wbx, rbx = bx_state["w"], bx_state["r"]
            bk, rb = bank()
            for kc in range(8):
                PE(mm(bk[:, 0:N], wbx[:, kc, m * 128:(m + 1) * 128], xT.t[:, kc, 0:N], kc == 0, kc == 7),
                   [rbx, xT.res[kc]], [rb], signal=(kc == 7))
            ACT(act(uview(bxb, c, 7, 3, 3, 3 + N) if prompt else uview(bxb, c, 7, 3, 3, 7),
                    bk[:, 0:N], AF.Copy), [rb], [bxb.res[c]])
            if need_state:
                n0 = N - 3 if prompt else 0
                nn = 3 if prompt else 64
                ACT(act(bxf.t[:, c, 0:nn], bk[:, n0:n0 + nn], AF.Copy), [rb], [bxf.res[c]])

        def get_diag(src, ncols, rsrc, build_in1):
            if not first:
                pk = dpre.pop((kind, j, id(rsrc)), None)
                if pk is not None:
                    return pk
                return d_load(src, ncols, rsrc)
            jd = dstate["n"]
            dstate["n"] += 1
            sl, rs = dslots[jd % ND], r_dslots[jd % ND]
            K = ncols // 128
            o3 = sl[:, 0:ncols].rearrange("p (k n) -> p k n", k=K)
            i0 = ident_f[:].unsqueeze(1).to_broadcast([128, K, 128])
            i1 = build_in1.unsqueeze(2).to_broadcast([128, K, 128])
            DVE(tt(o3, i0, i1, ALU.mult), [r_id, r_wdah, r_colv], [rs])
            S.dma("sp", (lambda e, sl=sl, src=src, ncols=ncols: e.dma_start(out=src, in_=sl[:, 0:ncols])),
                  "db%d" % (jd % ND), [rs], [rsrc])
            return sl, rs

        def conva_chunk(c):
            sl, rs = get_diag(dA[c], KA * 128, r_dA[c], wdah[:, c * KA:(c + 1) * KA])
            dg = sl[:, 0:KA * 128].rearrange("p (k n) -> p k n", k=KA)
            bk, rb = bank()
            for k in range(KA):
                rhs = uview(uA, c, 34, 30, k, k + N) if prompt else uview(uA, c, 34, 30, k, k + 4)
                PE(mm(bk[:, 0:N], dg[:, k, :], rhs, k == 0, k == KA - 1), [rs, uA.res[c]], [rb],
                   signal=(k == KA - 1))
            ACT(act(convo.t[:, c, 0:N], bk[:, 0:N], AF.Identity, bias=colv_t[:, C_BDA + c:C_BDA + c + 1]),
                [rb, r_colv], [convo.res[c]])

        mean_t, rmean = lnm, r_lnm
        rstd_t, rrstd = lnr, r_lnr

        def lna_stats():
            bmean, rbmean = bank()
            for c in range(8):
                PE(mm(bmean[:, 0:N], ones_f[:], convo.t[:, c, 0:N], c == 0, c == 7), [r_ones, convo.res[c]], [rbmean],
                   signal=(c == 7))
            bex2, rbex2 = bank()
            for c in range(8):
                t1, rt1 = tmp()
                ACT(act(t1[:, 0:N], convo.t[:, c, 0:N], AF.Square), [convo.res[c]], [rt1])
                PE(mm(bex2[:, 0:N], ones_f[:], t1[:, 0:N], c == 0, c == 7), [r_ones, rt1], [rbex2], signal=(c == 7))
            ACT(act(mean_t[:, 0:N], bmean[:, 0:N], AF.Copy), [rbmean], [rmean])
            DVE(tt(rstd_t[:, 0:N], mean_t[:, 0:N], mean_t[:, 0:N], ALU.mult), [rmean], [rrstd])
            DVE(tt(rstd_t[:, 0:N], bex2[:, 0:N], rstd_t[:, 0:N], ALU.subtract), [rbex2, rrstd], [rrstd])
            ACT(act(rstd_t[:, 0:N], rstd_t[:, 0:N], AF.Sqrt, bias=eps_t[:, 0:1]), [rrstd, r_eps], [rrstd])
            DVE(lambda e: e.reciprocal(out=rstd_t[:, 0:N], in_=rstd_t[:, 0:N]), [rrstd], [rrstd])

        def lna_norm(c):
            d1, rd1 = tmp()
            EW = POOL if POOL_OFF else DVE
            EW(tt(d1[:, 0:N], convo.t[:, c, 0:N], mean_t[:, 0:N], ALU.subtract), [convo.res[c], rmean], [rd1])
            EW(tt(d1[:, 0:N], d1[:, 0:N], rstd_t[:, 0:N], ALU.mult), [rd1, rrstd], [rd1])
            t1, rt1 = tmp()
            ACT(act(t1[:, 0:N], d1[:, 0:N], AF.Tanh, scale=dv[:, V_HLAG + c:V_HLAG + c + 1],
                    bias=dv[:, V_HLAB + c:V_HLAB + c + 1]), [rd1, r_dv], [rt1])
            EW(ts(d1[:, 0:N], d1[:, 0:N], colv_t[:, C_LAG + c:C_LAG + c + 1], colv_t[:, C_LAB + c:C_LAB + c + 1],
                  ALU.mult, ALU.add), [rd1, r_colv], [rd1])
            DVE(stt(ca2.t[:, c, 0:N], t1[:, 0:N], 1.0, d1[:, 0:N], ALU.add, ALU.mult), [rt1, rd1], [ca2.res[c]])

        rg_state = {}

        def convb_half(h):
            sl, rs = get_diag(dB[h], 20 * 128, r_dB[h], colv_t[:, C_WDB + 20 * h:C_WDB + 20 * h + 20])
            dg = sl[:, 0:20 * 128].rearrange("p (k n) -> p k n", k=20)
            for m in range(5):
                c = 5 * h + m
                bk, rb = bank()
                for k in range(KB):
                    rhs = uview(bxb, c, 7, 3, k, k + N) if prompt else uview(bxb, c, 7, 3, k, k + 4)
                    PE(mm(bk[:, 0:N], dg[:, m * KB + k, :], rhs, k == 0, k == KB - 1), [rs, bxb.res[c]], [rb],
                       signal=(k == KB - 1))
                ACT(act(cbb.t[:, m, 0:N], bk[:, 0:N], AF.Identity, bias=colv_t[:, C_BDB + c:C_BDB + c + 1]),
                    [rb, r_colv], [cbb.res[m]])
                ACT(act(cb.t[:, m, 0:N], bk[:, 0:N], AF.Identity, bias=colv_t[:, C_BDB + c:C_BDB + c + 1]),
                    [rb, r_colv], [cb.res[m]])
            (wrg, rrg, _, _), (wbg, rbgw, _, _) = w_take("rg", h, 2)
            rg_state.update(wrg=wrg, rrg=rrg, wbg=wbg, rbgw=rbgw)

        blk_idx = {}
        blk = 0
        for g in range(2):
            for m in range(5):
                for kk in RG_BLOCKS[m]:
                    blk_idx[(g, m, kk)] = blk
                    blk += 1

        def rg_stage1(h, m):
            wrg, rrg, wbg, rbgw = rg_state["wrg"], rg_state["rrg"], rg_state["wbg"], rg_state["rbgw"]
            c = 5 * h + m
            kks = RG_BLOCKS[m]
            br, rbr = bank()
            for q, kk in enumerate(kks):
                PE(mm(br[:, 0:N], wrg[:, blk_idx[(0, m, kk)], :], cbb.t[:, kk, 0:N], q == 0, q == len(kks) - 1),
                   [rrg, cbb.res[kk]], [rbr], signal=(q == len(kks) - 1))
            bi, rbi = bank()
            for q, kk in enumerate(kks):
                PE(mm(bi[:, 0:N], wrg[:, blk_idx[(1, m, kk)], :], cbb.t[:, kk, 0:N], q == 0, q == len(kks) - 1),
                   [rrg, cbb.res[kk]], [rbi], signal=(q == len(kks) - 1))
            bgt, rbgt = bank()
            for kc in range(8):
                PE(mm(bgt[:, 0:N], wbg[:, kc, m * 128:(m + 1) * 128], xT.t[:, kc, 0:N], kc == 0, kc == 7),
                   [rbgw, xT.res[kc]], [rbgt], signal=(kc == 7))
            tr_, rtr = tmp()
            ACT(act(tr_[:, 0:N], br[:, 0:N], AF.Tanh, scale=0.5, bias=dv[:, V_HBRA + c:V_HBRA + c + 1]),
                [rbr, r_dv], [rtr])
            ti_, rti = tmp()
            ACT(act(ti_[:, 0:N], bi[:, 0:N], AF.Tanh, scale=0.5, bias=dv[:, V_HBRX + c:V_HBRX + c + 1]),
                [rbi, r_dv], [rti])
            a_, ra = tmp()
            ACT(act(a_[:, 0:N], tr_[:, 0:N], AF.Exp, scale=dv[:, V_HC + c:V_HC + c + 1],
                    bias=dv[:, V_HC + c:V_HC + c + 1]), [rtr, r_dv], [ra])
            a2_, ra2 = tr_, rtr
            if POOL_OFF:
                POOL(tt(a2_[:, 0:N], a_[:, 0:N], a_[:, 0:N], ALU.mult), [ra, rtr], [ra2])
            else:
                ACT(act(a2_[:, 0:N], tr_[:, 0:N], AF.Exp, scale=dv[:, V_C + c:V_C + c + 1],
                        bias=dv[:, V_C + c:V_C + c + 1]), [rtr, r_dv], [ra2])
            sq_, rsq = tmp()
            ACT(act(sq_[:, 0:N], bgt[:, 0:N], AF.Square), [rbgt], [rsq])
            DVE(stt(ti_[:, 0:N], ti_[:, 0:N], 1.0, cb.t[:, m, 0:N], ALU.add, ALU.mult), [rti, cb.res[m]], [rti])
            DVE(ts(sq_[:, 0:N], sq_[:, 0:N], 0.044715 * GK, GK, ALU.mult, ALU.add), [rsq], [rsq])
            DVE(tt(sq_[:, 0:N], sq_[:, 0:N], bgt[:, 0:N], ALU.mult), [rsq, rbgt], [rsq])
            ACT(act(sq_[:, 0:N], sq_[:, 0:N], AF.Tanh), [rsq], [rsq])
            DVE(stt(sq_[:, 0:N], sq_[:, 0:N], 1.0, bgt[:, 0:N], ALU.add, ALU.mult), [rsq, rbgt], [rsq])
            return dict(c=c, m=m, a_=a_, ra=ra, a2_=a2_, ra2=ra2, ti_=ti_, rti=rti, sq_=sq_, rsq=rsq)

        def rg_stage2(st):
            c, m, a_, ra, a2_, ra2, ti_, rti, sq_, rsq = (st[k] for k in ("c", "m", "a_", "ra", "a2_", "ra2", "ti_", "rti",
                                                                         "sq_", "rsq"))
            ACT(act(a2_[:, 0:N], a2_[:, 0:N], AF.Sqrt, scale=-0.25, bias=quarter_t[:, 0:1]), [ra2, r_eps], [ra2])
            DVE(tt(a2_[:, 0:N], a2_[:, 0:N], ti_[:, 0:N], ALU.mult), [ra2, rti], [ra2])
            hh, rhh = tmp()
            if DEBUG and first:
                dbg_dump1("aa", c, a_[:, 0:N], ra)
                dbg_dump1("bt", c, a2_[:, 0:N], ra2)
            if prompt:
                if first:
                    DVE(ts1(a2_[:, 0:1], ti_[:, 0:1], 0.5, ALU.mult), [rti, ra2], [ra2])
                DVE((lambda e, hh=hh, a_=a_, a2_=a2_, c=c: e.tensor_tensor_scan(
                    out=hh[:, 0:N], data0=a_[:, 0:N], data1=a2_[:, 0:N],
                    initial=(0.0 if first else hcar[:, c:c + 1]), op0=ALU.mult, op1=ALU.add)),
                    [ra, ra2, r_hcar[c]], [rhh])
                DVE(lambda e, hh=hh, c=c: e.tensor_copy(out=hcar[:, c:c + 1], in_=hh[:, N - 1:N]), [rhh], [r_hcar[c]])
            else:
                for t_ in range(4):
                    prev = h0b.t[:, c, :] if t_ == 0 else hh[:, 16 * (t_ - 1):16 * t_]
                    rprev = [h0b.res[c]] if t_ == 0 else [rhh]
                    DVE(tt(hh[:, 16 * t_:16 * t_ + 16], a_[:, 16 * t_:16 * t_ + 16], prev, ALU.mult), [ra] + rprev, [rhh])
                    DVE(tt(hh[:, 16 * t_:16 * t_ + 16], hh[:, 16 * t_:16 * t_ + 16], a2_[:, 16 * t_:16 * t_ + 16], ALU.add),
                        [rhh, ra2], [rhh])
                DVE(lambda e, hh=hh, c=c: e.tensor_copy(out=hl.t[:, c, :], in_=hh[:, 48:64]), [rhh], [hl.res[c]])
            DVE(tt(hg2.t[:, c, 0:N], sq_[:, 0:N], hh[:, 0:N], ALU.mult), [rsq, rhh], [hg2.res[c]])
            if DEBUG and first:
                dbg_dump1("cb", c, cb.t[:, m, 0:N], cb.res[m])
                dbg_dump1("hh", c, hh[:, 0:N], rhh)

        for c in range(10):
            s2_group(c)
        if prompt and not last:
            DVE(lambda e: e.tensor_copy(out=bcar[:, :, 0:3], in_=bxb.t[:, :, NT:NT + 3]), flat(bxb.res), [r_bcar])
        if STOP == 'S3':
            return
        ca_next = [0]

        def filler(n):
            for _ in range(n):
                if ca_next[0] < 8:
                    conva_chunk(ca_next[0])
                    ca_next[0] += 1

        def rg_half(h, fill):
            for grp, nf in zip(((0, 1), (2, 3), (4,)), fill):
                sts = [rg_stage1(h, m) for m in grp]
                filler(nf)
                for st in sts:
                    rg_stage2(st)

        convb_half(0)
        filler(1)
        rg_half(0, (2, 1, 1))
        convb_half(1)
        filler(1)
        rg_half(1, (1, 1, 0))
        filler(8)
        if prompt and not last:
            DVE(lambda e: e.tensor_copy(out=ucar[:, :, :], in_=uA.t[:, :, NT:NT + 30]), flat(uA.res), [r_ucar])
        lna_stats()
        if STOP == 'S4':
            return
        for i in range(2):
            (wpb, rpb, _, _), (wgb, rgb, _, _) = w_take("pb", i, 2)
            for m in range(4):
                c = 4 * i + m
                by, rby = bank()
                for kc in range(10):
                    PE(mm(by[:, 0:N], wpb[:, kc, m * 128:(m + 1) * 128], hg2.t[:, kc, 0:N], kc == 0, kc == 9),
                       [rpb, hg2.res[kc]], [rby], signal=(kc == 9))
                bg, rbg = bank()
                for kc in range(8):
                    PE(mm(bg[:, 0:N], wgb[:, kc, m * 128:(m + 1) * 128], xT.t[:, kc, 0:N], kc == 0, kc == 7),
                       [rgb, xT.res[kc]], [rbg], signal=(kc == 7))
                t1, rt1 = tmp()
                ACT(act(t1[:, 0:N], bg[:, 0:N], AF.Tanh, scale=0.5), [rbg], [rt1])
                DVE(stt(m_b.t[:, c, 0:N], t1[:, 0:N], 1.0, by[:, 0:N], ALU.add, ALU.mult), [rt1, rby], [m_b.res[c]])
                lna_norm(c)
        for i in range(2):
            (wpa, rpa, _, _), (wga, rga, _, _) = w_take("pa", i, 2)
            for m in range(4):
                c = 4 * i + m
                by, rby = bank()
                for kc in range(8):
                    PE(mm(by[:, 0:N], wpa[:, kc, m * 128:(m + 1) * 128], ca2.t[:, kc, 0:N], kc == 0, kc == 7),
                       [rpa, ca2.res[kc]], [rby], signal=(kc == 7))
                bg, rbg = bank()
                for kc in range(8):
                    PE(mm(bg[:, 0:N], wga[:, kc, m * 128:(m + 1) * 128], xT.t[:, kc, 0:N], kc == 0, kc == 7),
                       [rga, xT.res[kc]], [rbg], signal=(kc == 7))
                t1, rt1 = tmp()
                ACT(act(t1[:, 0:N], bg[:, 0:N], AF.Tanh, scale=0.5), [rbg], [rt1])
                DVE(stt(t1[:, 0:N], t1[:, 0:N], 1.0, by[:, 0:N], ALU.add, ALU.mult), [rt1, rby], [rt1])
                DVE(tt(mixin.t[:, c, 0:N], t1[:, 0:N], m_b.t[:, c, 0:N], ALU.add), [rt1, m_b.res[c]], [mixin.res[c]])
        if DEBUG and first:
            dbg_dump("u2", uA, 8, 30, "bf16")
            dbg_dump("convo", convo, 8, 0, "f32")
            dbg_dump("ca2", ca2, 8, 0, "bf16")
            dbg_dump("hg2", hg2, 10, 0, "bf16")
            dbg_dump("mixin", mixin, 8, 0, "bf16")

        if STOP == 'S6':
            return
        def fm_to_rows(srcbuf, nchunks, ncols, dst_rows_fn):
            for c in range(nchunks):
                bk, rb = bank()
                PE(tr(bk[0:ncols, 0:128], srcbuf.t[:, c, 0:ncols], ident_f[:, :]), [srcbuf.res[c], r_id], [rb], signal=True)
                ACT(act(stg.t[0:ncols, 0, c * 128:(c + 1) * 128], bk[0:ncols, 0:128], AF.Copy), [rb], [stg.res[0]])
            dst_rows_fn()

        def emit_state_outputs():
            if last:
                fm_to_rows(ufp, 8, 30, lambda: out_toks.append(
                    S.dma("sp", lambda e: e.dma_start(out=ncap, in_=stg.t[0:30, 0, 0:D]), "ost", flat(stg.res), [])))
                fm_to_rows(bxf, 10, 3, lambda: out_toks.append(
                    S.dma("sp", lambda e: e.dma_start(out=ncbp, in_=stg.t[0:3, 0, 0:DR]), "ost", flat(stg.res), [])))
                bk, rb = bank()
                PE(tr(bk[0:10, 0:128], hcar[:, 0:10], ident_f[:, :]), r_hcar + [r_id], [rb], signal=True)
                ACT(act(stg.t[0:10, 0, 0:128], bk[0:10, 0:128], AF.Copy), [rb], [stg.res[0]])
                out_toks.append(S.dma("sp", lambda e: e.dma_start(out=nhp, in_=stg.t[0:10, 0, 0:128]), "ost", flat(stg.res), []))
            if not prompt:
                fm_to_rows(ufp, 8, 64, lambda: out_toks.append(
                    S.dma("sp", lambda e: e.dma_start(out=ncas_new, in_=stg.t[0:64, 0, 0:D]), "ost", flat(stg.res), [])))
                fm_to_rows(bxf, 10, 64, lambda: out_toks.append(
                    S.dma("sp", lambda e: e.dma_start(out=ncbs, in_=stg.t[16:64, 0, 0:DR]), "ost", flat(stg.res), [])))
                fm_to_rows(hl, 10, 16, lambda: out_toks.append(
                    S.dma("sp", lambda e: e.dma_start(out=nhs, in_=stg.t[0:16, 0, 0:DR]), "ost", flat(stg.res), [])))
                out_toks.append(S.dma("sp", lambda e: e.dma_start(out=ncas_old, in_=sca[64:480, :]), "ost2", [], []))


        if prompt:
            emit_state_outputs()
        if STOP == 'state':
            return
        arena.reset(offAB)
        hT = arena.alloc(32, N, BF16)
        x1T = arena.alloc(8, N, BF16)
        x1 = arena.alloc(4, D, F32)
        xfp = arena.alloc(2, D, F32)
        tg = arena.alloc(2, D, F32)
        yb = arena.alloc(2, D, F32)

        (wo0, rwo0, _, _), (wo1, rwo1, _, _) = w_take("wo", 0, 2)
        wos = ((wo0, rwo0), (wo1, rwo1))
        def wo_mm(s):
            S.dma("sp", (lambda e, s=s: e.dma_start(out=xfp.t[0:P, s % 2, :], in_=x_src[s * 128:s * 128 + P, :])),
                  "xfp%d" % (s % 2), [], [xfp.res[s % 2]])
            pr, (rb0, rb1) = bankpair()
            rbs = (rb0, rb1)
            for hf in range(2):
                for kc in range(8):
                    PE(mm(pr[0:P, hf * NT:(hf + 1) * NT], mixin.t[:, kc, s * 128:s * 128 + P], wos[hf][0][:, kc, :],
                          kc == 0, kc == 7), [wos[hf][1], mixin.res[kc]], [rbs[hf]], signal=(kc == 7))
            b = s % 2
            DVE(stt(x1.t[0:P, s, :], xfp.t[0:P, b, :], 4.0 * ALPHA, pr[0:P, :], ALU.mult, ALU.add),
                [xfp.res[b], rb0, rb1], [x1.res[s]])
            bi_ = bn_ptr[0] % 4
            bn_ptr[0] += 1
            for hf in range(2):
                DVE((lambda e, s=s, hf=hf, bi_=bi_: e.bn_stats(out=bnst[0:P, bi_, 6 * hf:6 * hf + 6],
                                                               in_=x1.t[0:P, s, hf * NT:(hf + 1) * NT])),
                    [x1.res[s]], [r_bnst[bi_]])
            mv, rmv = stat4()
            DVE(lambda e, mv=mv, bi_=bi_: e.bn_aggr(out=mv[0:P, 0:2], in_=bnst[0:P, bi_, :]), [r_bnst[bi_]], [rmv])
            DVE(ts1(mv[0:P, 2:3], mv[0:P, 1:2], 16.0 * EPS, ALU.add), [rmv], [rmv])
            POOL(tt(mv[0:P, 2:3], mv[0:P, 2:3], cnh[0:P, 0:1], ALU.pow), [rmv, r_cnh], [rmv])
            DVE(ts(x1.t[0:P, s, :], x1.t[0:P, s, :], mv[0:P, 0:1], mv[0:P, 2:3], ALU.subtract, ALU.mult),
                [x1.res[s], rmv], [x1.res[s]])

        def ln1_tr(s):
            for fc in range(8):
                bk, rb = bank()
                PE(tr(bk[:, 0:P], x1.t[0:P, s, fc * 128:(fc + 1) * 128], ident_f[0:P, 0:P]), [x1.res[s], r_id], [rb],
                   signal=True)
                ACT(act(x1T.t[:, fc, s * 128:s * 128 + P], bk[:, 0:P], AF.Identity,
                        scale=colv_t[:, C_L1G + fc:C_L1G + fc + 1], bias=colv_t[:, C_L1B + fc:C_L1B + fc + 1]),
                    [rb, r_colv], [x1T.res[fc]])
            EW = POOL if POOL_OFF else DVE
            EW(tt(x1.t[0:P, s, :], x1.t[0:P, s, :], bc_t[0:P, 0:D], ALU.mult), [x1.res[s], r_bc], [x1.res[s]])
            EW(tt(x1.t[0:P, s, :], x1.t[0:P, s, :], bc_t[0:P, D:2 * D], ALU.add), [x1.res[s], r_bc], [x1.res[s]])

        wo_mm(0)
        for s in range(1, NS):
            wo_mm(s)
            ln1_tr(s - 1)
        ln1_tr(NS - 1)
        if DEBUG and first:
            dbg_dump("x1T", x1T, 8, 0, "bf16")
            S.dma("sp", lambda e: e.dma_start(out=dbg["x1"], in_=x1.t[:, :, :]), "odbgx", flat(x1.res), [])

        if STOP == 'S7':
            return
        for i in range(8):
            ((wf, rwf, _, _),) = w_take("ff1", i, 1)
            for m in range(4):
                c = 4 * i + m
                bk, rb = bank()
                for kc in range(8):
                    PE(mm(bk[:, 0:N], wf[:, kc, m * 128:(m + 1) * 128], x1T.t[:, kc, 0:N], kc == 0, kc == 7),
                       [rwf, x1T.res[kc]], [rb], signal=(kc == 7))
                t1, rt1 = tmp()
                ACT(act(t1[:, 0:N], bk[:, 0:N], AF.Relu), [rb], [rt1])
                if c % 2 == 0:
                    ACT(act(hT.t[:, c, 0:N], t1[:, 0:N], AF.Square), [rt1], [hT.res[c]])
                else:
                    DVE(tt(hT.t[:, c, 0:N], t1[:, 0:N], t1[:, 0:N], ALU.mult), [rt1], [hT.res[c]])
        if DEBUG and first:
            dbg_dump("hT", hT, 32, 0, "bf16")

        if STOP == 'S8':
            return
        (wg0, rwg0, _, _), (wg1, rwg1, _, _), (wpj, rwpj, _, _) = w_take("pg", 0, 3)
        wgs = ((wg0, rwg0), (wg1, rwg1))
        ple = []
        for s in range(NS):
            b = s % 2
            pr, (rb0, rb1) = bankpair()
            rbs = (rb0, rb1)
            for hf in range(2):
                for kc in range(8):
                    PE(mm(pr[0:P, hf * NT:(hf + 1) * NT], x1T.t[:, kc, s * 128:s * 128 + P], wgs[hf][0][:, kc, :],
                          kc == 0, kc == 7), [wgs[hf][1], x1T.res[kc]], [rbs[hf]], signal=(kc == 7))
            ACT(act(tg.t[0:P, b, :], pr[0:P, :], AF.Tanh, scale=0.5), [rb0, rb1], [tg.res[b]])
            pr2, (rc0, rc1) = bankpair()
            rcs = (rc0, rc1)
            for hf in range(2):
                for kc in range(2):
                    PE(mm(pr2[0:P, hf * NT:(hf + 1) * NT], pT.t[:, kc, s * 128:s * 128 + P],
                          wpj[:, kc, hf * NT:(hf + 1) * NT], kc == 0, kc == 1), [rwpj, pT.res[kc]], [rcs[hf]],
                       signal=(kc == 1))
            DVE(stt(tg.t[0:P, b, :], tg.t[0:P, b, :], 1.0, pr2[0:P, :], ALU.add, ALU.mult), [tg.res[b], rc0, rc1],
                [tg.res[b]])
            DVE(ts1(x1.t[0:P, s, :], x1.t[0:P, s, :], ALPHA, ALU.mult), [x1.res[s]], [x1.res[s]])
            DVE(stt(x1.t[0:P, s, :], tg.t[0:P, b, :], 0.5, x1.t[0:P, s, :], ALU.mult, ALU.add), [tg.res[b], x1.res[s]],
                [x1.res[s]])
        if STOP == 'S9':
            return
        for hf in range(2):
            prs = []
            for s in range(NS):
                bk, rb = bank()
                prs.append((bk, rb))
            for kg in range(4):
                ((wf2, rwf2, _, _),) = w_take("ff2", hf * 4 + kg, 1)
                for s in range(NS):
                    bk, rb = prs[s]
                    for kc in range(8):
                        kglob = 8 * kg + kc
                        PE(mm(bk[0:P, :], hT.t[:, kglob, s * 128:s * 128 + P], wf2[:, kc, :], kglob == 0, kglob == 31),
                           [rwf2, hT.res[kglob]], [rb], signal=(kc == 7))
                if hf == 1 and kg == 1 and nxt_tile is not None and PREFETCH_X:
                    prefetched[nxt_tile] = early_loads(*nxt_tile)
            for s in range(NS):
                bk, rb = prs[s]
                DVE(tt(x1.t[0:P, s, hf * NT:(hf + 1) * NT], x1.t[0:P, s, hf * NT:(hf + 1) * NT], bk[0:P, :], ALU.add),
                    [x1.res[s], rb], [x1.res[s]])
        yield "ff2"
        if not prompt:
            emit_state_outputs()
        mv4 = ln2s
        for s in range(NS):
            bi_ = bn_ptr[0] % 4
            bn_ptr[0] += 1
            for hf in range(2):
                DVE((lambda e, s=s, hf=hf, bi_=bi_: e.bn_stats(out=bnst[0:P, bi_, 6 * hf:6 * hf + 6],
                                                               in_=x1.t[0:P, s, hf * NT:(hf + 1) * NT])),
                    [x1.res[s]], [r_bnst[bi_]])
            DVE(lambda e, s=s, bi_=bi_: e.bn_aggr(out=mv4[0:P, s, 0:2], in_=bnst[0:P, bi_, :]), [r_bnst[bi_]], [r_ln2s])
        ACT(act(mv4[0:P, 0:NS, 2], mv4[0:P, 0:NS, 1], AF.Sqrt, bias=eps_t[0:P, 0:1]), [r_ln2s, r_eps], [r_ln2s])
        DVE(lambda e: e.reciprocal(out=mv4[0:P, 0:NS, 2], in_=mv4[0:P, 0:NS, 2]), [r_ln2s], [r_ln2s])
        for s in range(NS):
            b = s % 2
            DVE(ts(yb.t[0:P, b, :], x1.t[0:P, s, :], mv4[0:P, s, 0:1], mv4[0:P, s, 2:3], ALU.subtract, ALU.mult),
                [x1.res[s], r_ln2s], [yb.res[b]])
            DVE(tt(yb.t[0:P, b, :], yb.t[0:P, b, :], bc_t[0:P, 2 * D:3 * D], ALU.mult), [yb.res[b], r_bc], [yb.res[b]])
            DVE(tt(yb.t[0:P, b, :], yb.t[0:P, b, :], bc_t[0:P, 3 * D:4 * D], ALU.add), [yb.res[b], r_bc], [yb.res[b]])
            out_toks.append(S.dma("sp", (lambda e, s=s, b=b: e.dma_start(out=y_dst[s * 128:s * 128 + P, :],
                                                                         in_=yb.t[0:P, b, :])),
                                  "oy%d" % b, [yb.res[b]], []))

    dbg_tmp = {}

    def dbg_dump(name, buf, C, off, kind):
        for c in range(C):
            dbg_dump1(name, c, buf.t[:, c, off:off + NT], buf.res[c])

    def dbg_dump1(name, c, ap, res):
        ti = tmp_ptr[0] % NTMP
        t1, rt1 = tmp()
        DVE(lambda e: e.tensor_copy(out=t1[:, 0:NT], in_=ap), [res], [rt1])
        S.dma("sp", lambda e: e.dma_start(out=dbg[name][:, c, :], in_=t1[:, 0:NT]), "odbg%d" % ti, [rt1], [])

    if STOP != "setup":
        tl = (TILES if TILES is not None else [("p", 0), ("p", 1), ("p", 2), ("p", 3), ("s", 0)])
        gens = [emit_tile(kind, j, tl[q + 1] if q + 1 < len(tl) else None) for q, (kind, j) in enumerate(tl)]

        def run_to(g, tag):
            for t_ in g:
                if t_ == tag:
                    return True
            return False

        run_to(gens[0], "head")
        for q in range(len(gens)):
            alive = run_to(gens[q], "ff2")
            if q + 1 < len(gens) and HEAD_OVERLAP:
                run_to(gens[q + 1], "head")
                nk, nj = tl[q + 1]
                dpre[(nk, nj, id(r_dB[0]))] = d_load(dB[0], 20 * 128, r_dB[0])
                dpre[(nk, nj, id(r_dA[0]))] = d_load(dA[0], KA * 128, r_dA[0])
            if alive:
                run_to(gens[q], None)
            if q + 1 < len(gens) and not HEAD_OVERLAP:
                run_to(gens[q + 1], "head")

    final = list(out_toks)
    for nm, (sem, cnt) in S.dsem.items():
        if nm.startswith("o"):
            final.append(("dma", nm, cnt, sem))
    S.wait_all("sp", final)

    with nc.Block() as block:
        @block.tensor
        def _(e):
            S.replay("pe", e)

        @block.scalar
        def _(e):
            S.replay("act", e)

        @block.vector
        def _(e):
            S.replay("dve", e)

        @block.gpsimd
        def _(e):
            S.replay("pool", e)

        @block.sync
        def _(e):
            S.replay("sp", e)
    es.close()
    return nc


def _host_prep(inp):
    f = lambda a: np.ascontiguousarray(np.asarray(a, dtype=np.float32))
    colv = np.zeros((128, NCOL), np.float32)

    def put(col0, vec):
        v = np.asarray(vec, np.float32).reshape(-1, 128)
        colv[:, col0:col0 + v.shape[0]] = v.T

    put(C_BDA, inp["b_dw_a"][0]); put(C_LAG, inp["ln_a_g"][0]); put(C_LAB, inp["ln_a_b"][0])
    put(C_BDB, inp["b_dw_b"][0]); put(C_BRA, inp["b_rg_a"][0].reshape(-1)); put(C_BRX, inp["b_rg_x"][0].reshape(-1))
    put(C_LAM, inp["rg_lam"][0]); put(C_L1G, inp["ln1_g"][0]); put(C_L1B, inp["ln1_b"][0])
    wda = np.asarray(inp["w_dw_a"][0], np.float32)
    for c in range(8):
        colv[:, C_WDA + c * KA:C_WDA + (c + 1) * KA] = wda[:, c * 128:(c + 1) * 128].T
    wdb = np.asarray(inp["w_dw_b"][0], np.float32)
    for c in range(10):
        colv[:, C_WDB + c * KB:C_WDB + (c + 1) * KB] = wdb[:, c * 128:(c + 1) * 128].T
    bcv = np.zeros((128, 4 * D), np.float32)
    for i, k in enumerate(("ln1_g", "ln1_b", "ln2_g", "ln2_b")):
        bcv[:, i * D:(i + 1) * D] = np.asarray(inp[k][0], np.float32)[None, :]
    rgw = np.zeros((2, 128, 2 * NRGB, 128), np.float32)
    for g, key in enumerate(("w_rg_a", "w_rg_x")):
        wfull = np.zeros((DR, DR), np.float32)
        w = np.asarray(inp[key][0], np.float32)
        for hd in range(16):
            wfull[80 * hd:80 * hd + 80, 80 * hd:80 * hd + 80] = w[hd]
        for h in range(2):
            blk = g * NRGB
            for m in range(5):
                for kk in RG_BLOCKS[m]:
                    r0 = 640 * h + 128 * kk
                    c0 = 640 * h + 128 * m
                    rgw[h, :, blk, :] = wfull[r0:r0 + 128, c0:c0 + 128]
                    blk += 1
    rgw = rgw.reshape(2, 128, 2 * NRGB * 128)
    common = {
        "w_in": f(inp["w_in"][0]), "w_pa": f(inp["w_proj_a"][0]), "w_pb": f(inp["w_proj_b"][0]),
        "w_out": f(inp["w_out"][0]), "w_ff1": f(inp["w_ff1"][0]), "w_ff2": f(inp["w_ff2"][0]),
        "w_pg": f(inp["w_ple_gate"][0]), "w_pp": f(inp["w_ple_proj"][0]),
        "rgw": np.ascontiguousarray(rgw), "colv": colv, "bcv": bcv,
    }
    maps = []
    for i in range(NCORES):
        m = dict(common)
        m["xp"] = f(inp["x_prompt"][i])
        m["xs"] = f(np.asarray(inp["x_sample"])[16 * i:16 * i + 16].transpose(1, 0, 2).reshape(64, D))
        m["ppr"] = f(inp["p_prompt"][0][i])
        m["psm"] = f(np.asarray(inp["p_sample"][0])[16 * i:16 * i + 16].transpose(1, 0, 2).reshape(64, DPL))
        m["sca"] = f(np.asarray(inp["state_conv_a"][0])[16 * i:16 * i + 16].transpose(1, 0, 2).reshape(480, D))
        m["scb"] = f(np.asarray(inp["state_conv_b"][0])[16 * i:16 * i + 16].transpose(1, 0, 2).reshape(48, DR))
        m["sh"] = f(np.asarray(inp["state_h"][0])[16 * i:16 * i + 16])
        maps.append(m)
    return maps


_NC_CACHE = {}


def kernel(**inputs):
    maps = _host_prep(inputs)
    if "nc" not in _NC_CACHE:
        _NC_CACHE["nc"] = build_nc()
    nc = _NC_CACHE["nc"]
    res = run_bass_kernel_spmd(nc, maps, core_ids=list(range(NCORES)))
    R = res.results
    y_p = np.stack([R[i]["yp"] for i in range(NCORES)], 0).astype(np.float32)
    y_s = np.concatenate([R[i]["ys"].reshape(4, 16, D).transpose(1, 0, 2) for i in range(NCORES)], 0).astype(np.float32)
    ca_p = np.stack([R[i]["ncap"] for i in range(NCORES)], 0)[None].astype(np.float32)
    cb_p = np.stack([R[i]["ncbp"] for i in range(NCORES)], 0)[None].astype(np.float32)
    h_p = np.stack([R[i]["nhp"].reshape(DR) for i in range(NCORES)], 0)[None].astype(np.float32)
    ca_s = np.concatenate([np.concatenate([R[i]["ncas_old"].reshape(26, 16, D), R[i]["ncas_new"].reshape(4, 16, D)], 0)
                           .transpose(1, 0, 2) for i in range(NCORES)], 0)[None].astype(np.float32)
    cb_s = np.concatenate([R[i]["ncbs"].reshape(3, 16, DR).transpose(1, 0, 2) for i in range(NCORES)], 0)[None].astype(np.float32)
    h_s = np.concatenate([R[i]["nhs"] for i in range(NCORES)], 0)[None].astype(np.float32)
    if DEBUG:
        kernel.debug = R
    return (y_p, y_s, ca_p, cb_p, h_p, ca_s, cb_s, h_s)
```

```python
import numpy as np
from contextlib import ExitStack
import concourse.bass as bass
import concourse.mybir as mybir
from concourse.bass_utils import run_bass_kernel_spmd

F32 = mybir.dt.float32
BF16 = mybir.dt.bfloat16
AF = mybir.ActivationFunctionType
ALU = mybir.AluOpType

NCORES = 8
D = 1024
DR = 1280
DFF = 4096
DPL = 256
DIN = 6656
SEQ = 2048
NT = 512
KA = 31
KB = 4
ALPHA = 2.0 ** 0.25
EPS = 1e-5
GK = 0.7978845608028654
C_BDA, C_LAG, C_LAB, C_BDB, C_BRA, C_BRX, C_LAM, C_L1G, C_L1B, C_WDA, C_WDB = 0, 8, 16, 24, 34, 44, 54, 64, 72, 80, 328
NCOL = 368
V_HBRA, V_HBRX, V_C, V_HC, V_HLAG, V_HLAB, V_E, V_SP = 0, 10, 20, 30, 40, 48, 56, 66
NDV = 80
RG_BLOCKS = {0: (0, 1), 1: (0, 1, 2), 2: (1, 2, 3), 3: (2, 3, 4), 4: (3, 4)}
NRGB = 13
DEBUG = False
BIS = set()
PREFETCH_X = True
HEAD_OVERLAP = True
W_SCRATCH = True
POOL_OFF = True
STOP = None
TILES = None


class Res:
    __slots__ = ("name", "w", "r", "excl")

    def __init__(self, name, excl=False):
        self.name = name
        self.w = None
        self.r = {}
        self.excl = excl


class Sched:
    ENGS = ("pe", "act", "dve", "pool", "sp")

    def __init__(self, nc, es):
        self.nc = nc
        self.es = es
        self.prog = {e: [] for e in self.ENGS}
        self.cnt = {e: 0 for e in self.ENGS}
        self.sem = {e: es.enter_context(nc.semaphore("s_" + e)) for e in ("pe", "act", "dve", "pool")}
        self.seen = {e: {} for e in self.ENGS}
        self.dsem = {}

    def dma_sem(self, name):
        if name not in self.dsem:
            self.dsem[name] = [self.es.enter_context(self.nc.semaphore("d_" + name)), 0]
        return self.dsem[name]

    def _need(self, eng, tok, waits, skip_same):
        if tok is None:
            return
        kind, key, val, sem = tok
        if kind == "eng" and key == eng and skip_same:
            return
        k = (kind, key)
        if self.seen[eng].get(k, 0) >= val:
            return
        if k not in waits or waits[k][1] < val:
            waits[k] = (sem, val)

    def _deps(self, eng, reads, writes, is_dma):
        waits = {}
        for r in reads:
            self._need(eng, r.w, waits, (eng == "pe") and not is_dma)
            if r.excl:
                for t in r.r.values():
                    self._need(eng, t, waits, True)
        for w in writes:
            self._need(eng, w.w, waits, (eng == "pe") and not is_dma)
            for t in w.r.values():
                self._need(eng, t, waits, (eng == "pe") and not is_dma)
        for k, (sem, val) in waits.items():
            self.seen[eng][k] = val
        return list(waits.values())

    def _commit(self, tok, reads, writes):
        for r in reads:
            k = (tok[0], tok[1])
            if k not in r.r or r.r[k][2] < tok[2]:
                r.r[k] = tok
        for w in writes:
            w.w = tok
            w.r = {}

    def op(self, eng, fn, reads=(), writes=(), signal=True):
        reads, writes = flat(reads), flat(writes)
        waits = self._deps(eng, reads, writes, False)
        if signal:
            self.cnt[eng] += 1
            tok = ("eng", eng, self.cnt[eng], self.sem[eng])
        else:
            tok = ("eng", eng, self.cnt[eng] + 1, self.sem[eng])
        self.prog[eng].append((waits, fn, self.sem[eng] if signal else None, 1))
        self._commit(tok, reads, writes)
        return tok

    def dma(self, q, fn, semname, reads=(), writes=()):
        reads, writes = flat(reads), flat(writes)
        waits = self._deps(q, reads, writes, True)
        ds = self.dma_sem(semname)
        ds[1] += 16
        tok = ("dma", semname, ds[1], ds[0])
        self.prog[q].append((waits, fn, ds[0], 16))
        self._commit(tok, reads, writes)
        return tok

    def wait_all(self, eng, toks):
        waits = {}
        for t in toks:
            self._need(eng, t, waits, False)
        for k, (sem, val) in waits.items():
            self.seen[eng][k] = val
        self.prog[eng].append((list(waits.values()), None, None, 0))

    def replay(self, eng, e):
        for waits, fn, sem, inc in self.prog[eng]:
            for s, v in waits:
                e.wait_ge(s, v)
            if fn is None:
                continue
            ins = fn(e)
            if sem is not None:
                ins.then_inc(sem, inc)


class Buf:
    def __init__(self, ap3, res_list):
        self.t = ap3
        self.res = res_list

    def r(self, c):
        return self.res[c]


BLK = 2048


class Arena:
    def __init__(self, nc, es, name, nbytes):
        self.nbytes = nbytes
        self.t = es.enter_context(nc.sbuf_tensor(name, [128, nbytes // 4], F32))
        self.blocks = [Res("%s_b%d" % (name, i)) for i in range((nbytes + BLK - 1) // BLK)]
        self.off = 0

    def reset(self, off=0):
        self.off = off

    def alloc(self, C, n, dtype):
        esz = 2 if dtype == BF16 else 4
        nb = C * n * esz
        nb_al = (nb + 63) // 64 * 64
        lo = self.off
        assert lo + nb_al <= self.nbytes, (lo, nb_al, self.nbytes)
        self.off += nb_al
        ap = self.t[:, lo // 4:(lo + nb) // 4]
        if dtype == BF16:
            ap = ap.bitcast(BF16)
        ap = ap.rearrange("p (c n) -> p c n", c=C)
        res = []
        for c in range(C):
            a = lo + c * n * esz
            b = a + n * esz
            res.append(MultiRes([self.blocks[i] for i in range(a // BLK, (b - 1) // BLK + 1)]))
        return Buf(ap, res)


class MultiRes:
    def __init__(self, blocks):
        self.blocks = blocks


def flat(rs):
    out = []
    for r in rs:
        if isinstance(r, MultiRes):
            out.extend(r.blocks)
        elif isinstance(r, (list, tuple)):
            out.extend(flat(r))
        else:
            out.append(r)
    seen = set()
    o2 = []
    for r in out:
        if id(r) not in seen:
            seen.add(id(r))
            o2.append(r)
    return o2


def build_nc():
    nc = bass.Bass("TRN2", target_bir_lowering=False)

    def din(name, shape):
        return nc.dram_tensor(name, list(shape), F32, kind="ExternalInput").ap()

    def dout(name, shape):
        return nc.dram_tensor(name, list(shape), F32, kind="ExternalOutput").ap()

    xp = din("xp", [SEQ, D]); xs = din("xs", [64, D])
    ppr = din("ppr", [SEQ, DPL]); psm = din("psm", [64, DPL])
    sca = din("sca", [480, D]); scb = din("scb", [48, DR]); sh = din("sh", [16, DR])
    w_in = din("w_in", [D, DIN]); w_pa = din("w_pa", [D, D]); w_pb = din("w_pb", [DR, D])
    w_out = din("w_out", [D, D]); w_ff1 = din("w_ff1", [D, DFF]); w_ff2 = din("w_ff2", [DFF, D])
    w_pg = din("w_pg", [D, D]); w_pp = din("w_pp", [DPL, D])
    rgw = din("rgw", [2, 128, 2 * NRGB * 128])
    colv = din("colv", [128, NCOL]); bcv = din("bcv", [128, 4 * D])
    yp = dout("yp", [SEQ, D]); ys = dout("ys", [64, D])
    ncap = dout("ncap", [30, D]); ncbp = dout("ncbp", [3, DR]); nhp = dout("nhp", [10, 128])
    ncas_new = dout("ncas_new", [64, D]); ncas_old = dout("ncas_old", [416, D])
    ncbs = dout("ncbs", [48, DR]); nhs = dout("nhs", [16, DR])
    dA = nc.dram_tensor("dA", [8, 128, KA * 128], BF16, kind="Internal").ap()
    dB = nc.dram_tensor("dB", [2, 128, 5 * KB * 128], BF16, kind="Internal").ap()
    dbg = {}
    if DEBUG:
        for nm, C in (("u2", 8), ("convo", 8), ("ca2", 8), ("cb", 10), ("hh", 10), ("aa", 10), ("bt", 10), ("hg2", 10), ("mixin", 8),
                      ("x1T", 8), ("hT", 32)):
            dbg[nm] = nc.dram_tensor("dbg_" + nm, [128, C, NT], F32, kind="ExternalOutput").ap()
        dbg["x1"] = nc.dram_tensor("dbg_x1", [128, 4, D], F32, kind="ExternalOutput").ap()

    es = ExitStack()
    S = Sched(nc, es)

    def sb(name, shape, dt):
        return es.enter_context(nc.sbuf_tensor(name, list(shape), dt))

    colv_t = sb("colv_t", [128, NCOL], F32); r_colv = Res("colv")
    dv = sb("dv", [128, NDV], F32); r_dv = Res("dv")
    wdah = sb("wdah", [128, 8 * KA], F32); r_wdah = Res("wdah")
    bc_t = sb("bc_t", [128, 4 * D], F32); r_bc = Res("bc")
    ident_f = sb("ident_f", [128, 128], F32); ident_b = sb("ident_b", [128, 128], BF16); r_id = Res("ident")
    ones_f = sb("ones_f", [128, 128], F32); r_ones = Res("ones")
    eps_t = sb("eps_t", [128, 1], F32); quarter_t = sb("quarter_t", [128, 1], F32); r_eps = Res("eps")
    cnh = sb("cnh", [128, 8], F32); r_cnh = Res("cnh")
    hcar = sb("hcar", [128, 10], F32); r_hcar = [Res("hcar%d" % i) for i in range(10)]
    ucar = sb("ucar", [128, 8, 30], BF16); r_ucar = Res("ucar")
    bcar = sb("bcar", [128, 10, 4], BF16); r_bcar = Res("bcar")
    NTMP = 12
    tmps = [sb("tmp%d" % i, [128, NT], F32) for i in range(NTMP)]
    r_tmps = [Res("tmp%d" % i) for i in range(NTMP)]
    tmp_ptr = [0]

    def tmp():
        i = tmp_ptr[0] % NTMP
        tmp_ptr[0] += 1
        return tmps[i], r_tmps[i]

    lnm = sb("lnm", [128, NT], F32); r_lnm = Res("lnm")
    lnr = sb("lnr", [128, NT], F32); r_lnr = Res("lnr")
    stat = sb("stat", [128, 64], F32)
    r_stat = [Res("stat%d" % i) for i in range(16)]
    stat_ptr = [0]

    def stat4():
        i = stat_ptr[0] % 16
        stat_ptr[0] += 1
        return stat[:, 4 * i:4 * i + 4], r_stat[i]

    ln2s = sb("ln2s", [128, 4, 4], F32); r_ln2s = Res("ln2s")
    bnst = sb("bnst", [128, 4, 12], F32)
    r_bnst = [Res("bnst%d" % i) for i in range(4)]
    bn_ptr = [0]

    WSLOT = 5120
    NW = 4
    wslots = [sb("wslot%d" % i, [128, WSLOT], BF16) for i in range(NW)]
    r_wslots = [Res("wslot%d" % i) for i in range(NW)]
    DSLOT = KA * 128
    ND = 2
    dslots = [sb("dslot%d" % i, [128, DSLOT], BF16) for i in range(ND)]
    r_dslots = [Res("dslot%d" % i) for i in range(ND)]

    pairs = [es.enter_context(nc.psum_tensor("pp%d" % i, [128, 2 * NT], F32)) for i in range(4)]
    r_banks = [Res("bank%d" % i, excl=True) for i in range(8)]
    bank_ptr = [0]

    def bank():
        i = bank_ptr[0] % 8
        bank_ptr[0] += 1
        return pairs[i // 2][:, (i % 2) * NT:(i % 2 + 1) * NT], r_banks[i]

    def bankpair():
        if bank_ptr[0] % 2:
            bank_ptr[0] += 1
        i = bank_ptr[0] % 8
        bank_ptr[0] += 2
        return pairs[i // 2], (r_banks[i], r_banks[i + 1])

    ARENA = 100 * 1024
    arena = Arena(nc, es, "arena", ARENA)

    def ACT(fn, reads, writes):
        return S.op("act", fn, flat(reads), flat(writes))

    def DVE(fn, reads, writes):
        return S.op("dve", fn, flat(reads), flat(writes))

    def POOL(fn, reads, writes):
        return S.op("pool", fn, flat(reads), flat(writes))

    def PE(fn, reads, writes, signal):
        return S.op("pe", fn, flat(reads), flat(writes), signal=signal)

    def act(out, in_, func, scale=1.0, bias=0.0):
        return lambda e: e.activation(out=out, in_=in_, func=func, scale=scale, bias=bias)

    def tt(out, in0, in1, op):
        return lambda e: e.tensor_tensor(out=out, in0=in0, in1=in1, op=op)

    def ts(out, in0, s1, s2, op0, op1):
        return lambda e: e.tensor_scalar(out=out, in0=in0, scalar1=s1, scalar2=s2, op0=op0, op1=op1)

    def ts1(out, in0, s1, op0):
        return lambda e: e.tensor_single_scalar(out=out, in_=in0, scalar=s1, op=op0)

    def stt(out, in0, scalar, in1, op0, op1):
        return lambda e: e.scalar_tensor_tensor(out=out, in0=in0, scalar=scalar, in1=in1, op0=op0, op1=op1)

    def mm(out, lhsT, rhs, start, stop):
        return lambda e: e.matmul(out, lhsT=lhsT, rhs=rhs, start=start, stop=stop)

    def tr(out, in_, ident):
        return lambda e: e.transpose(out, in_, ident)

    S.dma("sp", lambda e: e.dma_start(out=colv_t[:], in_=colv), "const0", [], [r_colv])
    S.dma("sp", lambda e: e.dma_start(out=bc_t[:], in_=bcv), "const1", [], [r_bc])
    POOL(lambda e: e.memset(ident_f[:], 0.0), [], [r_id])
    POOL(lambda e: e.affine_select(out=ident_f[:], in_=ident_f[:], pattern=[[-1, 128]], compare_op=ALU.not_equal,
                                   fill=1.0, base=0, channel_multiplier=1), [r_id], [r_id])
    POOL(lambda e: e.tensor_copy(out=ident_b[:], in_=ident_f[:]), [r_id], [r_id])
    POOL(lambda e: e.memset(ones_f[:], 1.0 / D), [], [r_ones])
    POOL(lambda e: e.memset(eps_t[:], EPS), [], [r_eps])
    POOL(lambda e: e.memset(quarter_t[:], 0.25), [r_eps], [r_eps])
    POOL(lambda e: e.memset(cnh[:], -0.5), [], [r_cnh])
    POOL(lambda e: e.memset(hcar[:], 0.0), [], r_hcar)
    ACT(act(dv[:, V_E:V_E + 10], colv_t[:, C_LAM:C_LAM + 10], AF.Exp, scale=-1.0), [r_colv], [r_dv])
    ACT(act(dv[:, V_SP:V_SP + 10], dv[:, V_E:V_E + 10], AF.Ln, bias=1.0), [r_dv], [r_dv])
    DVE(ts1(dv[:, V_C:V_C + 10], dv[:, V_SP:V_SP + 10], -8.0, ALU.mult), [r_dv], [r_dv])
    DVE(ts1(dv[:, V_HC:V_HC + 10], dv[:, V_SP:V_SP + 10], -4.0, ALU.mult), [r_dv], [r_dv])
    DVE(ts1(dv[:, V_HBRA:V_HBRA + 10], colv_t[:, C_BRA:C_BRA + 10], 0.5, ALU.mult), [r_colv], [r_dv])
    DVE(ts1(dv[:, V_HBRX:V_HBRX + 10], colv_t[:, C_BRX:C_BRX + 10], 0.5, ALU.mult), [r_colv], [r_dv])
    DVE(ts1(dv[:, V_HLAG:V_HLAG + 8], colv_t[:, C_LAG:C_LAG + 8], 0.5, ALU.mult), [r_colv], [r_dv])
    DVE(ts1(dv[:, V_HLAB:V_HLAB + 8], colv_t[:, C_LAB:C_LAB + 8], 0.5, ALU.mult), [r_colv], [r_dv])
    DVE(ts1(wdah[:], colv_t[:, C_WDA:C_WDA + 8 * KA], 0.5, ALU.mult), [r_colv], [r_wdah])
    r_dA = [Res("dA%d" % c) for c in range(8)]
    r_dB = [Res("dB%d" % h) for h in range(2)]

    def wsrc(w, K, c0, nc_):
        return w.rearrange("(k p) n -> p k n", p=128)[:, 0:K, c0:c0 + nc_]

    def tile_plan():
        pl = []
        for i in range(2):
            pl.append(("in_av", i, wsrc(w_in, 8, 512 * i, 512), 8, 512))
            pl.append(("in_ag", i, wsrc(w_in, 8, 1024 + 512 * i, 512), 8, 512))
        for h in range(2):
            pl.append(("in_bx", h, wsrc(w_in, 8, 2048 + 640 * h, 640), 8, 640))
        for h in range(2):
            pl.append(("rg", h, rgw[h].rearrange("p (k n) -> p k n", k=2 * NRGB), 2 * NRGB, 128))
            pl.append(("in_bg", h, wsrc(w_in, 8, 3328 + 640 * h, 640), 8, 640))
        for i in range(2):
            pl.append(("pb", i, wsrc(w_pb, 10, 512 * i, 512), 10, 512))
            pl.append(("in_gb", i, wsrc(w_in, 8, 5632 + 512 * i, 512), 8, 512))
        for i in range(2):
            pl.append(("pa", i, wsrc(w_pa, 8, 512 * i, 512), 8, 512))
            pl.append(("in_ga", i, wsrc(w_in, 8, 4608 + 512 * i, 512), 8, 512))
        for i in range(2):
            pl.append(("wo", i, wsrc(w_out, 8, 512 * i, 512), 8, 512))
        for i in range(8):
            pl.append(("ff1", i, wsrc(w_ff1, 8, 512 * i, 512), 8, 512))
        for i in range(2):
            pl.append(("pg", i, wsrc(w_pg, 8, 512 * i, 512), 8, 512))
        pl.append(("ppj", 0, wsrc(w_pp, 2, 0, 1024), 2, 1024))
        for hf in range(2):
            for kg in range(4):
                src = w_ff2.rearrange("(k p) n -> p k n", p=128)[:, 8 * kg:8 * kg + 8, 512 * hf:512 * hf + 512]
                pl.append(("ff2", hf * 4 + kg, src, 8, 512))
        return pl

    NTILES = 5
    wplan = []
    for t in range(NTILES):
        wplan.extend(tile_plan())
    wstate = {"loaded": 0, "cur": 0}

    NPIECE = len(tile_plan())
    wscr = nc.dram_tensor("wscr", [NPIECE, 128, WSLOT], BF16, kind="Internal").ap()
    r_wscr = [Res("wscr%d" % i) for i in range(NPIECE)]

    def w_advance():
        lim = min(len(wplan), wstate["cur"] + NW)
        while wstate["loaded"] < lim:
            j = wstate["loaded"]
            nm, idx, src, K, ncol = wplan[j]
            sl, rs = wslots[j % NW], r_wslots[j % NW]
            jl = j % NPIECE
            if (not W_SCRATCH) or j < NPIECE or (j < 2 * NPIECE and jl % 2 == 1):
                dst = sl[:, 0:K * ncol].rearrange("p (k n) -> p k n", k=K)
                S.dma("pool", (lambda e, dst=dst, src=src: e.dma_start(out=dst, in_=src)), "w%d" % (j % NW), [], [rs])
            else:
                S.dma("pool", (lambda e, sl=sl, jl=jl, n_=K * ncol: e.dma_start(out=sl[:, 0:n_], in_=wscr[jl][:, 0:n_])),
                      "w%d" % (j % NW), [r_wscr[jl]], [rs])
            wstate["loaded"] += 1

    def w_take(name, idx, n=1):
        w_advance()
        out = []
        for q in range(n):
            j = wstate["cur"] + q
            nm, ix, src, K, ncol = wplan[j]
            assert j < wstate["loaded"], (j, wstate)
            sl, rs = wslots[j % NW], r_wslots[j % NW]
            out.append((sl[:, 0:K * ncol].rearrange("p (k n) -> p k n", k=K), rs, nm, ix))
            jl_ = j % NPIECE
            if W_SCRATCH and ((j < NPIECE and jl_ % 2 == 0) or (NPIECE <= j < 2 * NPIECE and jl_ % 2 == 1)):
                S.dma("sp", (lambda e, sl=sl, jl=jl_, n_=K * ncol: e.dma_start(out=wscr[jl][:, 0:n_], in_=sl[:, 0:n_])),
                      "wb%d" % (j % NW), [rs], [r_wscr[jl_]])
        assert out[0][2] == name and out[0][3] == idx, (out[0][2:], name, idx)
        wstate["cur"] += n
        return out

    dstate = {"n": 0}

    def d_load(src, ncols, rsrc):
        j = dstate["n"]
        dstate["n"] += 1
        sl, rs = dslots[j % ND], r_dslots[j % ND]
        S.dma("sp", (lambda e, sl=sl, src=src, ncols=ncols: e.dma_start(out=sl[:, 0:ncols], in_=src)),
              "d%d" % (j % ND), [rsrc], [rs])
        return sl, rs

    out_toks = []

    def early_loads(kind, j):
        prompt = (kind == "p")
        N = NT if prompt else 64
        x_src = xp[j * NT:j * NT + N, :] if prompt else xs
        p_src = ppr[j * NT:j * NT + N, :] if prompt else psm
        arena.reset(0)
        mixin = arena.alloc(8, N, BF16)
        pT = arena.alloc(2, N, BF16)
        offAB = arena.off
        xbf = arena.alloc(4, D, BF16)
        pbf = arena.alloc(4, DPL, BF16)
        if prompt:
            S.dma("pool", lambda e: e.dma_start(out=xbf.t[:, :, :], in_=x_src.rearrange("(s p) d -> p s d", p=128)),
                  "xbf", [], flat(xbf.res))
            S.dma("pool", lambda e: e.dma_start(out=pbf.t[:, :, :], in_=p_src.rearrange("(s p) d -> p s d", p=128)),
                  "pbf", [], flat(pbf.res))
        else:
            S.dma("pool", lambda e: e.dma_start(out=xbf.t[0:64, 0, :], in_=x_src), "xbf", [], flat(xbf.res))
            S.dma("pool", lambda e: e.dma_start(out=pbf.t[0:64, 0, :], in_=p_src), "pbf", [], flat(pbf.res))
        return dict(mixin=mixin, pT=pT, offAB=offAB, xbf=xbf, pbf=pbf, off=arena.off)

    prefetched = {}
    dpre = {}

    def emit_tile(kind, j, nxt_tile=None):
        prompt = (kind == "p")
        N = NT if prompt else 64
        NS = 4 if prompt else 1
        P = 128 if prompt else 64
        first = prompt and j == 0
        last = prompt and j == 3
        tok0 = j * NT
        x_src = xp[tok0:tok0 + N, :] if prompt else xs
        p_src = ppr[tok0:tok0 + N, :] if prompt else psm
        y_dst = yp[tok0:tok0 + N, :] if prompt else ys
        LA = 30 + NT if prompt else 16 * 34
        LB = 3 + NT if prompt else 16 * 7

        pre = prefetched.pop((kind, j), None)
        if pre is None:
            pre = early_loads(kind, j)
        mixin, pT, offAB, xbf, pbf = pre["mixin"], pre["pT"], pre["offAB"], pre["xbf"], pre["pbf"]
        arena.reset(pre["off"])
        xT = arena.alloc(8, N, BF16)
        _off_u = arena.off
        uA = arena.alloc(8, 544, BF16)
        bxb = arena.alloc(10, 516 if prompt else 112, BF16)
        _save = arena.off
        arena.reset(_off_u)
        m_b = arena.alloc(8, N, F32)
        assert arena.off <= _save
        arena.reset(_save)
        if last or not prompt:
            ufp = arena.alloc(8, 64, F32)
            h0b = arena.alloc(10, 16, F32)
        if not prompt:
            sca_t = arena.alloc(4, D, BF16)
            scb_t = arena.alloc(1, DR, BF16)
            sh_t = arena.alloc(1, DR, F32)
        assert arena.off <= 50 * 1024, arena.off
        convo = arena.alloc(8, N, F32)
        _save = arena.off
        arena.reset(offAB)
        ca2 = arena.alloc(8, N, BF16)
        arena.reset(_save)
        cb = arena.alloc(5, N, F32)
        cbb = arena.alloc(5, N, BF16)
        hg2 = arena.alloc(10, N, BF16)
        if last or not prompt:
            bxf = arena.alloc(10, 64, F32)
            hl = arena.alloc(10, 16, F32)
            stg = arena.alloc(1, DR, F32)

        def uview(buf, c, L, ctx, a, b):
            if prompt:
                return buf.t[:, c, a:b]
            return buf.t[:, c, 16 * a:16 * b]

        def nview(ap2):
            return ap2

        def transpose_in(src_buf, nfc, dst_buf):
            for fc in range(nfc):
                bk, rb = bank()
                bkb = bk[:, 0:NT // 2].bitcast(BF16)
                for s in range(NS):
                    PE(tr(bkb[:, s * 128:s * 128 + P], src_buf.t[0:P, s, fc * 128:(fc + 1) * 128], ident_b[0:P, 0:P]),
                       [src_buf.res[s], r_id], [rb], signal=(s == NS - 1))
                ACT(act(dst_buf.t[:, fc, 0:N], bkb[:, 0:N], AF.Copy), [rb], [dst_buf.res[fc]])

        transpose_in(xbf, 8, xT)
        transpose_in(pbf, 2, pT)

        if STOP == 'T':
            return
        if prompt and first:
            DVE(lambda e: e.memset(uA.t[:, :, 0:30], 0.0), [], flat(uA.res))
            DVE(lambda e: e.memset(bxb.t[:, :, 0:3], 0.0), [], flat(bxb.res))
        elif prompt:
            DVE(lambda e: e.tensor_copy(out=uA.t[:, :, 0:30], in_=ucar[:, :, :]), [r_ucar], flat(uA.res))
            DVE(lambda e: e.tensor_copy(out=bxb.t[:, :, 0:3], in_=bcar[:, :, 0:3]), [r_bcar], flat(bxb.res))
        else:
            for s_ in range(4):
                S.dma("pool", (lambda e, s_=s_: e.dma_start(out=sca_t.t[0:120, s_, :], in_=sca[120 * s_:120 * s_ + 120, :])),
                      "sca%d" % s_, [], [sca_t.res[s_]])
            for fc in range(8):
                for s_ in range(4):
                    bk, rb = bank()
                    bkb = bk[:, 0:NT // 2].bitcast(BF16)
                    PE(tr(bkb[:, 0:120], sca_t.t[0:120, s_, fc * 128:(fc + 1) * 128], ident_b[0:120, 0:120]),
                       [sca_t.res[s_], r_id], [rb], signal=True)
                    ACT(act(uA.t[:, fc, 120 * s_:120 * s_ + 120], bkb[:, 0:120], AF.Copy, scale=2.0), [rb], [uA.res[fc]])
            S.dma("pool", lambda e: e.dma_start(out=scb_t.t[0:48, 0, :], in_=scb), "scb", [], flat(scb_t.res))
            S.dma("sp", lambda e: e.dma_start(out=sh_t.t[0:16, 0, :], in_=sh), "sh", [], flat(sh_t.res))
            for fc in range(10):
                bk, rb = bank()
                bkb = bk[:, 0:NT // 2].bitcast(BF16)
                PE(tr(bkb[:, 0:48], scb_t.t[0:48, 0, fc * 128:(fc + 1) * 128], ident_b[0:48, 0:48]),
                   [scb_t.res[0], r_id], [rb], signal=True)
                ACT(act(bxb.t[:, fc, 0:48], bkb[:, 0:48], AF.Copy), [rb], [bxb.res[fc]])
                bk, rb = bank()
                PE(tr(bk[:, 0:16], sh_t.t[0:16, 0, fc * 128:(fc + 1) * 128], ident_f[0:16, 0:16]),
                   [sh_t.res[0], r_id], [rb], signal=True)
                ACT(act(h0b.t[:, fc, :], bk[:, 0:16], AF.Copy), [rb], [h0b.res[fc]])

        need_state = last or not prompt

        build_list = [("B", 0), ("A", 0), ("A", 1), ("A", 2), ("A", 3), ("A", 4), ("B", 1), ("A", 5), ("A", 6), ("A", 7)]

        def build_next():
            if not build_list:
                return
            typ, q = build_list.pop(0)
            if typ == "A":
                src, ncols, rsrc, in1 = dA[q], KA * 128, r_dA[q], wdah[:, q * KA:(q + 1) * KA]
            else:
                src, ncols, rsrc, in1 = dB[q], 20 * 128, r_dB[q], colv_t[:, C_WDB + 20 * q:C_WDB + 20 * q + 20]
            jd = dstate["n"]
            dstate["n"] += 1
            sl, rs = dslots[jd % ND], r_dslots[jd % ND]
            K = ncols // 128
            o3 = sl[:, 0:ncols].rearrange("p (k n) -> p k n", k=K)
            i0 = ident_f[:].unsqueeze(1).to_broadcast([128, K, 128])
            i1 = in1.unsqueeze(2).to_broadcast([128, K, 128])
            DVE(tt(o3, i0, i1, ALU.mult), [r_id, r_wdah, r_colv], [rs])
            S.dma("sp", (lambda e, sl=sl, src=src, ncols=ncols: e.dma_start(out=src, in_=sl[:, 0:ncols])),
                  "db%d" % (jd % ND), [rs], [rsrc])

        for i in range(2):
            (wav, rav, _, _), (wag, rag, _, _) = w_take("in_av", i, 2)
            for m in range(4):
                c = 4 * i + m
                bv, rbv = bank()
                for kc in range(8):
                    PE(mm(bv[:, 0:N], wav[:, kc, m * 128:(m + 1) * 128], xT.t[:, kc, 0:N], kc == 0, kc == 7),
                       [rav, xT.res[kc]], [rbv], signal=(kc == 7))
                bg, rbg = bank()
                for kc in range(8):
                    PE(mm(bg[:, 0:N], wag[:, kc, m * 128:(m + 1) * 128], xT.t[:, kc, 0:N], kc == 0, kc == 7),
                       [rag, xT.res[kc]], [rbg], signal=(kc == 7))
                t1, rt1 = tmp()
                ACT(act(t1[:, 0:N], bg[:, 0:N], AF.Tanh, scale=0.5), [rbg], [rt1])
                DVE(stt(uview(uA, c, 34, 30, 30, 30 + N) if prompt else uview(uA, c, 34, 30, 30, 34),
                        nview(t1[:, 0:N]), 1.0, nview(bv[:, 0:N]), ALU.add, ALU.mult), [rt1, rbv], [uA.res[c]])
                if need_state:
                    n0 = N - 30 if prompt else 0
                    nn = 30 if prompt else 64
                    DVE(stt(ufp.t[:, c, 0:nn], t1[:, n0:n0 + nn], 1.0, bv[:, n0:n0 + nn], ALU.add, ALU.mult),
                        [rt1, rbv], [ufp.res[c]])
                    DVE(ts1(ufp.t[:, c, 0:nn], ufp.t[:, c, 0:nn], 0.5, ALU.mult), [ufp.res[c]], [ufp.res[c]])
                if first:
                    build_next()
        if first:
            while build_list:
                build_next()

        yield "head"
        if STOP == 'S1':
            return

        bx_state = {}

        def s2_group(c):
            h, m = divmod(c, 5)
            if m == 0:
                ((bx_state["w"], bx_state["r"], _, _),) = w_take("in_bx", h, 1)
            wbx, rbx = bx_state["w"], bx_state["r"]
            bk, rb = bank()
            for kc in range(8):
                PE(mm(bk[:, 0:N], wbx[:, kc, m * 128:(m + 1) * 128], xT.t[:, kc, 0:N], kc == 0, kc == 7),
                   [rbx, xT.res[kc]], [rb], signal=(kc == 7))
            ACT(act(uview(bxb, c, 7, 3, 3, 3 + N) if prompt else uview(bxb, c, 7, 3, 3, 7),
                    bk[:, 0:N], AF.Copy), [rb], [bxb.res[c]])
            if need_state:
                n0 = N - 3 if prompt else 0
                nn = 3 if prompt else 64
                ACT(act(bxf.t[:, c, 0:nn], bk[:, n0:n0 + nn], AF.Copy), [rb], [bxf.res[c]])

        def get_diag(src, ncols, rsrc, build_in1):
            pk = dpre.pop((kind, j, id(rsrc)), None)
            if pk is not None:
                return pk
            return d_load(src, ncols, rsrc)

        def conva_chunk(c):
            sl, rs = get_diag(dA[c], KA * 128, r_dA[c], wdah[:, c * KA:(c + 1) * KA])
            dg = sl[:, 0:KA * 128].rearrange("p (k n) -> p k n", k=KA)
            bk, rb = bank()
            for k in range(KA):
                rhs = uview(uA, c, 34, 30, k, k + N) if prompt else uview(uA, c, 34, 30, k, k + 4)
                PE(mm(bk[:, 0:N], dg[:, k, :], rhs, k == 0, k == KA - 1), [rs, uA.res[c]], [rb],
                   signal=(k == KA - 1))
            ACT(act(convo.t[:, c, 0:N], bk[:, 0:N], AF.Identity, bias=colv_t[:, C_BDA + c:C_BDA + c + 1]),
                [rb, r_colv], [convo.res[c]])

        mean_t, rmean = lnm, r_lnm
        rstd_t, rrstd = lnr, r_lnr

        def lna_stats():
            bmean, rbmean = bank()
            for c in range(8):
                PE(mm(bmean[:, 0:N], ones_f[:], convo.t[:, c, 0:N], c == 0, c == 7), [r_ones, convo.res[c]], [rbmean],
                   signal=(c == 7))
            bex2, rbex2 = bank()
            for c in range(8):
                t1, rt1 = tmp()
                ACT(act(t1[:, 0:N], convo.t[:, c, 0:N], AF.Square), [convo.res[c]], [rt1])
                PE(mm(bex2[:, 0:N], ones_f[:], t1[:, 0:N], c == 0, c == 7), [r_ones, rt1], [rbex2], signal=(c == 7))
            ACT(act(mean_t[:, 0:N], bmean[:, 0:N], AF.Copy), [rbmean], [rmean])
            DVE(tt(rstd_t[:, 0:N], mean_t[:, 0:N], mean_t[:, 0:N], ALU.mult), [rmean], [rrstd])
            DVE(tt(rstd_t[:, 0:N], bex2[:, 0:N], rstd_t[:, 0:N], ALU.subtract), [rbex2, rrstd], [rrstd])
            ACT(act(rstd_t[:, 0:N], rstd_t[:, 0:N], AF.Sqrt, bias=eps_t[:, 0:1]), [rrstd, r_eps], [rrstd])
            DVE(lambda e: e.reciprocal(out=rstd_t[:, 0:N], in_=rstd_t[:, 0:N]), [rrstd], [rrstd])

        def lna_norm(c):
            d1, rd1 = tmp()
            EW = POOL if POOL_OFF else DVE
            EW(tt(d1[:, 0:N], convo.t[:, c, 0:N], mean_t[:, 0:N], ALU.subtract), [convo.res[c], rmean], [rd1])
            EW(tt(d1[:, 0:N], d1[:, 0:N], rstd_t[:, 0:N], ALU.mult), [rd1, rrstd], [rd1])
            t1, rt1 = tmp()
            ACT(act(t1[:, 0:N], d1[:, 0:N], AF.Tanh, scale=dv[:, V_HLAG + c:V_HLAG + c + 1],
                    bias=dv[:, V_HLAB + c:V_HLAB + c + 1]), [rd1, r_dv], [rt1])
            EW(ts(d1[:, 0:N], d1[:, 0:N], colv_t[:, C_LAG + c:C_LAG + c + 1], colv_t[:, C_LAB + c:C_LAB + c + 1],
                  ALU.mult, ALU.add), [rd1, r_colv], [rd1])
            DVE(stt(ca2.t[:, c, 0:N], t1[:, 0:N], 1.0, d1[:, 0:N], ALU.add, ALU.mult), [rt1, rd1], [ca2.res[c]])

        rg_state = {}

        def convb_half(h):
            sl, rs = get_diag(dB[h], 20 * 128, r_dB[h], colv_t[:, C_WDB + 20 * h:C_WDB + 20 * h + 20])
            dg = sl[:, 0:20 * 128].rearrange("p (k n) -> p k n", k=20)
            for m in range(5):
                c = 5 * h + m
                bk, rb = bank()
                for k in range(KB):
                    rhs = uview(bxb, c, 7, 3, k, k + N) if prompt else uview(bxb, c, 7, 3, k, k + 4)
                    PE(mm(bk[:, 0:N], dg[:, m * KB + k, :], rhs, k == 0, k == KB - 1), [rs, bxb.res[c]], [rb],
                       signal=(k == KB - 1))
                ACT(act(cbb.t[:, m, 0:N], bk[:, 0:N], AF.Identity, bias=colv_t[:, C_BDB + c:C_BDB + c + 1]),
                    [rb, r_colv], [cbb.res[m]])
                ACT(act(cb.t[:, m, 0:N], bk[:, 0:N], AF.Identity, bias=colv_t[:, C_BDB + c:C_BDB + c + 1]),
                    [rb, r_colv], [cb.res[m]])
            (wrg, rrg, _, _), (wbg, rbgw, _, _) = w_take("rg", h, 2)
            rg_state.update(wrg=wrg, rrg=rrg, wbg=wbg, rbgw=rbgw)

        blk_idx = {}
        blk = 0
        for g in range(2):
            for m in range(5):
                for kk in RG_BLOCKS[m]:
                    blk_idx[(g, m, kk)] = blk
                    blk += 1

        def rg_stage1(h, m):
            wrg, rrg, wbg, rbgw = rg_state["wrg"], rg_state["rrg"], rg_state["wbg"], rg_state["rbgw"]
            c = 5 * h + m
            kks = RG_BLOCKS[m]
            br, rbr = bank()
            for q, kk in enumerate(kks):
                PE(mm(br[:, 0:N], wrg[:, blk_idx[(0, m, kk)], :], cbb.t[:, kk, 0:N], q == 0, q == len(kks) - 1),
                   [rrg, cbb.res[kk]], [rbr], signal=(q == len(kks) - 1))
            bi, rbi = bank()
            for q, kk in enumerate(kks):
                PE(mm(bi[:, 0:N], wrg[:, blk_idx[(1, m, kk)], :], cbb.t[:, kk, 0:N], q == 0, q == len(kks) - 1),
                   [rrg, cbb.res[kk]], [rbi], signal=(q == len(kks) - 1))
            bgt, rbgt = bank()
            for kc in range(8):
                PE(mm(bgt[:, 0:N], wbg[:, kc, m * 128:(m + 1) * 128], xT.t[:, kc, 0:N], kc == 0, kc == 7),
                   [rbgw, xT.res[kc]], [rbgt], signal=(kc == 7))
            tr_, rtr = tmp()
            ACT(act(tr_[:, 0:N], br[:, 0:N], AF.Tanh, scale=0.5, bias=dv[:, V_HBRA + c:V_HBRA + c + 1]),
                [rbr, r_dv], [rtr])
            ti_, rti = tmp()
            ACT(act(ti_[:, 0:N], bi[:, 0:N], AF.Tanh, scale=0.5, bias=dv[:, V_HBRX + c:V_HBRX + c + 1]),
                [rbi, r_dv], [rti])
            a_, ra = tmp()
            ACT(act(a_[:, 0:N], tr_[:, 0:N], AF.Exp, scale=dv[:, V_HC + c:V_HC + c + 1],
                    bias=dv[:, V_HC + c:V_HC + c + 1]), [rtr, r_dv], [ra])
            a2_, ra2 = tr_, rtr
            if POOL_OFF:
                POOL(tt(a2_[:, 0:N], a_[:, 0:N], a_[:, 0:N], ALU.mult), [ra, rtr], [ra2])
            else:
                ACT(act(a2_[:, 0:N], tr_[:, 0:N], AF.Exp, scale=dv[:, V_C + c:V_C + c + 1],
                        bias=dv[:, V_C + c:V_C + c + 1]), [rtr, r_dv], [ra2])
            sq_, rsq = tmp()
            ACT(act(sq_[:, 0:N], bgt[:, 0:N], AF.Square), [rbgt], [rsq])
            DVE(stt(ti_[:, 0:N], ti_[:, 0:N], 1.0, cb.t[:, m, 0:N], ALU.add, ALU.mult), [rti, cb.res[m]], [rti])
            DVE(ts(sq_[:, 0:N], sq_[:, 0:N], 0.044715 * GK, GK, ALU.mult, ALU.add), [rsq], [rsq])
            DVE(tt(sq_[:, 0:N], sq_[:, 0:N], bgt[:, 0:N], ALU.mult), [rsq, rbgt], [rsq])
            ACT(act(sq_[:, 0:N], sq_[:, 0:N], AF.Tanh), [rsq], [rsq])
            DVE(stt(sq_[:, 0:N], sq_[:, 0:N], 1.0, bgt[:, 0:N], ALU.add, ALU.mult), [rsq, rbgt], [rsq])
            return dict(c=c, m=m, a_=a_, ra=ra, a2_=a2_, ra2=ra2, ti_=ti_, rti=rti, sq_=sq_, rsq=rsq)

        def rg_stage2(st):
            c, m, a_, ra, a2_, ra2, ti_, rti, sq_, rsq = (st[k] for k in ("c", "m", "a_", "ra", "a2_", "ra2", "ti_", "rti",
                                                                         "sq_", "rsq"))
            ACT(act(a2_[:, 0:N], a2_[:, 0:N], AF.Sqrt, scale=-0.25, bias=quarter_t[:, 0:1]), [ra2, r_eps], [ra2])
            DVE(tt(a2_[:, 0:N], a2_[:, 0:N], ti_[:, 0:N], ALU.mult), [ra2, rti], [ra2])
            hh, rhh = tmp()
            if DEBUG and first:
                dbg_dump1("aa", c, a_[:, 0:N], ra)
                dbg_dump1("bt", c, a2_[:, 0:N], ra2)
            if prompt:
                if first:
                    DVE(ts1(a2_[:, 0:1], ti_[:, 0:1], 0.5, ALU.mult), [rti, ra2], [ra2])
                DVE((lambda e, hh=hh, a_=a_, a2_=a2_, c=c: e.tensor_tensor_scan(
                    out=hh[:, 0:N], data0=a_[:, 0:N], data1=a2_[:, 0:N],
                    initial=(0.0 if first else hcar[:, c:c + 1]), op0=ALU.mult, op1=ALU.add)),
                    [ra, ra2, r_hcar[c]], [rhh])
                DVE(lambda e, hh=hh, c=c: e.tensor_copy(out=hcar[:, c:c + 1], in_=hh[:, N - 1:N]), [rhh], [r_hcar[c]])
            else:
                for t_ in range(4):
                    prev = h0b.t[:, c, :] if t_ == 0 else hh[:, 16 * (t_ - 1):16 * t_]
                    rprev = [h0b.res[c]] if t_ == 0 else [rhh]
                    DVE(tt(hh[:, 16 * t_:16 * t_ + 16], a_[:, 16 * t_:16 * t_ + 16], prev, ALU.mult), [ra] + rprev, [rhh])
                    DVE(tt(hh[:, 16 * t_:16 * t_ + 16], hh[:, 16 * t_:16 * t_ + 16], a2_[:, 16 * t_:16 * t_ + 16], ALU.add),
                        [rhh, ra2], [rhh])
                DVE(lambda e, hh=hh, c=c: e.tensor_copy(out=hl.t[:, c, :], in_=hh[:, 48:64]), [rhh], [hl.res[c]])
            DVE(tt(hg2.t[:, c, 0:N], sq_[:, 0:N], hh[:, 0:N], ALU.mult), [rsq, rhh], [hg2.res[c]])
            if DEBUG and first:
                dbg_dump1("cb", c, cb.t[:, m, 0:N], cb.res[m])
                dbg_dump1("hh", c, hh[:, 0:N], rhh)

        for c in range(10):
            s2_group(c)
        if prompt and not last:
            DVE(lambda e: e.tensor_copy(out=bcar[:, :, 0:3], in_=bxb.t[:, :, NT:NT + 3]), flat(bxb.res), [r_bcar])
        if STOP == 'S3':
            return
        ca_next = [0]

        def filler(n):
            for _ in range(n):
                if ca_next[0] < 8:
                    conva_chunk(ca_next[0])
                    ca_next[0] += 1

        def rg_half(h, fill):
            for grp, nf in zip(((0, 1), (2, 3), (4,)), fill):
                sts = [rg_stage1(h, m) for m in grp]
                filler(nf)
                for st in sts:
                    rg_stage2(st)

        convb_half(0)
        filler(1)
        rg_half(0, (2, 1, 1))
        convb_half(1)
        filler(1)
        rg_half(1, (1, 1, 0))
        filler(8)
        if prompt and not last:
            DVE(lambda e: e.tensor_copy(out=ucar[:, :, :], in_=uA.t[:, :, NT:NT + 30]), flat(uA.res), [r_ucar])
        lna_stats()
        if STOP == 'S4':
            return
        for i in range(2):
            (wpb, rpb, _, _), (wgb, rgb, _, _) = w_take("pb", i, 2)
            for m in range(4):
                c = 4 * i + m
                by, rby = bank()
                for kc in range(10):
                    PE(mm(by[:, 0:N], wpb[:, kc, m * 128:(m + 1) * 128], hg2.t[:, kc, 0:N], kc == 0, kc == 9),
                       [rpb, hg2.res[kc]], [rby], signal=(kc == 9))
                bg, rbg = bank()
                for kc in range(8):
                    PE(mm(bg[:, 0:N], wgb[:, kc, m * 128:(m + 1) * 128], xT.t[:, kc, 0:N], kc == 0, kc == 7),
                       [rgb, xT.res[kc]], [rbg], signal=(kc == 7))
                t1, rt1 = tmp()
                ACT(act(t1[:, 0:N], bg[:, 0:N], AF.Tanh, scale=0.5), [rbg], [rt1])
                DVE(stt(m_b.t[:, c, 0:N], t1[:, 0:N], 1.0, by[:, 0:N], ALU.add, ALU.mult), [rt1, rby], [m_b.res[c]])
                lna_norm(c)
        for i in range(2):
            (wpa, rpa, _, _), (wga, rga, _, _) = w_take("pa", i, 2)
            for m in range(4):
                c = 4 * i + m
                by, rby = bank()
                for kc in range(8):
                    PE(mm(by[:, 0:N], wpa[:, kc, m * 128:(m + 1) * 128], ca2.t[:, kc, 0:N], kc == 0, kc == 7),
                       [rpa, ca2.res[kc]], [rby], signal=(kc == 7))
                bg, rbg = bank()
                for kc in range(8):
                    PE(mm(bg[:, 0:N], wga[:, kc, m * 128:(m + 1) * 128], xT.t[:, kc, 0:N], kc == 0, kc == 7),
                       [rga, xT.res[kc]], [rbg], signal=(kc == 7))
                t1, rt1 = tmp()
                ACT(act(t1[:, 0:N], bg[:, 0:N], AF.Tanh, scale=0.5), [rbg], [rt1])
                DVE(stt(t1[:, 0:N], t1[:, 0:N], 1.0, by[:, 0:N], ALU.add, ALU.mult), [rt1, rby], [rt1])
                DVE(tt(mixin.t[:, c, 0:N], t1[:, 0:N], m_b.t[:, c, 0:N], ALU.add), [rt1, m_b.res[c]], [mixin.res[c]])
        if DEBUG and first:
            dbg_dump("u2", uA, 8, 30, "bf16")
            dbg_dump("convo", convo, 8, 0, "f32")
            dbg_dump("ca2", ca2, 8, 0, "bf16")
            dbg_dump("hg2", hg2, 10, 0, "bf16")
            dbg_dump("mixin", mixin, 8, 0, "bf16")

        if STOP == 'S6':
            return
        def fm_to_rows(srcbuf, nchunks, ncols, dst_rows_fn):
            for c in range(nchunks):
                bk, rb = bank()
                PE(tr(bk[0:ncols, 0:128], srcbuf.t[:, c, 0:ncols], ident_f[:, :]), [srcbuf.res[c], r_id], [rb], signal=True)
                ACT(act(stg.t[0:ncols, 0, c * 128:(c + 1) * 128], bk[0:ncols, 0:128], AF.Copy), [rb], [stg.res[0]])
            dst_rows_fn()

        if last:
            fm_to_rows(ufp, 8, 30, lambda: out_toks.append(
                S.dma("sp", lambda e: e.dma_start(out=ncap, in_=stg.t[0:30, 0, 0:D]), "ost", flat(stg.res), [])))
            fm_to_rows(bxf, 10, 3, lambda: out_toks.append(
                S.dma("sp", lambda e: e.dma_start(out=ncbp, in_=stg.t[0:3, 0, 0:DR]), "ost", flat(stg.res), [])))
            bk, rb = bank()
            PE(tr(bk[0:10, 0:128], hcar[:, 0:10], ident_f[:, :]), r_hcar + [r_id], [rb], signal=True)
            ACT(act(stg.t[0:10, 0, 0:128], bk[0:10, 0:128], AF.Copy), [rb], [stg.res[0]])
            out_toks.append(S.dma("sp", lambda e: e.dma_start(out=nhp, in_=stg.t[0:10, 0, 0:128]), "ost", flat(stg.res), []))
        if not prompt:
            fm_to_rows(ufp, 8, 64, lambda: out_toks.append(
                S.dma("sp", lambda e: e.dma_start(out=ncas_new, in_=stg.t[0:64, 0, 0:D]), "ost", flat(stg.res), [])))
            fm_to_rows(bxf, 10, 64, lambda: out_toks.append(
                S.dma("sp", lambda e: e.dma_start(out=ncbs, in_=stg.t[16:64, 0, 0:DR]), "ost", flat(stg.res), [])))
            fm_to_rows(hl, 10, 16, lambda: out_toks.append(
                S.dma("sp", lambda e: e.dma_start(out=nhs, in_=stg.t[0:16, 0, 0:DR]), "ost", flat(stg.res), [])))
            out_toks.append(S.dma("sp", lambda e: e.dma_start(out=ncas_old, in_=sca[64:480, :]), "ost2", [], []))

        if STOP == 'state':
            return
        arena.reset(offAB)
        hT = arena.alloc(32, N, BF16)
        x1T = arena.alloc(8, N, BF16)
        x1 = arena.alloc(4, D, F32)
        xfp = arena.alloc(2, D, F32)
        tg = arena.alloc(2, D, F32)
        yb = arena.alloc(2, D, F32)

        (wo0, rwo0, _, _), (wo1, rwo1, _, _) = w_take("wo", 0, 2)
        wos = ((wo0, rwo0), (wo1, rwo1))
        def wo_mm(s):
            S.dma("sp", (lambda e, s=s: e.dma_start(out=xfp.t[0:P, s % 2, :], in_=x_src[s * 128:s * 128 + P, :])),
                  "xfp%d" % (s % 2), [], [xfp.res[s % 2]])
            pr, (rb0, rb1) = bankpair()
            rbs = (rb0, rb1)
            for hf in range(2):
                for kc in range(8):
                    PE(mm(pr[0:P, hf * NT:(hf + 1) * NT], mixin.t[:, kc, s * 128:s * 128 + P], wos[hf][0][:, kc, :],
                          kc == 0, kc == 7), [wos[hf][1], mixin.res[kc]], [rbs[hf]], signal=(kc == 7))
            b = s % 2
            DVE(stt(x1.t[0:P, s, :], xfp.t[0:P, b, :], 4.0 * ALPHA, pr[0:P, :], ALU.mult, ALU.add),
                [xfp.res[b], rb0, rb1], [x1.res[s]])
            bi_ = bn_ptr[0] % 4
            bn_ptr[0] += 1
            for hf in range(2):
                DVE((lambda e, s=s, hf=hf, bi_=bi_: e.bn_stats(out=bnst[0:P, bi_, 6 * hf:6 * hf + 6],
                                                               in_=x1.t[0:P, s, hf * NT:(hf + 1) * NT])),
                    [x1.res[s]], [r_bnst[bi_]])
            mv, rmv = stat4()
            DVE(lambda e, mv=mv, bi_=bi_: e.bn_aggr(out=mv[0:P, 0:2], in_=bnst[0:P, bi_, :]), [r_bnst[bi_]], [rmv])
            DVE(ts1(mv[0:P, 2:3], mv[0:P, 1:2], 16.0 * EPS, ALU.add), [rmv], [rmv])
            POOL(tt(mv[0:P, 2:3], mv[0:P, 2:3], cnh[0:P, 0:1], ALU.pow), [rmv, r_cnh], [rmv])
            DVE(ts(x1.t[0:P, s, :], x1.t[0:P, s, :], mv[0:P, 0:1], mv[0:P, 2:3], ALU.subtract, ALU.mult),
                [x1.res[s], rmv], [x1.res[s]])

        def ln1_tr(s):
            for fc in range(8):
                bk, rb = bank()
                PE(tr(bk[:, 0:P], x1.t[0:P, s, fc * 128:(fc + 1) * 128], ident_f[0:P, 0:P]), [x1.res[s], r_id], [rb],
                   signal=True)
                ACT(act(x1T.t[:, fc, s * 128:s * 128 + P], bk[:, 0:P], AF.Identity,
                        scale=colv_t[:, C_L1G + fc:C_L1G + fc + 1], bias=colv_t[:, C_L1B + fc:C_L1B + fc + 1]),
                    [rb, r_colv], [x1T.res[fc]])
            EW = POOL if POOL_OFF else DVE
            EW(tt(x1.t[0:P, s, :], x1.t[0:P, s, :], bc_t[0:P, 0:D], ALU.mult), [x1.res[s], r_bc], [x1.res[s]])
            EW(tt(x1.t[0:P, s, :], x1.t[0:P, s, :], bc_t[0:P, D:2 * D], ALU.add), [x1.res[s], r_bc], [x1.res[s]])

        wo_mm(0)
        for s in range(1, NS):
            wo_mm(s)
            ln1_tr(s - 1)
        ln1_tr(NS - 1)
        if DEBUG and first:
            dbg_dump("x1T", x1T, 8, 0, "bf16")
            S.dma("sp", lambda e: e.dma_start(out=dbg["x1"], in_=x1.t[:, :, :]), "odbgx", flat(x1.res), [])

        if STOP == 'S7':
            return
        for i in range(8):
            ((wf, rwf, _, _),) = w_take("ff1", i, 1)
            for m in range(4):
                c = 4 * i + m
                bk, rb = bank()
                for kc in range(8):
                    PE(mm(bk[:, 0:N], wf[:, kc, m * 128:(m + 1) * 128], x1T.t[:, kc, 0:N], kc == 0, kc == 7),
                       [rwf, x1T.res[kc]], [rb], signal=(kc == 7))
                t1, rt1 = tmp()
                ACT(act(t1[:, 0:N], bk[:, 0:N], AF.Relu), [rb], [rt1])
                if c % 2 == 0:
                    ACT(act(hT.t[:, c, 0:N], t1[:, 0:N], AF.Square), [rt1], [hT.res[c]])
                else:
                    DVE(tt(hT.t[:, c, 0:N], t1[:, 0:N], t1[:, 0:N], ALU.mult), [rt1], [hT.res[c]])
        if DEBUG and first:
            dbg_dump("hT", hT, 32, 0, "bf16")

        if STOP == 'S8':
            return
        (wg0, rwg0, _, _), (wg1, rwg1, _, _), (wpj, rwpj, _, _) = w_take("pg", 0, 3)
        wgs = ((wg0, rwg0), (wg1, rwg1))
        ple = []
        for s in range(NS):
            b = s % 2
            pr, (rb0, rb1) = bankpair()
            rbs = (rb0, rb1)
            for hf in range(2):
                for kc in range(8):
                    PE(mm(pr[0:P, hf * NT:(hf + 1) * NT], x1T.t[:, kc, s * 128:s * 128 + P], wgs[hf][0][:, kc, :],
                          kc == 0, kc == 7), [wgs[hf][1], x1T.res[kc]], [rbs[hf]], signal=(kc == 7))
            ACT(act(tg.t[0:P, b, :], pr[0:P, :], AF.Tanh, scale=0.5), [rb0, rb1], [tg.res[b]])
            pr2, (rc0, rc1) = bankpair()
            rcs = (rc0, rc1)
            for hf in range(2):
                for kc in range(2):
                    PE(mm(pr2[0:P, hf * NT:(hf + 1) * NT], pT.t[:, kc, s * 128:s * 128 + P],
                          wpj[:, kc, hf * NT:(hf + 1) * NT], kc == 0, kc == 1), [rwpj, pT.res[kc]], [rcs[hf]],
                       signal=(kc == 1))
            DVE(stt(tg.t[0:P, b, :], tg.t[0:P, b, :], 1.0, pr2[0:P, :], ALU.add, ALU.mult), [tg.res[b], rc0, rc1],
                [tg.res[b]])
            DVE(ts1(x1.t[0:P, s, :], x1.t[0:P, s, :], ALPHA, ALU.mult), [x1.res[s]], [x1.res[s]])
            DVE(stt(x1.t[0:P, s, :], tg.t[0:P, b, :], 0.5, x1.t[0:P, s, :], ALU.mult, ALU.add), [tg.res[b], x1.res[s]],
                [x1.res[s]])
        if STOP == 'S9':
            return
        for hf in range(2):
            prs = []
            for s in range(NS):
                bk, rb = bank()
                prs.append((bk, rb))
            for kg in range(4):
                ((wf2, rwf2, _, _),) = w_take("ff2", hf * 4 + kg, 1)
                for s in range(NS):
                    bk, rb = prs[s]
                    for kc in range(8):
                        kglob = 8 * kg + kc
                        PE(mm(bk[0:P, :], hT.t[:, kglob, s * 128:s * 128 + P], wf2[:, kc, :], kglob == 0, kglob == 31),
                           [rwf2, hT.res[kglob]], [rb], signal=(kc == 7))
                if hf == 1 and kg == 1 and nxt_tile is not None and PREFETCH_X:
                    prefetched[nxt_tile] = early_loads(*nxt_tile)
            for s in range(NS):
                bk, rb = prs[s]
                DVE(tt(x1.t[0:P, s, hf * NT:(hf + 1) * NT], x1.t[0:P, s, hf * NT:(hf + 1) * NT], bk[0:P, :], ALU.add),
                    [x1.res[s], rb], [x1.res[s]])
        yield "ff2"
        mv4 = ln2s
        for s in range(NS):
            bi_ = bn_ptr[0] % 4
            bn_ptr[0] += 1
            for hf in range(2):
                DVE((lambda e, s=s, hf=hf, bi_=bi_: e.bn_stats(out=bnst[0:P, bi_, 6 * hf:6 * hf + 6],
                                                               in_=x1.t[0:P, s, hf * NT:(hf + 1) * NT])),
                    [x1.res[s]], [r_bnst[bi_]])
            DVE(lambda e, s=s, bi_=bi_: e.bn_aggr(out=mv4[0:P, s, 0:2], in_=bnst[0:P, bi_, :]), [r_bnst[bi_]], [r_ln2s])
        ACT(act(mv4[0:P, 0:NS, 2], mv4[0:P, 0:NS, 1], AF.Sqrt, bias=eps_t[0:P, 0:1]), [r_ln2s, r_eps], [r_ln2s])
        DVE(lambda e: e.reciprocal(out=mv4[0:P, 0:NS, 2], in_=mv4[0:P, 0:NS, 2]), [r_ln2s], [r_ln2s])
        for s in range(NS):
            b = s % 2
            DVE(ts(yb.t[0:P, b, :], x1.t[0:P, s, :], mv4[0:P, s, 0:1], mv4[0:P, s, 2:3], ALU.subtract, ALU.mult),
                [x1.res[s], r_ln2s], [yb.res[b]])
            DVE(tt(yb.t[0:P, b, :], yb.t[0:P, b, :], bc_t[0:P, 2 * D:3 * D], ALU.mult), [yb.res[b], r_bc], [yb.res[b]])
            DVE(tt(yb.t[0:P, b, :], yb.t[0:P, b, :], bc_t[0:P, 3 * D:4 * D], ALU.add), [yb.res[b], r_bc], [yb.res[b]])
            out_toks.append(S.dma("sp", (lambda e, s=s, b=b: e.dma_start(out=y_dst[s * 128:s * 128 + P, :],
                                                                         in_=yb.t[0:P, b, :])),
                                  "oy%d" % b, [yb.res[b]], []))

    dbg_tmp = {}

    def dbg_dump(name, buf, C, off, kind):
        for c in range(C):
            dbg_dump1(name, c, buf.t[:, c, off:off + NT], buf.res[c])

    def dbg_dump1(name, c, ap, res):
        ti = tmp_ptr[0] % NTMP
        t1, rt1 = tmp()
        DVE(lambda e: e.tensor_copy(out=t1[:, 0:NT], in_=ap), [res], [rt1])
        S.dma("sp", lambda e: e.dma_start(out=dbg[name][:, c, :], in_=t1[:, 0:NT]), "odbg%d" % ti, [rt1], [])

    if STOP != "setup":
        tl = (TILES if TILES is not None else [("p", 0), ("p", 1), ("p", 2), ("p", 3), ("s", 0)])
        gens = [emit_tile(kind, j, tl[q + 1] if q + 1 < len(tl) else None) for q, (kind, j) in enumerate(tl)]

        def run_to(g, tag):
            for t_ in g:
                if t_ == tag:
                    return True
            return False

        run_to(gens[0], "head")
        for q in range(len(gens)):
            alive = run_to(gens[q], "ff2")
            if q + 1 < len(gens) and HEAD_OVERLAP:
                run_to(gens[q + 1], "head")
                nk, nj = tl[q + 1]
                dpre[(nk, nj, id(r_dB[0]))] = d_load(dB[0], 20 * 128, r_dB[0])
                dpre[(nk, nj, id(r_dA[0]))] = d_load(dA[0], KA * 128, r_dA[0])
            if alive:
                run_to(gens[q], None)
            if q + 1 < len(gens) and not HEAD_OVERLAP:
                run_to(gens[q + 1], "head")

    final = list(out_toks)
    for nm, (sem, cnt) in S.dsem.items():
        if nm.startswith("o"):
            final.append(("dma", nm, cnt, sem))
    S.wait_all("sp", final)

    with nc.Block() as block:
        @block.tensor
        def _(e):
            S.replay("pe", e)

        @block.scalar
        def _(e):
            S.replay("act", e)

        @block.vector
        def _(e):
            S.replay("dve", e)

        @block.gpsimd
        def _(e):
            S.replay("pool", e)

        @block.sync
        def _(e):
            S.replay("sp", e)
    es.close()
    return nc


def _host_prep(inp):
    f = lambda a: np.ascontiguousarray(np.asarray(a, dtype=np.float32))
    colv = np.zeros((128, NCOL), np.float32)

    def put(col0, vec):
        v = np.asarray(vec, np.float32).reshape(-1, 128)
        colv[:, col0:col0 + v.shape[0]] = v.T

    put(C_BDA, inp["b_dw_a"][0]); put(C_LAG, inp["ln_a_g"][0]); put(C_LAB, inp["ln_a_b"][0])
    put(C_BDB, inp["b_dw_b"][0]); put(C_BRA, inp["b_rg_a"][0].reshape(-1)); put(C_BRX, inp["b_rg_x"][0].reshape(-1))
    put(C_LAM, inp["rg_lam"][0]); put(C_L1G, inp["ln1_g"][0]); put(C_L1B, inp["ln1_b"][0])
    wda = np.asarray(inp["w_dw_a"][0], np.float32)
    for c in range(8):
        colv[:, C_WDA + c * KA:C_WDA + (c + 1) * KA] = wda[:, c * 128:(c + 1) * 128].T
    wdb = np.asarray(inp["w_dw_b"][0], np.float32)
    for c in range(10):
        colv[:, C_WDB + c * KB:C_WDB + (c + 1) * KB] = wdb[:, c * 128:(c + 1) * 128].T
    bcv = np.zeros((128, 4 * D), np.float32)
    for i, k in enumerate(("ln1_g", "ln1_b", "ln2_g", "ln2_b")):
        bcv[:, i * D:(i + 1) * D] = np.asarray(inp[k][0], np.float32)[None, :]
    rgw = np.zeros((2, 128, 2 * NRGB, 128), np.float32)
    for g, key in enumerate(("w_rg_a", "w_rg_x")):
        wfull = np.zeros((DR, DR), np.float32)
        w = np.asarray(inp[key][0], np.float32)
        for hd in range(16):
            wfull[80 * hd:80 * hd + 80, 80 * hd:80 * hd + 80] = w[hd]
        for h in range(2):
            blk = g * NRGB
            for m in range(5):
                for kk in RG_BLOCKS[m]:
                    r0 = 640 * h + 128 * kk
                    c0 = 640 * h + 128 * m
                    rgw[h, :, blk, :] = wfull[r0:r0 + 128, c0:c0 + 128]
                    blk += 1
    rgw = rgw.reshape(2, 128, 2 * NRGB * 128)
    common = {
        "w_in": f(inp["w_in"][0]), "w_pa": f(inp["w_proj_a"][0]), "w_pb": f(inp["w_proj_b"][0]),
        "w_out": f(inp["w_out"][0]), "w_ff1": f(inp["w_ff1"][0]), "w_ff2": f(inp["w_ff2"][0]),
        "w_pg": f(inp["w_ple_gate"][0]), "w_pp": f(inp["w_ple_proj"][0]),
        "rgw": np.ascontiguousarray(rgw), "colv": colv, "bcv": bcv,
    }
    maps = []
    for i in range(NCORES):
        m = dict(common)
        m["xp"] = f(inp["x_prompt"][i])
        m["xs"] = f(np.asarray(inp["x_sample"])[16 * i:16 * i + 16].transpose(1, 0, 2).reshape(64, D))
        m["ppr"] = f(inp["p_prompt"][0][i])
        m["psm"] = f(np.asarray(inp["p_sample"][0])[16 * i:16 * i + 16].transpose(1, 0, 2).reshape(64, DPL))
        m["sca"] = f(np.asarray(inp["state_conv_a"][0])[16 * i:16 * i + 16].transpose(1, 0, 2).reshape(480, D))
        m["scb"] = f(np.asarray(inp["state_conv_b"][0])[16 * i:16 * i + 16].transpose(1, 0, 2).reshape(48, DR))
        m["sh"] = f(np.asarray(inp["state_h"][0])[16 * i:16 * i + 16])
        maps.append(m)
    return maps


_NC_CACHE = {}


def kernel(**inputs):
    maps = _host_prep(inputs)
    if "nc" not in _NC_CACHE:
        _NC_CACHE["nc"] = build_nc()
    nc = _NC_CACHE["nc"]
    res = run_bass_kernel_spmd(nc, maps, core_ids=list(range(NCORES)))
    R = res.results
    y_p = np.stack([R[i]["yp"] for i in range(NCORES)], 0).astype(np.float32)
    y_s = np.concatenate([R[i]["ys"].reshape(4, 16, D).transpose(1, 0, 2) for i in range(NCORES)], 0).astype(np.float32)
    ca_p = np.stack([R[i]["ncap"] for i in range(NCORES)], 0)[None].astype(np.float32)
    cb_p = np.stack([R[i]["ncbp"] for i in range(NCORES)], 0)[None].astype(np.float32)
    h_p = np.stack([R[i]["nhp"].reshape(DR) for i in range(NCORES)], 0)[None].astype(np.float32)
    ca_s = np.concatenate([np.concatenate([R[i]["ncas_old"].reshape(26, 16, D), R[i]["ncas_new"].reshape(4, 16, D)], 0)
                           .transpose(1, 0, 2) for i in range(NCORES)], 0)[None].astype(np.float32)
    cb_s = np.concatenate([R[i]["ncbs"].reshape(3, 16, DR).transpose(1, 0, 2) for i in range(NCORES)], 0)[None].astype(np.float32)
    h_s = np.concatenate([R[i]["nhs"] for i in range(NCORES)], 0)[None].astype(np.float32)
    if DEBUG:
        kernel.debug = R
    return (y_p, y_s, ca_p, cb_p, h_p, ca_s, cb_s, h_s)
```

```python
import numpy as np
from contextlib import ExitStack
import concourse.bass as bass
import concourse.mybir as mybir
from concourse.bass_utils import run_bass_kernel_spmd

F32 = mybir.dt.float32
BF16 = mybir.dt.bfloat16
AF = mybir.ActivationFunctionType
ALU = mybir.AluOpType

NCORES = 8
D = 1024
DR = 1280
DFF = 4096
DPL = 256
DIN = 6656
SEQ = 2048
NT = 512
KA = 31
KB = 4
ALPHA = 2.0 ** 0.25
EPS = 1e-5
GK = 0.7978845608028654
C_BDA, C_LAG, C_LAB, C_BDB, C_BRA, C_BRX, C_LAM, C_L1G, C_L1B, C_WDA, C_WDB = 0, 8, 16, 24, 34, 44, 54, 64, 72, 80, 328
NCOL = 368
V_HBRA, V_HBRX, V_C, V_HC, V_HLAG, V_HLAB, V_E, V_SP = 0, 10, 20, 30, 40, 48, 56, 66
NDV = 80
RG_BLOCKS = {0: (0, 1), 1: (0, 1, 2), 2: (1, 2, 3), 3: (2, 3, 4), 4: (3, 4)}
NRGB = 13
DEBUG = False
BIS = set()
PREFETCH_X = True
HEAD_OVERLAP = True
W_SCRATCH = True
POOL_OFF = True
STOP = None
TILES = None


class Res:
    __slots__ = ("name", "w", "r", "excl")

    def __init__(self, name, excl=False):
        self.name = name
        self.w = None
        self.r = {}
        self.excl = excl


class Sched:
    ENGS = ("pe", "act", "dve", "pool", "sp")

    def __init__(self, nc, es):
        self.nc = nc
        self.es = es
        self.prog = {e: [] for e in self.ENGS}
        self.cnt = {e: 0 for e in self.ENGS}
        self.sem = {e: es.enter_context(nc.semaphore("s_" + e)) for e in ("pe", "act", "dve", "pool")}
        self.seen = {e: {} for e in self.ENGS}
        self.dsem = {}

    def dma_sem(self, name):
        if name not in self.dsem:
            self.dsem[name] = [self.es.enter_context(self.nc.semaphore("d_" + name)), 0]
        return self.dsem[name]

    def _need(self, eng, tok, waits, skip_same):
        if tok is None:
            return
        kind, key, val, sem = tok
        if kind == "eng" and key == eng and skip_same:
            return
        k = (kind, key)
        if self.seen[eng].get(k, 0) >= val:
            return
        if k not in waits or waits[k][1] < val:
            waits[k] = (sem, val)

    def _deps(self, eng, reads, writes, is_dma):
        waits = {}
        for r in reads:
            self._need(eng, r.w, waits, (eng == "pe") and not is_dma)
            if r.excl:
                for t in r.r.values():
                    self._need(eng, t, waits, True)
        for w in writes:
            self._need(eng, w.w, waits, (eng == "pe") and not is_dma)
            for t in w.r.values():
                self._need(eng, t, waits, (eng == "pe") and not is_dma)
        for k, (sem, val) in waits.items():
            self.seen[eng][k] = val
        return list(waits.values())

    def _commit(self, tok, reads, writes):
        for r in reads:
            k = (tok[0], tok[1])
            if k not in r.r or r.r[k][2] < tok[2]:
                r.r[k] = tok
        for w in writes:
            w.w = tok
            w.r = {}

    def op(self, eng, fn, reads=(), writes=(), signal=True):
        reads, writes = flat(reads), flat(writes)
        waits = self._deps(eng, reads, writes, False)
        if signal:
            self.cnt[eng] += 1
            tok = ("eng", eng, self.cnt[eng], self.sem[eng])
        else:
            tok = ("eng", eng, self.cnt[eng] + 1, self.sem[eng])
        self.prog[eng].append((waits, fn, self.sem[eng] if signal else None, 1))
        self._commit(tok, reads, writes)
        return tok

    def dma(self, q, fn, semname, reads=(), writes=()):
        reads, writes = flat(reads), flat(writes)
        waits = self._deps(q, reads, writes, True)
        ds = self.dma_sem(semname)
        ds[1] += 16
        tok = ("dma", semname, ds[1], ds[0])
        self.prog[q].append((waits, fn, ds[0], 16))
        self._commit(tok, reads, writes)
        return tok

    def wait_all(self, eng, toks):
        waits = {}
        for t in toks:
            self._need(eng, t, waits, False)
        for k, (sem, val) in waits.items():
            self.seen[eng][k] = val
        self.prog[eng].append((list(waits.values()), None, None, 0))

    def replay(self, eng, e):
        for waits, fn, sem, inc in self.prog[eng]:
            for s, v in waits:
                e.wait_ge(s, v)
            if fn is None:
                continue
            ins = fn(e)
            if sem is not None:
                ins.then_inc(sem, inc)


class Buf:
    def __init__(self, ap3, res_list):
        self.t = ap3
        self.res = res_list

    def r(self, c):
        return self.res[c]


BLK = 2048


class Arena:
    def __init__(self, nc, es, name, nbytes):
        self.nbytes = nbytes
        self.t = es.enter_context(nc.sbuf_tensor(name, [128, nbytes // 4], F32))
        self.blocks = [Res("%s_b%d" % (name, i)) for i in range((nbytes + BLK - 1) // BLK)]
        self.off = 0

    def reset(self, off=0):
        self.off = off

    def alloc(self, C, n, dtype):
        esz = 2 if dtype == BF16 else 4
        nb = C * n * esz
        nb_al = (nb + 63) // 64 * 64
        lo = self.off
        assert lo + nb_al <= self.nbytes, (lo, nb_al, self.nbytes)
        self.off += nb_al
        ap = self.t[:, lo // 4:(lo + nb) // 4]
        if dtype == BF16:
            ap = ap.bitcast(BF16)
        ap = ap.rearrange("p (c n) -> p c n", c=C)
        res = []
        for c in range(C):
            a = lo + c * n * esz
            b = a + n * esz
            res.append(MultiRes([self.blocks[i] for i in range(a // BLK, (b - 1) // BLK + 1)]))
        return Buf(ap, res)


class MultiRes:
    def __init__(self, blocks):
        self.blocks = blocks


def flat(rs):
    out = []
    for r in rs:
        if isinstance(r, MultiRes):
            out.extend(r.blocks)
        elif isinstance(r, (list, tuple)):
            out.extend(flat(r))
        else:
            out.append(r)
    seen = set()
    o2 = []
    for r in out:
        if id(r) not in seen:
            seen.add(id(r))
            o2.append(r)
    return o2


def build_nc():
    nc = bass.Bass("TRN2", target_bir_lowering=False)

    def din(name, shape):
        return nc.dram_tensor(name, list(shape), F32, kind="ExternalInput").ap()

    def dout(name, shape):
        return nc.dram_tensor(name, list(shape), F32, kind="ExternalOutput").ap()

    xp = din("xp", [SEQ, D]); xs = din("xs", [64, D])
    ppr = din("ppr", [SEQ, DPL]); psm = din("psm", [64, DPL])
    sca = din("sca", [480, D]); scb = din("scb", [48, DR]); sh = din("sh", [16, DR])
    w_in = din("w_in", [D, DIN]); w_pa = din("w_pa", [D, D]); w_pb = din("w_pb", [DR, D])
    w_out = din("w_out", [D, D]); w_ff1 = din("w_ff1", [D, DFF]); w_ff2 = din("w_ff2", [DFF, D])
    w_pg = din("w_pg", [D, D]); w_pp = din("w_pp", [DPL, D])
    rgw = din("rgw", [2, 128, 2 * NRGB * 128])
    colv = din("colv", [128, NCOL]); bcv = din("bcv", [128, 4 * D])
    yp = dout("yp", [SEQ, D]); ys = dout("ys", [64, D])
    ncap = dout("ncap", [30, D]); ncbp = dout("ncbp", [3, DR]); nhp = dout("nhp", [10, 128])
    ncas_new = dout("ncas_new", [64, D]); ncas_old = dout("ncas_old", [416, D])
    ncbs = dout("ncbs", [48, DR]); nhs = dout("nhs", [16, DR])
    dA = nc.dram_tensor("dA", [8, 128, KA * 128], BF16, kind="Internal").ap()
    dB = nc.dram_tensor("dB", [2, 128, 5 * KB * 128], BF16, kind="Internal").ap()
    dbg = {}
    if DEBUG:
        for nm, C in (("u2", 8), ("convo", 8), ("ca2", 8), ("cb", 10), ("hh", 10), ("aa", 10), ("bt", 10), ("hg2", 10), ("mixin", 8),
                      ("x1T", 8), ("hT", 32)):
            dbg[nm] = nc.dram_tensor("dbg_" + nm, [128, C, NT], F32, kind="ExternalOutput").ap()
        dbg["x1"] = nc.dram_tensor("dbg_x1", [128, 4, D], F32, kind="ExternalOutput").ap()

    es = ExitStack()
    S = Sched(nc, es)

    def sb(name, shape, dt):
        return es.enter_context(nc.sbuf_tensor(name, list(shape), dt))

    colv_t = sb("colv_t", [128, NCOL], F32); r_colv = Res("colv")
    dv = sb("dv", [128, NDV], F32); r_dv = Res("dv")
    wdah = sb("wdah", [128, 8 * KA], F32); r_wdah = Res("wdah")
    bc_t = sb("bc_t", [128, 4 * D], F32); r_bc = Res("bc")
    ident_f = sb("ident_f", [128, 128], F32); ident_b = sb("ident_b", [128, 128], BF16); r_id = Res("ident")
    ones_f = sb("ones_f", [128, 128], F32); r_ones = Res("ones")
    eps_t = sb("eps_t", [128, 1], F32); quarter_t = sb("quarter_t", [128, 1], F32); r_eps = Res("eps")
    cnh = sb("cnh", [128, 8], F32); r_cnh = Res("cnh")
    hcar = sb("hcar", [128, 10], F32); r_hcar = [Res("hcar%d" % i) for i in range(10)]
    ucar = sb("ucar", [128, 8, 30], BF16); r_ucar = Res("ucar")
    bcar = sb("bcar", [128, 10, 4], BF16); r_bcar = Res("bcar")
    NTMP = 12
    tmps = [sb("tmp%d" % i, [128, NT], F32) for i in range(NTMP)]
    r_tmps = [Res("tmp%d" % i) for i in range(NTMP)]
    tmp_ptr = [0]

    def tmp():
        i = tmp_ptr[0] % NTMP
        tmp_ptr[0] += 1
        return tmps[i], r_tmps[i]

    lnm = sb("lnm", [128, NT], F32); r_lnm = Res("lnm")
    lnr = sb("lnr", [128, NT], F32); r_lnr = Res("lnr")
    stat = sb("stat", [128, 64], F32)
    r_stat = [Res("stat%d" % i) for i in range(16)]
    stat_ptr = [0]

    def stat4():
        i = stat_ptr[0] % 16
        stat_ptr[0] += 1
        return stat[:, 4 * i:4 * i + 4], r_stat[i]

    ln2s = sb("ln2s", [128, 4, 4], F32); r_ln2s = Res("ln2s")
    bnst = sb("bnst", [128, 4, 12], F32)
    r_bnst = [Res("bnst%d" % i) for i in range(4)]
    bn_ptr = [0]

    WSLOT = 5120
    NW = 4
    wslots = [sb("wslot%d" % i, [128, WSLOT], BF16) for i in range(NW)]
    r_wslots = [Res("wslot%d" % i) for i in range(NW)]
    DSLOT = KA * 128
    ND = 2
    dslots = [sb("dslot%d" % i, [128, DSLOT], BF16) for i in range(ND)]
    r_dslots = [Res("dslot%d" % i) for i in range(ND)]

    pairs = [es.enter_context(nc.psum_tensor("pp%d" % i, [128, 2 * NT], F32)) for i in range(4)]
    r_banks = [Res("bank%d" % i, excl=True) for i in range(8)]
    bank_ptr = [0]

    def bank():
        i = bank_ptr[0] % 8
        bank_ptr[0] += 1
        return pairs[i // 2][:, (i % 2) * NT:(i % 2 + 1) * NT], r_banks[i]

    def bankpair():
        if bank_ptr[0] % 2:
            bank_ptr[0] += 1
        i = bank_ptr[0] % 8
        bank_ptr[0] += 2
        return pairs[i // 2], (r_banks[i], r_banks[i + 1])

    ARENA = 100 * 1024
    arena = Arena(nc, es, "arena", ARENA)

    def ACT(fn, reads, writes):
        return S.op("act", fn, flat(reads), flat(writes))

    def DVE(fn, reads, writes):
        return S.op("dve", fn, flat(reads), flat(writes))

    def POOL(fn, reads, writes):
        return S.op("pool", fn, flat(reads), flat(writes))

    def PE(fn, reads, writes, signal):
        return S.op("pe", fn, flat(reads), flat(writes), signal=signal)

    def act(out, in_, func, scale=1.0, bias=0.0):
        return lambda e: e.activation(out=out, in_=in_, func=func, scale=scale, bias=bias)

    def tt(out, in0, in1, op):
        return lambda e: e.tensor_tensor(out=out, in0=in0, in1=in1, op=op)

    def ts(out, in0, s1, s2, op0, op1):
        return lambda e: e.tensor_scalar(out=out, in0=in0, scalar1=s1, scalar2=s2, op0=op0, op1=op1)

    def ts1(out, in0, s1, op0):
        return lambda e: e.tensor_single_scalar(out=out, in_=in0, scalar=s1, op=op0)

    def stt(out, in0, scalar, in1, op0, op1):
        return lambda e: e.scalar_tensor_tensor(out=out, in0=in0, scalar=scalar, in1=in1, op0=op0, op1=op1)

    def mm(out, lhsT, rhs, start, stop):
        return lambda e: e.matmul(out, lhsT=lhsT, rhs=rhs, start=start, stop=stop)

    def tr(out, in_, ident):
        return lambda e: e.transpose(out, in_, ident)

    S.dma("sp", lambda e: e.dma_start(out=colv_t[:], in_=colv), "const0", [], [r_colv])
    S.dma("sp", lambda e: e.dma_start(out=bc_t[:], in_=bcv), "const1", [], [r_bc])
    POOL(lambda e: e.memset(ident_f[:], 0.0), [], [r_id])
    POOL(lambda e: e.affine_select(out=ident_f[:], in_=ident_f[:], pattern=[[-1, 128]], compare_op=ALU.not_equal,
                                   fill=1.0, base=0, channel_multiplier=1), [r_id], [r_id])
    POOL(lambda e: e.tensor_copy(out=ident_b[:], in_=ident_f[:]), [r_id], [r_id])
    POOL(lambda e: e.memset(ones_f[:], 1.0 / D), [], [r_ones])
    POOL(lambda e: e.memset(eps_t[:], EPS), [], [r_eps])
    POOL(lambda e: e.memset(quarter_t[:], 0.25), [r_eps], [r_eps])
    POOL(lambda e: e.memset(cnh[:], -0.5), [], [r_cnh])
    POOL(lambda e: e.memset(hcar[:], 0.0), [], r_hcar)
    ACT(act(dv[:, V_E:V_E + 10], colv_t[:, C_LAM:C_LAM + 10], AF.Exp, scale=-1.0), [r_colv], [r_dv])
    ACT(act(dv[:, V_SP:V_SP + 10], dv[:, V_E:V_E + 10], AF.Ln, bias=1.0), [r_dv], [r_dv])
    DVE(ts1(dv[:, V_C:V_C + 10], dv[:, V_SP:V_SP + 10], -8.0, ALU.mult), [r_dv], [r_dv])
    DVE(ts1(dv[:, V_HC:V_HC + 10], dv[:, V_SP:V_SP + 10], -4.0, ALU.mult), [r_dv], [r_dv])
    DVE(ts1(dv[:, V_HBRA:V_HBRA + 10], colv_t[:, C_BRA:C_BRA + 10], 0.5, ALU.mult), [r_colv], [r_dv])
    DVE(ts1(dv[:, V_HBRX:V_HBRX + 10], colv_t[:, C_BRX:C_BRX + 10], 0.5, ALU.mult), [r_colv], [r_dv])
    DVE(ts1(dv[:, V_HLAG:V_HLAG + 8], colv_t[:, C_LAG:C_LAG + 8], 0.5, ALU.mult), [r_colv], [r_dv])
    DVE(ts1(dv[:, V_HLAB:V_HLAB + 8], colv_t[:, C_LAB:C_LAB + 8], 0.5, ALU.mult), [r_colv], [r_dv])
    DVE(ts1(wdah[:], colv_t[:, C_WDA:C_WDA + 8 * KA], 0.5, ALU.mult), [r_colv], [r_wdah])
    r_dA = [Res("dA%d" % c) for c in range(8)]
    r_dB = [Res("dB%d" % h) for h in range(2)]

    def wsrc(w, K, c0, nc_):
        return w.rearrange("(k p) n -> p k n", p=128)[:, 0:K, c0:c0 + nc_]

    def tile_plan():
        pl = []
        for i in range(2):
            pl.append(("in_av", i, wsrc(w_in, 8, 512 * i, 512), 8, 512))
            pl.append(("in_ag", i, wsrc(w_in, 8, 1024 + 512 * i, 512), 8, 512))
        for h in range(2):
            pl.append(("in_bx", h, wsrc(w_in, 8, 2048 + 640 * h, 640), 8, 640))
        for h in range(2):
            pl.append(("rg", h, rgw[h].rearrange("p (k n) -> p k n", k=2 * NRGB), 2 * NRGB, 128))
            pl.append(("in_bg", h, wsrc(w_in, 8, 3328 + 640 * h, 640), 8, 640))
        for i in range(2):
            pl.append(("pb", i, wsrc(w_pb, 10, 512 * i, 512), 10, 512))
            pl.append(("in_gb", i, wsrc(w_in, 8, 5632 + 512 * i, 512), 8, 512))
        for i in range(2):
            pl.append(("pa", i, wsrc(w_pa, 8, 512 * i, 512), 8, 512))
            pl.append(("in_ga", i, wsrc(w_in, 8, 4608 + 512 * i, 512), 8, 512))
        for i in range(2):
            pl.append(("wo", i, wsrc(w_out, 8, 512 * i, 512), 8, 512))
        for i in range(8):
            pl.append(("ff1", i, wsrc(w_ff1, 8, 512 * i, 512), 8, 512))
        for i in range(2):
            pl.append(("pg", i, wsrc(w_pg, 8, 512 * i, 512), 8, 512))
        pl.append(("ppj", 0, wsrc(w_pp, 2, 0, 1024), 2, 1024))
        for hf in range(2):
            for kg in range(4):
                src = w_ff2.rearrange("(k p) n -> p k n", p=128)[:, 8 * kg:8 * kg + 8, 512 * hf:512 * hf + 512]
                pl.append(("ff2", hf * 4 + kg, src, 8, 512))
        return pl

    NTILES = 5
    wplan = []
    for t in range(NTILES):
        wplan.extend(tile_plan())
    wstate = {"loaded": 0, "cur": 0}

    NPIECE = len(tile_plan())
    wscr = nc.dram_tensor("wscr", [NPIECE, 128, WSLOT], BF16, kind="Internal").ap()
    r_wscr = [Res("wscr%d" % i) for i in range(NPIECE)]

    def w_advance():
        lim = min(len(wplan), wstate["cur"] + NW)
        while wstate["loaded"] < lim:
            j = wstate["loaded"]
            nm, idx, src, K, ncol = wplan[j]
            sl, rs = wslots[j % NW], r_wslots[j % NW]
            jl = j % NPIECE
            if (not W_SCRATCH) or j < NPIECE or (j < 2 * NPIECE and jl % 2 == 1):
                dst = sl[:, 0:K * ncol].rearrange("p (k n) -> p k n", k=K)
                S.dma("pool", (lambda e, dst=dst, src=src: e.dma_start(out=dst, in_=src)), "w%d" % (j % NW), [], [rs])
            else:
                S.dma("pool", (lambda e, sl=sl, jl=jl, n_=K * ncol: e.dma_start(out=sl[:, 0:n_], in_=wscr[jl][:, 0:n_])),
                      "w%d" % (j % NW), [r_wscr[jl]], [rs])
            wstate["loaded"] += 1

    def w_take(name, idx, n=1):
        w_advance()
        out = []
        for q in range(n):
            j = wstate["cur"] + q
            nm, ix, src, K, ncol = wplan[j]
            assert j < wstate["loaded"], (j, wstate)
            sl, rs = wslots[j % NW], r_wslots[j % NW]
            out.append((sl[:, 0:K * ncol].rearrange("p (k n) -> p k n", k=K), rs, nm, ix))
            jl_ = j % NPIECE
            if W_SCRATCH and ((j < NPIECE and jl_ % 2 == 0) or (NPIECE <= j < 2 * NPIECE and jl_ % 2 == 1)):
                S.dma("sp", (lambda e, sl=sl, jl=jl_, n_=K * ncol: e.dma_start(out=wscr[jl][:, 0:n_], in_=sl[:, 0:n_])),
                      "wb%d" % (j % NW), [rs], [r_wscr[jl_]])
        assert out[0][2] == name and out[0][3] == idx, (out[0][2:], name, idx)
        wstate["cur"] += n
        return out

    dstate = {"n": 0}

    def d_load(src, ncols, rsrc):
        j = dstate["n"]
        dstate["n"] += 1
        sl, rs = dslots[j % ND], r_dslots[j % ND]
        S.dma("sp", (lambda e, sl=sl, src=src, ncols=ncols: e.dma_start(out=sl[:, 0:ncols], in_=src)),
              "d%d" % (j % ND), [rsrc], [rs])
        return sl, rs

    out_toks = []

    def early_loads(kind, j):
        prompt = (kind == "p")
        N = NT if prompt else 64
        x_src = xp[j * NT:j * NT + N, :] if prompt else xs
        p_src = ppr[j * NT:j * NT + N, :] if prompt else psm
        arena.reset(0)
        mixin = arena.alloc(8, N, BF16)
        pT = arena.alloc(2, N, BF16)
        offAB = arena.off
        xbf = arena.alloc(4, D, BF16)
        pbf = arena.alloc(4, DPL, BF16)
        if prompt:
            S.dma("pool", lambda e: e.dma_start(out=xbf.t[:, :, :], in_=x_src.rearrange("(s p) d -> p s d", p=128)),
                  "xbf", [], flat(xbf.res))
            S.dma("pool", lambda e: e.dma_start(out=pbf.t[:, :, :], in_=p_src.rearrange("(s p) d -> p s d", p=128)),
                  "pbf", [], flat(pbf.res))
        else:
            S.dma("pool", lambda e: e.dma_start(out=xbf.t[0:64, 0, :], in_=x_src), "xbf", [], flat(xbf.res))
            S.dma("pool", lambda e: e.dma_start(out=pbf.t[0:64, 0, :], in_=p_src), "pbf", [], flat(pbf.res))
        return dict(mixin=mixin, pT=pT, offAB=offAB, xbf=xbf, pbf=pbf, off=arena.off)

    prefetched = {}
    dpre = {}

    def emit_tile(kind, j, nxt_tile=None):
        prompt = (kind == "p")
        N = NT if prompt else 64
        NS = 4 if prompt else 1
        P = 128 if prompt else 64
        first = prompt and j == 0
        last = prompt and j == 3
        tok0 = j * NT
        x_src = xp[tok0:tok0 + N, :] if prompt else xs
        p_src = ppr[tok0:tok0 + N, :] if prompt else psm
        y_dst = yp[tok0:tok0 + N, :] if prompt else ys
        LA = 30 + NT if prompt else 16 * 34
        LB = 3 + NT if prompt else 16 * 7

        pre = prefetched.pop((kind, j), None)
        if pre is None:
            pre = early_loads(kind, j)
        mixin, pT, offAB, xbf, pbf = pre["mixin"], pre["pT"], pre["offAB"], pre["xbf"], pre["pbf"]
        arena.reset(pre["off"])
        xT = arena.alloc(8, N, BF16)
        _off_u = arena.off
        uA = arena.alloc(8, 544, BF16)
        bxb = arena.alloc(10, 516 if prompt else 112, BF16)
        _save = arena.off
        arena.reset(_off_u)
        m_b = arena.alloc(8, N, F32)
        assert arena.off <= _save
        arena.reset(_save)
        if last or not prompt:
            ufp = arena.alloc(8, 64, F32)
            h0b = arena.alloc(10, 16, F32)
        if not prompt:
            sca_t = arena.alloc(4, D, BF16)
            scb_t = arena.alloc(1, DR, BF16)
            sh_t = arena.alloc(1, DR, F32)
        assert arena.off <= 50 * 1024, arena.off
        convo = arena.alloc(8, N, F32)
        _save = arena.off
        arena.reset(offAB)
        ca2 = arena.alloc(8, N, BF16)
        arena.reset(_save)
        cb = arena.alloc(5, N, F32)
        cbb = arena.alloc(5, N, BF16)
        hg2 = arena.alloc(10, N, BF16)
        if last or not prompt:
            bxf = arena.alloc(10, 64, F32)
            hl = arena.alloc(10, 16, F32)
            stg = arena.alloc(1, DR, F32)

        def uview(buf, c, L, ctx, a, b):
            if prompt:
                return buf.t[:, c, a:b]
            return buf.t[:, c, 16 * a:16 * b]

        def nview(ap2):
            return ap2

        def transpose_in(src_buf, nfc, dst_buf):
            for fc in range(nfc):
                bk, rb = bank()
                bkb = bk[:, 0:NT // 2].bitcast(BF16)
                for s in range(NS):
                    PE(tr(bkb[:, s * 128:s * 128 + P], src_buf.t[0:P, s, fc * 128:(fc + 1) * 128], ident_b[0:P, 0:P]),
                       [src_buf.res[s], r_id], [rb], signal=(s == NS - 1))
                ACT(act(dst_buf.t[:, fc, 0:N], bkb[:, 0:N], AF.Copy), [rb], [dst_buf.res[fc]])

        transpose_in(xbf, 8, xT)
        transpose_in(pbf, 2, pT)

        if STOP == 'T':
            return
        if prompt and first:
            DVE(lambda e: e.memset(uA.t[:, :, 0:30], 0.0), [], flat(uA.res))
            DVE(lambda e: e.memset(bxb.t[:, :, 0:3], 0.0), [], flat(bxb.res))
        elif prompt:
            DVE(lambda e: e.tensor_copy(out=uA.t[:, :, 0:30], in_=ucar[:, :, :]), [r_ucar], flat(uA.res))
            DVE(lambda e: e.tensor_copy(out=bxb.t[:, :, 0:3], in_=bcar[:, :, 0:3]), [r_bcar], flat(bxb.res))
        else:
            for s_ in range(4):
                S.dma("pool", (lambda e, s_=s_: e.dma_start(out=sca_t.t[0:120, s_, :], in_=sca[120 * s_:120 * s_ + 120, :])),
                      "sca%d" % s_, [], [sca_t.res[s_]])
            for fc in range(8):
                for s_ in range(4):
                    bk, rb = bank()
                    bkb = bk[:, 0:NT // 2].bitcast(BF16)
                    PE(tr(bkb[:, 0:120], sca_t.t[0:120, s_, fc * 128:(fc + 1) * 128], ident_b[0:120, 0:120]),
                       [sca_t.res[s_], r_id], [rb], signal=True)
                    ACT(act(uA.t[:, fc, 120 * s_:120 * s_ + 120], bkb[:, 0:120], AF.Copy, scale=2.0), [rb], [uA.res[fc]])
            S.dma("pool", lambda e: e.dma_start(out=scb_t.t[0:48, 0, :], in_=scb), "scb", [], flat(scb_t.res))
            S.dma("sp", lambda e: e.dma_start(out=sh_t.t[0:16, 0, :], in_=sh), "sh", [], flat(sh_t.res))
            for fc in range(10):
                bk, rb = bank()
                bkb = bk[:, 0:NT // 2].bitcast(BF16)
                PE(tr(bkb[:, 0:48], scb_t.t[0:48, 0, fc * 128:(fc + 1) * 128], ident_b[0:48, 0:48]),
                   [scb_t.res[0], r_id], [rb], signal=True)
                ACT(act(bxb.t[:, fc, 0:48], bkb[:, 0:48], AF.Copy), [rb], [bxb.res[fc]])
                bk, rb = bank()
                PE(tr(bk[:, 0:16], sh_t.t[0:16, 0, fc * 128:(fc + 1) * 128], ident_f[0:16, 0:16]),
                   [sh_t.res[0], r_id], [rb], signal=True)
                ACT(act(h0b.t[:, fc, :], bk[:, 0:16], AF.Copy), [rb], [h0b.res[fc]])

        need_state = last or not prompt

        build_list = [("B", 0), ("A", 0), ("A", 1), ("A", 2), ("A", 3), ("A", 4), ("B", 1), ("A", 5), ("A", 6), ("A", 7)]

        def build_next():
            if not build_list:
                return
            typ, q = build_list.pop(0)
            if typ == "A":
                src, ncols, rsrc, in1 = dA[q], KA * 128, r_dA[q], wdah[:, q * KA:(q + 1) * KA]
            else:
                src, ncols, rsrc, in1 = dB[q], 20 * 128, r_dB[q], colv_t[:, C_WDB + 20 * q:C_WDB + 20 * q + 20]
            jd = dstate["n"]
            dstate["n"] += 1
            sl, rs = dslots[jd % ND], r_dslots[jd % ND]
            K = ncols // 128
            o3 = sl[:, 0:ncols].rearrange("p (k n) -> p k n", k=K)
            i0 = ident_f[:].unsqueeze(1).to_broadcast([128, K, 128])
            i1 = in1.unsqueeze(2).to_broadcast([128, K, 128])
            DVE(tt(o3, i0, i1, ALU.mult), [r_id, r_wdah, r_colv], [rs])
            S.dma("sp", (lambda e, sl=sl, src=src, ncols=ncols: e.dma_start(out=src, in_=sl[:, 0:ncols])),
                  "db%d" % (jd % ND), [rs], [rsrc])

        for i in range(2):
            (wav, rav, _, _), (wag, rag, _, _) = w_take("in_av", i, 2)
            for m in range(4):
                c = 4 * i + m
                bv, rbv = bank()
                for kc in range(8):
                    PE(mm(bv[:, 0:N], wav[:, kc, m * 128:(m + 1) * 128], xT.t[:, kc, 0:N], kc == 0, kc == 7),
                       [rav, xT.res[kc]], [rbv], signal=(kc == 7))
                bg, rbg = bank()
                for kc in range(8):
                    PE(mm(bg[:, 0:N], wag[:, kc, m * 128:(m + 1) * 128], xT.t[:, kc, 0:N], kc == 0, kc == 7),
                       [rag, xT.res[kc]], [rbg], signal=(kc == 7))
                t1, rt1 = tmp()
                ACT(act(t1[:, 0:N], bg[:, 0:N], AF.Tanh, scale=0.5), [rbg], [rt1])
                DVE(stt(uview(uA, c, 34, 30, 30, 30 + N) if prompt else uview(uA, c, 34, 30, 30, 34),
                        nview(t1[:, 0:N]), 1.0, nview(bv[:, 0:N]), ALU.add, ALU.mult), [rt1, rbv], [uA.res[c]])
                if need_state:
                    n0 = N - 30 if prompt else 0
                    nn = 30 if prompt else 64
                    DVE(stt(ufp.t[:, c, 0:nn], t1[:, n0:n0 + nn], 1.0, bv[:, n0:n0 + nn], ALU.add, ALU.mult),
                        [rt1, rbv], [ufp.res[c]])
                    DVE(ts1(ufp.t[:, c, 0:nn], ufp.t[:, c, 0:nn], 0.5, ALU.mult), [ufp.res[c]], [ufp.res[c]])
                if first:
                    build_next()
        if first:
            while build_list:
                build_next()

        yield "head"
        if STOP == 'S1':
            return

        bx_state = {}

        def s2_group(c):
            h, m = divmod(c, 5)
            if m == 0:
                ((bx_state["w"], bx_state["r"], _, _),) = w_take("in_bx", h, 1)
            wbx, rbx = bx_state["w"], bx_state["r"]
            bk, rb = bank()
            for kc in range(8):
                PE(mm(bk[:, 0:N], wbx[:, kc, m * 128:(m + 1) * 128], xT.t[:, kc, 0:N], kc == 0, kc == 7),
                   [rbx, xT.res[kc]], [rb], signal=(kc == 7))
            ACT(act(uview(bxb, c, 7, 3, 3, 3 + N) if prompt else uview(bxb, c, 7, 3, 3, 7),
                    bk[:, 0:N], AF.Copy), [rb], [bxb.res[c]])
            if need_state:
                n0 = N - 3 if prompt else 0
                nn = 3 if prompt else 64
                ACT(act(bxf.t[:, c, 0:nn], bk[:, n0:n0 + nn], AF.Copy), [rb], [bxf.res[c]])

        def get_diag(src, ncols, rsrc, build_in1):
            pk = dpre.pop((kind, j, id(rsrc)), None)
            if pk is not None:
                return pk
            return d_load(src, ncols, rsrc)

        def conva_chunk(c):
            sl, rs = get_diag(dA[c], KA * 128, r_dA[c], wdah[:, c * KA:(c + 1) * KA])
            dg = sl[:, 0:KA * 128].rearrange("p (k n) -> p k n", k=KA)
            bk, rb = bank()
            for k in range(KA):
                rhs = uview(uA, c, 34, 30, k, k + N) if prompt else uview(uA, c, 34, 30, k, k + 4)
                PE(mm(bk[:, 0:N], dg[:, k, :], rhs, k == 0, k == KA - 1), [rs, uA.res[c]], [rb],
                   signal=(k == KA - 1))
            ACT(act(convo.t[:, c, 0:N], bk[:, 0:N], AF.Identity, bias=colv_t[:, C_BDA + c:C_BDA + c + 1]),
                [rb, r_colv], [convo.res[c]])

        mean_t, rmean = lnm, r_lnm
        rstd_t, rrstd = lnr, r_lnr

        def lna_stats():
            bmean, rbmean = bank()
            for c in range(8):
                PE(mm(bmean[:, 0:N], ones_f[:], convo.t[:, c, 0:N], c == 0, c == 7), [r_ones, convo.res[c]], [rbmean],
                   signal=(c == 7))
            bex2, rbex2 = bank()
            for c in range(8):
                t1, rt1 = tmp()
                ACT(act(t1[:, 0:N], convo.t[:, c, 0:N], AF.Square), [convo.res[c]], [rt1])
                PE(mm(bex2[:, 0:N], ones_f[:], t1[:, 0:N], c == 0, c == 7), [r_ones, rt1], [rbex2], signal=(c == 7))
            ACT(act(mean_t[:, 0:N], bmean[:, 0:N], AF.Copy), [rbmean], [rmean])
            DVE(tt(rstd_t[:, 0:N], mean_t[:, 0:N], mean_t[:, 0:N], ALU.mult), [rmean], [rrstd])
            DVE(tt(rstd_t[:, 0:N], bex2[:, 0:N], rstd_t[:, 0:N], ALU.subtract), [rbex2, rrstd], [rrstd])
            ACT(act(rstd_t[:, 0:N], rstd_t[:, 0:N], AF.Sqrt, bias=eps_t[:, 0:1]), [rrstd, r_eps], [rrstd])
            DVE(lambda e: e.reciprocal(out=rstd_t[:, 0:N], in_=rstd_t[:, 0:N]), [rrstd], [rrstd])

        def lna_norm(c):
            d1, rd1 = tmp()
            EW = POOL if POOL_OFF else DVE
            EW(tt(d1[:, 0:N], convo.t[:, c, 0:N], mean_t[:, 0:N], ALU.subtract), [convo.res[c], rmean], [rd1])
            EW(tt(d1[:, 0:N], d1[:, 0:N], rstd_t[:, 0:N], ALU.mult), [rd1, rrstd], [rd1])
            t1, rt1 = tmp()
            ACT(act(t1[:, 0:N], d1[:, 0:N], AF.Tanh, scale=dv[:, V_HLAG + c:V_HLAG + c + 1],
                    bias=dv[:, V_HLAB + c:V_HLAB + c + 1]), [rd1, r_dv], [rt1])
            EW(ts(d1[:, 0:N], d1[:, 0:N], colv_t[:, C_LAG + c:C_LAG + c + 1], colv_t[:, C_LAB + c:C_LAB + c + 1],
                  ALU.mult, ALU.add), [rd1, r_colv], [rd1])
            DVE(stt(ca2.t[:, c, 0:N], t1[:, 0:N], 1.0, d1[:, 0:N], ALU.add, ALU.mult), [rt1, rd1], [ca2.res[c]])

        rg_state = {}

        def convb_half(h):
            sl, rs = get_diag(dB[h], 20 * 128, r_dB[h], colv_t[:, C_WDB + 20 * h:C_WDB + 20 * h + 20])
            dg = sl[:, 0:20 * 128].rearrange("p (k n) -> p k n", k=20)
            for m in range(5):
                c = 5 * h + m
                bk, rb = bank()
                for k in range(KB):
                    rhs = uview(bxb, c, 7, 3, k, k + N) if prompt else uview(bxb, c, 7, 3, k, k + 4)
                    PE(mm(bk[:, 0:N], dg[:, m * KB + k, :], rhs, k == 0, k == KB - 1), [rs, bxb.res[c]], [rb],
                       signal=(k == KB - 1))
                ACT(act(cbb.t[:, m, 0:N], bk[:, 0:N], AF.Identity, bias=colv_t[:, C_BDB + c:C_BDB + c + 1]),
                    [rb, r_colv], [cbb.res[m]])
                ACT(act(cb.t[:, m, 0:N], bk[:, 0:N], AF.Identity, bias=colv_t[:, C_BDB + c:C_BDB + c + 1]),
                    [rb, r_colv], [cb.res[m]])
            (wrg, rrg, _, _), (wbg, rbgw, _, _) = w_take("rg", h, 2)
            rg_state.update(wrg=wrg, rrg=rrg, wbg=wbg, rbgw=rbgw)

        blk_idx = {}
        blk = 0
        for g in range(2):
            for m in range(5):
                for kk in RG_BLOCKS[m]:
                    blk_idx[(g, m, kk)] = blk
                    blk += 1

        def rg_stage1(h, m):
            wrg, rrg, wbg, rbgw = rg_state["wrg"], rg_state["rrg"], rg_state["wbg"], rg_state["rbgw"]
            c = 5 * h + m
            kks = RG_BLOCKS[m]
            br, rbr = bank()
            for q, kk in enumerate(kks):
                PE(mm(br[:, 0:N], wrg[:, blk_idx[(0, m, kk)], :], cbb.t[:, kk, 0:N], q == 0, q == len(kks) - 1),
                   [rrg, cbb.res[kk]], [rbr], signal=(q == len(kks) - 1))
            bi, rbi = bank()
            for q, kk in enumerate(kks):
                PE(mm(bi[:, 0:N], wrg[:, blk_idx[(1, m, kk)], :], cbb.t[:, kk, 0:N], q == 0, q == len(kks) - 1),
                   [rrg, cbb.res[kk]], [rbi], signal=(q == len(kks) - 1))
            bgt, rbgt = bank()
            for kc in range(8):
                PE(mm(bgt[:, 0:N], wbg[:, kc, m * 128:(m + 1) * 128], xT.t[:, kc, 0:N], kc == 0, kc == 7),
                   [rbgw, xT.res[kc]], [rbgt], signal=(kc == 7))
            tr_, rtr = tmp()
            ACT(act(tr_[:, 0:N], br[:, 0:N], AF.Tanh, scale=0.5, bias=dv[:, V_HBRA + c:V_HBRA + c + 1]),
                [rbr, r_dv], [rtr])
            ti_, rti = tmp()
            ACT(act(ti_[:, 0:N], bi[:, 0:N], AF.Tanh, scale=0.5, bias=dv[:, V_HBRX + c:V_HBRX + c + 1]),
                [rbi, r_dv], [rti])
            a_, ra = tmp()
            ACT(act(a_[:, 0:N], tr_[:, 0:N], AF.Exp, scale=dv[:, V_HC + c:V_HC + c + 1],
                    bias=dv[:, V_HC + c:V_HC + c + 1]), [rtr, r_dv], [ra])
            a2_, ra2 = tr_, rtr
            if POOL_OFF:
                POOL(tt(a2_[:, 0:N], a_[:, 0:N], a_[:, 0:N], ALU.mult), [ra, rtr], [ra2])
            else:
                ACT(act(a2_[:, 0:N], tr_[:, 0:N], AF.Exp, scale=dv[:, V_C + c:V_C + c + 1],
                        bias=dv[:, V_C + c:V_C + c + 1]), [rtr, r_dv], [ra2])
            sq_, rsq = tmp()
            DVE(stt(ti_[:, 0:N], ti_[:, 0:N], 1.0, cb.t[:, m, 0:N], ALU.add, ALU.mult), [rti, cb.res[m]], [rti])
            return dict(c=c, m=m, a_=a_, ra=ra, a2_=a2_, ra2=ra2, ti_=ti_, rti=rti, sq_=sq_, rsq=rsq, bgt=bgt, rbgt=rbgt)

        def rg_stage1b(st):
            ACT(act(st["sq_"][:, 0:N], st["bgt"][:, 0:N], AF.Gelu_apprx_tanh), [st["rbgt"]], [st["rsq"]])

        def rg_stage2(st):
            c, m, a_, ra, a2_, ra2, ti_, rti, sq_, rsq = (st[k] for k in ("c", "m", "a_", "ra", "a2_", "ra2", "ti_", "rti",
                                                                         "sq_", "rsq"))
            ACT(act(a2_[:, 0:N], a2_[:, 0:N], AF.Sqrt, scale=-0.25, bias=quarter_t[:, 0:1]), [ra2, r_eps], [ra2])
            DVE(tt(a2_[:, 0:N], a2_[:, 0:N], ti_[:, 0:N], ALU.mult), [ra2, rti], [ra2])
            hh, rhh = tmp()
            if DEBUG and first:
                dbg_dump1("aa", c, a_[:, 0:N], ra)
                dbg_dump1("bt", c, a2_[:, 0:N], ra2)
            if prompt:
                if first:
                    DVE(ts1(a2_[:, 0:1], ti_[:, 0:1], 0.5, ALU.mult), [rti, ra2], [ra2])
                DVE((lambda e, hh=hh, a_=a_, a2_=a2_, c=c: e.tensor_tensor_scan(
                    out=hh[:, 0:N], data0=a_[:, 0:N], data1=a2_[:, 0:N],
                    initial=(0.0 if first else hcar[:, c:c + 1]), op0=ALU.mult, op1=ALU.add)),
                    [ra, ra2, r_hcar[c]], [rhh])
                DVE(lambda e, hh=hh, c=c: e.tensor_copy(out=hcar[:, c:c + 1], in_=hh[:, N - 1:N]), [rhh], [r_hcar[c]])
            else:
                for t_ in range(4):
                    prev = h0b.t[:, c, :] if t_ == 0 else hh[:, 16 * (t_ - 1):16 * t_]
                    rprev = [h0b.res[c]] if t_ == 0 else [rhh]
                    DVE(tt(hh[:, 16 * t_:16 * t_ + 16], a_[:, 16 * t_:16 * t_ + 16], prev, ALU.mult), [ra] + rprev, [rhh])
                    DVE(tt(hh[:, 16 * t_:16 * t_ + 16], hh[:, 16 * t_:16 * t_ + 16], a2_[:, 16 * t_:16 * t_ + 16], ALU.add),
                        [rhh, ra2], [rhh])
                DVE(lambda e, hh=hh, c=c: e.tensor_copy(out=hl.t[:, c, :], in_=hh[:, 48:64]), [rhh], [hl.res[c]])
            DVE(stt(hg2.t[:, c, 0:N], sq_[:, 0:N], 2.0, hh[:, 0:N], ALU.mult, ALU.mult), [rsq, rhh], [hg2.res[c]])
            if DEBUG and first:
                dbg_dump1("cb", c, cb.t[:, m, 0:N], cb.res[m])
                dbg_dump1("hh", c, hh[:, 0:N], rhh)

        for c in range(10):
            s2_group(c)
        if prompt and not last:
            DVE(lambda e: e.tensor_copy(out=bcar[:, :, 0:3], in_=bxb.t[:, :, NT:NT + 3]), flat(bxb.res), [r_bcar])
        if STOP == 'S3':
            return
        ca_next = [0]

        def filler(n):
            for _ in range(n):
                if ca_next[0] < 8:
                    conva_chunk(ca_next[0])
                    ca_next[0] += 1

        def rg_half(h, fill):
            for grp, nf in zip(((0, 1), (2, 3), (4,)), fill):
                sts = [rg_stage1(h, m) for m in grp]
                for st in sts:
                    rg_stage1b(st)
                filler(nf)
                for st in sts:
                    rg_stage2(st)

        convb_half(0)
        filler(1)
        rg_half(0, (2, 1, 1))
        convb_half(1)
        filler(1)
        rg_half(1, (1, 1, 0))
        filler(8)
        if prompt and not last:
            DVE(lambda e: e.tensor_copy(out=ucar[:, :, :], in_=uA.t[:, :, NT:NT + 30]), flat(uA.res), [r_ucar])
        lna_stats()
        if STOP == 'S4':
            return
        for i in range(2):
            (wpb, rpb, _, _), (wgb, rgb, _, _) = w_take("pb", i, 2)
            for m in range(4):
                c = 4 * i + m
                by, rby = bank()
                for kc in range(10):
                    PE(mm(by[:, 0:N], wpb[:, kc, m * 128:(m + 1) * 128], hg2.t[:, kc, 0:N], kc == 0, kc == 9),
                       [rpb, hg2.res[kc]], [rby], signal=(kc == 9))
                bg, rbg = bank()
                for kc in range(8):
                    PE(mm(bg[:, 0:N], wgb[:, kc, m * 128:(m + 1) * 128], xT.t[:, kc, 0:N], kc == 0, kc == 7),
                       [rgb, xT.res[kc]], [rbg], signal=(kc == 7))
                t1, rt1 = tmp()
                ACT(act(t1[:, 0:N], bg[:, 0:N], AF.Tanh, scale=0.5), [rbg], [rt1])
                DVE(stt(m_b.t[:, c, 0:N], t1[:, 0:N], 1.0, by[:, 0:N], ALU.add, ALU.mult), [rt1, rby], [m_b.res[c]])
                lna_norm(c)
        for i in range(2):
            (wpa, rpa, _, _), (wga, rga, _, _) = w_take("pa", i, 2)
            for m in range(4):
                c = 4 * i + m
                by, rby = bank()
                for kc in range(8):
                    PE(mm(by[:, 0:N], wpa[:, kc, m * 128:(m + 1) * 128], ca2.t[:, kc, 0:N], kc == 0, kc == 7),
                       [rpa, ca2.res[kc]], [rby], signal=(kc == 7))
                bg, rbg = bank()
                for kc in range(8):
                    PE(mm(bg[:, 0:N], wga[:, kc, m * 128:(m + 1) * 128], xT.t[:, kc, 0:N], kc == 0, kc == 7),
                       [rga, xT.res[kc]], [rbg], signal=(kc == 7))
                t1, rt1 = tmp()
                ACT(act(t1[:, 0:N], bg[:, 0:N], AF.Tanh, scale=0.5), [rbg], [rt1])
                DVE(stt(t1[:, 0:N], t1[:, 0:N], 1.0, by[:, 0:N], ALU.add, ALU.mult), [rt1, rby], [rt1])
                DVE(tt(mixin.t[:, c, 0:N], t1[:, 0:N], m_b.t[:, c, 0:N], ALU.add), [rt1, m_b.res[c]], [mixin.res[c]])
        if DEBUG and first:
            dbg_dump("u2", uA, 8, 30, "bf16")
            dbg_dump("convo", convo, 8, 0, "f32")
            dbg_dump("ca2", ca2, 8, 0, "bf16")
            dbg_dump("hg2", hg2, 10, 0, "bf16")
            dbg_dump("mixin", mixin, 8, 0, "bf16")

        if STOP == 'S6':
            return
        def fm_to_rows(srcbuf, nchunks, ncols, dst_rows_fn):
            for c in range(nchunks):
                bk, rb = bank()
                PE(tr(bk[0:ncols, 0:128], srcbuf.t[:, c, 0:ncols], ident_f[:, :]), [srcbuf.res[c], r_id], [rb], signal=True)
                ACT(act(stg.t[0:ncols, 0, c * 128:(c + 1) * 128], bk[0:ncols, 0:128], AF.Copy), [rb], [stg.res[0]])
            dst_rows_fn()

        if last:
            fm_to_rows(ufp, 8, 30, lambda: out_toks.append(
                S.dma("sp", lambda e: e.dma_start(out=ncap, in_=stg.t[0:30, 0, 0:D]), "ost", flat(stg.res), [])))
            fm_to_rows(bxf, 10, 3, lambda: out_toks.append(
                S.dma("sp", lambda e: e.dma_start(out=ncbp, in_=stg.t[0:3, 0, 0:DR]), "ost", flat(stg.res), [])))
            bk, rb = bank()
            PE(tr(bk[0:10, 0:128], hcar[:, 0:10], ident_f[:, :]), r_hcar + [r_id], [rb], signal=True)
            ACT(act(stg.t[0:10, 0, 0:128], bk[0:10, 0:128], AF.Copy), [rb], [stg.res[0]])
            out_toks.append(S.dma("sp", lambda e: e.dma_start(out=nhp, in_=stg.t[0:10, 0, 0:128]), "ost", flat(stg.res), []))
        if not prompt:
            fm_to_rows(ufp, 8, 64, lambda: out_toks.append(
                S.dma("sp", lambda e: e.dma_start(out=ncas_new, in_=stg.t[0:64, 0, 0:D]), "ost", flat(stg.res), [])))
            fm_to_rows(bxf, 10, 64, lambda: out_toks.append(
                S.dma("sp", lambda e: e.dma_start(out=ncbs, in_=stg.t[16:64, 0, 0:DR]), "ost", flat(stg.res), [])))
            fm_to_rows(hl, 10, 16, lambda: out_toks.append(
                S.dma("sp", lambda e: e.dma_start(out=nhs, in_=stg.t[0:16, 0, 0:DR]), "ost", flat(stg.res), [])))
            out_toks.append(S.dma("sp", lambda e: e.dma_start(out=ncas_old, in_=sca[64:480, :]), "ost2", [], []))

        if STOP == 'state':
            return
        arena.reset(offAB)
        hT = arena.alloc(32, N, BF16)
        x1T = arena.alloc(8, N, BF16)
        x1 = arena.alloc(4, D, F32)
        xfp = arena.alloc(2, D, F32)
        tg = arena.alloc(2, D, F32)
        yb = arena.alloc(2, D, F32)

        (wo0, rwo0, _, _), (wo1, rwo1, _, _) = w_take("wo", 0, 2)
        wos = ((wo0, rwo0), (wo1, rwo1))
        def wo_mm(s):
            S.dma("sp", (lambda e, s=s: e.dma_start(out=xfp.t[0:P, s % 2, :], in_=x_src[s * 128:s * 128 + P, :])),
                  "xfp%d" % (s % 2), [], [xfp.res[s % 2]])
            pr, (rb0, rb1) = bankpair()
            rbs = (rb0, rb1)
            for hf in range(2):
                for kc in range(8):
                    PE(mm(pr[0:P, hf * NT:(hf + 1) * NT], mixin.t[:, kc, s * 128:s * 128 + P], wos[hf][0][:, kc, :],
                          kc == 0, kc == 7), [wos[hf][1], mixin.res[kc]], [rbs[hf]], signal=(kc == 7))
            b = s % 2
            DVE(stt(x1.t[0:P, s, :], xfp.t[0:P, b, :], 4.0 * ALPHA, pr[0:P, :], ALU.mult, ALU.add),
                [xfp.res[b], rb0, rb1], [x1.res[s]])
            bi_ = bn_ptr[0] % 4
            bn_ptr[0] += 1
            for hf in range(2):
                DVE((lambda e, s=s, hf=hf, bi_=bi_: e.bn_stats(out=bnst[0:P, bi_, 6 * hf:6 * hf + 6],
                                                               in_=x1.t[0:P, s, hf * NT:(hf + 1) * NT])),
                    [x1.res[s]], [r_bnst[bi_]])
            mv, rmv = stat4()
            DVE(lambda e, mv=mv, bi_=bi_: e.bn_aggr(out=mv[0:P, 0:2], in_=bnst[0:P, bi_, :]), [r_bnst[bi_]], [rmv])
            DVE(ts1(mv[0:P, 2:3], mv[0:P, 1:2], 16.0 * EPS, ALU.add), [rmv], [rmv])
            POOL(tt(mv[0:P, 2:3], mv[0:P, 2:3], cnh[0:P, 0:1], ALU.pow), [rmv, r_cnh], [rmv])
            DVE(ts(x1.t[0:P, s, :], x1.t[0:P, s, :], mv[0:P, 0:1], mv[0:P, 2:3], ALU.subtract, ALU.mult),
                [x1.res[s], rmv], [x1.res[s]])

        def ln1_tr(s):
            for fc in range(8):
                bk, rb = bank()
                PE(tr(bk[:, 0:P], x1.t[0:P, s, fc * 128:(fc + 1) * 128], ident_f[0:P, 0:P]), [x1.res[s], r_id], [rb],
                   signal=True)
                ACT(act(x1T.t[:, fc, s * 128:s * 128 + P], bk[:, 0:P], AF.Identity,
                        scale=colv_t[:, C_L1G + fc:C_L1G + fc + 1], bias=colv_t[:, C_L1B + fc:C_L1B + fc + 1]),
                    [rb, r_colv], [x1T.res[fc]])
            EW = POOL if POOL_OFF else DVE
            EW(tt(x1.t[0:P, s, :], x1.t[0:P, s, :], bc_t[0:P, 0:D], ALU.mult), [x1.res[s], r_bc], [x1.res[s]])
            EW(tt(x1.t[0:P, s, :], x1.t[0:P, s, :], bc_t[0:P, D:2 * D], ALU.add), [x1.res[s], r_bc], [x1.res[s]])

        wo_mm(0)
        for s in range(1, NS):
            wo_mm(s)
            ln1_tr(s - 1)
        ln1_tr(NS - 1)
        if DEBUG and first:
            dbg_dump("x1T", x1T, 8, 0, "bf16")
            S.dma("sp", lambda e: e.dma_start(out=dbg["x1"], in_=x1.t[:, :, :]), "odbgx", flat(x1.res), [])

        if STOP == 'S7':
            return
        for i in range(8):
            ((wf, rwf, _, _),) = w_take("ff1", i, 1)
            for m in range(4):
                c = 4 * i + m
                bk, rb = bank()
                for kc in range(8):
                    PE(mm(bk[:, 0:N], wf[:, kc, m * 128:(m + 1) * 128], x1T.t[:, kc, 0:N], kc == 0, kc == 7),
                       [rwf, x1T.res[kc]], [rb], signal=(kc == 7))
                t1, rt1 = tmp()
                ACT(act(t1[:, 0:N], bk[:, 0:N], AF.Relu), [rb], [rt1])
                if c % 2 == 0:
                    ACT(act(hT.t[:, c, 0:N], t1[:, 0:N], AF.Square), [rt1], [hT.res[c]])
                else:
                    DVE(tt(hT.t[:, c, 0:N], t1[:, 0:N], t1[:, 0:N], ALU.mult), [rt1], [hT.res[c]])
        if DEBUG and first:
            dbg_dump("hT", hT, 32, 0, "bf16")

        if STOP == 'S8':
            return
        (wg0, rwg0, _, _), (wg1, rwg1, _, _), (wpj, rwpj, _, _) = w_take("pg", 0, 3)
        wgs = ((wg0, rwg0), (wg1, rwg1))
        ple = []
        for s in range(NS):
            b = s % 2
            pr, (rb0, rb1) = bankpair()
            rbs = (rb0, rb1)
            for hf in range(2):
                for kc in range(8):
                    PE(mm(pr[0:P, hf * NT:(hf + 1) * NT], x1T.t[:, kc, s * 128:s * 128 + P], wgs[hf][0][:, kc, :],
                          kc == 0, kc == 7), [wgs[hf][1], x1T.res[kc]], [rbs[hf]], signal=(kc == 7))
            ACT(act(tg.t[0:P, b, :], pr[0:P, :], AF.Tanh, scale=0.5), [rb0, rb1], [tg.res[b]])
            pr2, (rc0, rc1) = bankpair()
            rcs = (rc0, rc1)
            for hf in range(2):
                for kc in range(2):
                    PE(mm(pr2[0:P, hf * NT:(hf + 1) * NT], pT.t[:, kc, s * 128:s * 128 + P],
                          wpj[:, kc, hf * NT:(hf + 1) * NT], kc == 0, kc == 1), [rwpj, pT.res[kc]], [rcs[hf]],
                       signal=(kc == 1))
            DVE(stt(tg.t[0:P, b, :], tg.t[0:P, b, :], 1.0, pr2[0:P, :], ALU.add, ALU.mult), [tg.res[b], rc0, rc1],
                [tg.res[b]])
            DVE(ts1(x1.t[0:P, s, :], x1.t[0:P, s, :], ALPHA, ALU.mult), [x1.res[s]], [x1.res[s]])
            DVE(stt(x1.t[0:P, s, :], tg.t[0:P, b, :], 0.5, x1.t[0:P, s, :], ALU.mult, ALU.add), [tg.res[b], x1.res[s]],
                [x1.res[s]])
        if STOP == 'S9':
            return
        for hf in range(2):
            prs = []
            for s in range(NS):
                bk, rb = bank()
                prs.append((bk, rb))
            for kg in range(4):
                ((wf2, rwf2, _, _),) = w_take("ff2", hf * 4 + kg, 1)
                for s in range(NS):
                    bk, rb = prs[s]
                    for kc in range(8):
                        kglob = 8 * kg + kc
                        PE(mm(bk[0:P, :], hT.t[:, kglob, s * 128:s * 128 + P], wf2[:, kc, :], kglob == 0, kglob == 31),
                           [rwf2, hT.res[kglob]], [rb], signal=(kc == 7))
                if hf == 1 and kg == 1 and nxt_tile is not None and PREFETCH_X:
                    prefetched[nxt_tile] = early_loads(*nxt_tile)
            for s in range(NS):
                bk, rb = prs[s]
                DVE(tt(x1.t[0:P, s, hf * NT:(hf + 1) * NT], x1.t[0:P, s, hf * NT:(hf + 1) * NT], bk[0:P, :], ALU.add),
                    [x1.res[s], rb], [x1.res[s]])
        yield "ff2"
        mv4 = ln2s
        for s in range(NS):
            bi_ = bn_ptr[0] % 4
            bn_ptr[0] += 1
            for hf in range(2):
                DVE((lambda e, s=s, hf=hf, bi_=bi_: e.bn_stats(out=bnst[0:P, bi_, 6 * hf:6 * hf + 6],
                                                               in_=x1.t[0:P, s, hf * NT:(hf + 1) * NT])),
                    [x1.res[s]], [r_bnst[bi_]])
            DVE(lambda e, s=s, bi_=bi_: e.bn_aggr(out=mv4[0:P, s, 0:2], in_=bnst[0:P, bi_, :]), [r_bnst[bi_]], [r_ln2s])
        ACT(act(mv4[0:P, 0:NS, 2], mv4[0:P, 0:NS, 1], AF.Sqrt, bias=eps_t[0:P, 0:1]), [r_ln2s, r_eps], [r_ln2s])
        DVE(lambda e: e.reciprocal(out=mv4[0:P, 0:NS, 2], in_=mv4[0:P, 0:NS, 2]), [r_ln2s], [r_ln2s])
        for s in range(NS):
            b = s % 2
            DVE(ts(yb.t[0:P, b, :], x1.t[0:P, s, :], mv4[0:P, s, 0:1], mv4[0:P, s, 2:3], ALU.subtract, ALU.mult),
                [x1.res[s], r_ln2s], [yb.res[b]])
            DVE(tt(yb.t[0:P, b, :], yb.t[0:P, b, :], bc_t[0:P, 2 * D:3 * D], ALU.mult), [yb.res[b], r_bc], [yb.res[b]])
            DVE(tt(yb.t[0:P, b, :], yb.t[0:P, b, :], bc_t[0:P, 3 * D:4 * D], ALU.add), [yb.res[b], r_bc], [yb.res[b]])
            out_toks.append(S.dma("sp", (lambda e, s=s, b=b: e.dma_start(out=y_dst[s * 128:s * 128 + P, :],
                                                                         in_=yb.t[0:P, b, :])),
                                  "oy%d" % b, [yb.res[b]], []))

    dbg_tmp = {}

    def dbg_dump(name, buf, C, off, kind):
        for c in range(C):
            dbg_dump1(name, c, buf.t[:, c, off:off + NT], buf.res[c])

    def dbg_dump1(name, c, ap, res):
        ti = tmp_ptr[0] % NTMP
        t1, rt1 = tmp()
        DVE(lambda e: e.tensor_copy(out=t1[:, 0:NT], in_=ap), [res], [rt1])
        S.dma("sp", lambda e: e.dma_start(out=dbg[name][:, c, :], in_=t1[:, 0:NT]), "odbg%d" % ti, [rt1], [])

    if STOP != "setup":
        tl = (TILES if TILES is not None else [("p", 0), ("p", 1), ("p", 2), ("p", 3), ("s", 0)])
        gens = [emit_tile(kind, j, tl[q + 1] if q + 1 < len(tl) else None) for q, (kind, j) in enumerate(tl)]

        def run_to(g, tag):
            for t_ in g:
                if t_ == tag:
                    return True
            return False

        run_to(gens[0], "head")
        for q in range(len(gens)):
            alive = run_to(gens[q], "ff2")
            if q + 1 < len(gens) and HEAD_OVERLAP:
                run_to(gens[q + 1], "head")
                nk, nj = tl[q + 1]
                dpre[(nk, nj, id(r_dB[0]))] = d_load(dB[0], 20 * 128, r_dB[0])
                dpre[(nk, nj, id(r_dA[0]))] = d_load(dA[0], KA * 128, r_dA[0])
            if alive:
                run_to(gens[q], None)
            if q + 1 < len(gens) and not HEAD_OVERLAP:
                run_to(gens[q + 1], "head")

    final = list(out_toks)
    for nm, (sem, cnt) in S.dsem.items():
        if nm.startswith("o"):
            final.append(("dma", nm, cnt, sem))
    S.wait_all("sp", final)

    with nc.Block() as block:
        @block.tensor
        def _(e):
            S.replay("pe", e)

        @block.scalar
        def _(e):
            S.replay("act", e)

        @block.vector
        def _(e):
            S.replay("dve", e)

        @block.gpsimd
        def _(e):
            S.replay("pool", e)

        @block.sync
        def _(e):
            S.replay("sp", e)
    es.close()
    return nc


def _host_prep(inp):
    f = lambda a: np.ascontiguousarray(np.asarray(a, dtype=np.float32))
    colv = np.zeros((128, NCOL), np.float32)

    def put(col0, vec):
        v = np.asarray(vec, np.float32).reshape(-1, 128)
        colv[:, col0:col0 + v.shape[0]] = v.T

    put(C_BDA, inp["b_dw_a"][0]); put(C_LAG, inp["ln_a_g"][0]); put(C_LAB, inp["ln_a_b"][0])
    put(C_BDB, inp["b_dw_b"][0]); put(C_BRA, inp["b_rg_a"][0].reshape(-1)); put(C_BRX, inp["b_rg_x"][0].reshape(-1))
    put(C_LAM, inp["rg_lam"][0]); put(C_L1G, inp["ln1_g"][0]); put(C_L1B, inp["ln1_b"][0])
    wda = np.asarray(inp["w_dw_a"][0], np.float32)
    for c in range(8):
        colv[:, C_WDA + c * KA:C_WDA + (c + 1) * KA] = wda[:, c * 128:(c + 1) * 128].T
    wdb = np.asarray(inp["w_dw_b"][0], np.float32)
    for c in range(10):
        colv[:, C_WDB + c * KB:C_WDB + (c + 1) * KB] = wdb[:, c * 128:(c + 1) * 128].T
    bcv = np.zeros((128, 4 * D), np.float32)
    for i, k in enumerate(("ln1_g", "ln1_b", "ln2_g", "ln2_b")):
        bcv[:, i * D:(i + 1) * D] = np.asarray(inp[k][0], np.float32)[None, :]
    rgw = np.zeros((2, 128, 2 * NRGB, 128), np.float32)
    for g, key in enumerate(("w_rg_a", "w_rg_x")):
        wfull = np.zeros((DR, DR), np.float32)
        w = np.asarray(inp[key][0], np.float32)
        for hd in range(16):
            wfull[80 * hd:80 * hd + 80, 80 * hd:80 * hd + 80] = w[hd]
        for h in range(2):
            blk = g * NRGB
            for m in range(5):
                for kk in RG_BLOCKS[m]:
                    r0 = 640 * h + 128 * kk
                    c0 = 640 * h + 128 * m
                    rgw[h, :, blk, :] = wfull[r0:r0 + 128, c0:c0 + 128]
                    blk += 1
    rgw = rgw.reshape(2, 128, 2 * NRGB * 128)
    common = {
        "w_in": f(inp["w_in"][0]), "w_pa": f(inp["w_proj_a"][0]), "w_pb": f(inp["w_proj_b"][0]),
        "w_out": f(inp["w_out"][0]), "w_ff1": f(inp["w_ff1"][0]), "w_ff2": f(inp["w_ff2"][0]),
        "w_pg": f(inp["w_ple_gate"][0]), "w_pp": f(inp["w_ple_proj"][0]),
        "rgw": np.ascontiguousarray(rgw), "colv": colv, "bcv": bcv,
    }
    maps = []
    for i in range(NCORES):
        m = dict(common)
        m["xp"] = f(inp["x_prompt"][i])
        m["xs"] = f(np.asarray(inp["x_sample"])[16 * i:16 * i + 16].transpose(1, 0, 2).reshape(64, D))
        m["ppr"] = f(inp["p_prompt"][0][i])
        m["psm"] = f(np.asarray(inp["p_sample"][0])[16 * i:16 * i + 16].transpose(1, 0, 2).reshape(64, DPL))
        m["sca"] = f(np.asarray(inp["state_conv_a"][0])[16 * i:16 * i + 16].transpose(1, 0, 2).reshape(480, D))
        m["scb"] = f(np.asarray(inp["state_conv_b"][0])[16 * i:16 * i + 16].transpose(1, 0, 2).reshape(48, DR))
        m["sh"] = f(np.asarray(inp["state_h"][0])[16 * i:16 * i + 16])
        maps.append(m)
    return maps


_NC_CACHE = {}


def kernel(**inputs):
    maps = _host_prep(inputs)
    if "nc" not in _NC_CACHE:
        _NC_CACHE["nc"] = build_nc()
    nc = _NC_CACHE["nc"]
    res = run_bass_kernel_spmd(nc, maps, core_ids=list(range(NCORES)))
    R = res.results
    y_p = np.stack([R[i]["yp"] for i in range(NCORES)], 0).astype(np.float32)
    y_s = np.concatenate([R[i]["ys"].reshape(4, 16, D).transpose(1, 0, 2) for i in range(NCORES)], 0).astype(np.float32)
    ca_p = np.stack([R[i]["ncap"] for i in range(NCORES)], 0)[None].astype(np.float32)
    cb_p = np.stack([R[i]["ncbp"] for i in range(NCORES)], 0)[None].astype(np.float32)
    h_p = np.stack([R[i]["nhp"].reshape(DR) for i in range(NCORES)], 0)[None].astype(np.float32)
    ca_s = np.concatenate([np.concatenate([R[i]["ncas_old"].reshape(26, 16, D), R[i]["ncas_new"].reshape(4, 16, D)], 0)
                           .transpose(1, 0, 2) for i in range(NCORES)], 0)[None].astype(np.float32)
    cb_s = np.concatenate([R[i]["ncbs"].reshape(3, 16, DR).transpose(1, 0, 2) for i in range(NCORES)], 0)[None].astype(np.float32)
    h_s = np.concatenate([R[i]["nhs"] for i in range(NCORES)], 0)[None].astype(np.float32)
    if DEBUG:
        kernel.debug = R
    return (y_p, y_s, ca_p, cb_p, h_p, ca_s, cb_s, h_s)
```

```python
import numpy as np
from contextlib import ExitStack
import concourse.bass as bass
import concourse.mybir as mybir
from concourse.bass_utils import run_bass_kernel_spmd

F32 = mybir.dt.float32
BF16 = mybir.dt.bfloat16
AF = mybir.ActivationFunctionType
ALU = mybir.AluOpType

NCORES = 8
D = 1024
DR = 1280
DFF = 4096
DPL = 256
DIN = 6656
SEQ = 2048
NT = 512
KA = 31
KB = 4
ALPHA = 2.0 ** 0.25
EPS = 1e-5
GK = 0.7978845608028654
C_BDA, C_LAG, C_LAB, C_BDB, C_BRA, C_BRX, C_LAM, C_L1G, C_L1B, C_WDA, C_WDB = 0, 8, 16, 24, 34, 44, 54, 64, 72, 80, 328
NCOL = 368
V_HBRA, V_HBRX, V_C, V_HC, V_HLAG, V_HLAB, V_E, V_SP = 0, 10, 20, 30, 40, 48, 56, 66
NDV = 80
RG_BLOCKS = {0: (0, 1), 1: (0, 1, 2), 2: (1, 2, 3), 3: (2, 3, 4), 4: (3, 4)}
NRGB = 13
DEBUG = False
BIS = set()
PREFETCH_X = True
HEAD_OVERLAP = True
W_SCRATCH = True
POOL_OFF = True
STOP = None
TILES = None


class Res:
    __slots__ = ("name", "w", "r", "excl")

    def __init__(self, name, excl=False):
        self.name = name
        self.w = None
        self.r = {}
        self.excl = excl


class Sched:
    ENGS = ("pe", "act", "dve", "pool", "sp")

    def __init__(self, nc, es):
        self.nc = nc
        self.es = es
        self.prog = {e: [] for e in self.ENGS}
        self.cnt = {e: 0 for e in self.ENGS}
        self.sem = {e: es.enter_context(nc.semaphore("s_" + e)) for e in ("pe", "act", "dve", "pool")}
        self.seen = {e: {} for e in self.ENGS}
        self.dsem = {}

    def dma_sem(self, name):
        if name not in self.dsem:
            self.dsem[name] = [self.es.enter_context(self.nc.semaphore("d_" + name)), 0]
        return self.dsem[name]

    def _need(self, eng, tok, waits, skip_same):
        if tok is None:
            return
        kind, key, val, sem = tok
        if kind == "eng" and key == eng and skip_same:
            return
        k = (kind, key)
        if self.seen[eng].get(k, 0) >= val:
            return
        if k not in waits or waits[k][1] < val:
            waits[k] = (sem, val)

    def _deps(self, eng, reads, writes, is_dma):
        waits = {}
        for r in reads:
            self._need(eng, r.w, waits, (eng == "pe") and not is_dma)
            if r.excl:
                for t in r.r.values():
                    self._need(eng, t, waits, True)
        for w in writes:
            self._need(eng, w.w, waits, (eng == "pe") and not is_dma)
            for t in w.r.values():
                self._need(eng, t, waits, (eng == "pe") and not is_dma)
        for k, (sem, val) in waits.items():
            self.seen[eng][k] = val
        return list(waits.values())

    def _commit(self, tok, reads, writes):
        for r in reads:
            k = (tok[0], tok[1])
            if k not in r.r or r.r[k][2] < tok[2]:
                r.r[k] = tok
        for w in writes:
            w.w = tok
            w.r = {}

    def op(self, eng, fn, reads=(), writes=(), signal=True):
        reads, writes = flat(reads), flat(writes)
        waits = self._deps(eng, reads, writes, False)
        if signal:
            self.cnt[eng] += 1
            tok = ("eng", eng, self.cnt[eng], self.sem[eng])
        else:
            tok = ("eng", eng, self.cnt[eng] + 1, self.sem[eng])
        self.prog[eng].append((waits, fn, self.sem[eng] if signal else None, 1))
        self._commit(tok, reads, writes)
        return tok

    def dma(self, q, fn, semname, reads=(), writes=()):
        reads, writes = flat(reads), flat(writes)
        waits = self._deps(q, reads, writes, True)
        ds = self.dma_sem(semname)
        ds[1] += 16
        tok = ("dma", semname, ds[1], ds[0])
        self.prog[q].append((waits, fn, ds[0], 16))
        self._commit(tok, reads, writes)
        return tok

    def wait_all(self, eng, toks):
        waits = {}
        for t in toks:
            self._need(eng, t, waits, False)
        for k, (sem, val) in waits.items():
            self.seen[eng][k] = val
        self.prog[eng].append((list(waits.values()), None, None, 0))

    def replay(self, eng, e):
        for waits, fn, sem, inc in self.prog[eng]:
            for s, v in waits:
                e.wait_ge(s, v)
            if fn is None:
                continue
            ins = fn(e)
            if sem is not None:
                ins.then_inc(sem, inc)


class Buf:
    def __init__(self, ap3, res_list):
        self.t = ap3
        self.res = res_list

    def r(self, c):
        return self.res[c]


BLK = 2048


class Arena:
    def __init__(self, nc, es, name, nbytes):
        self.nbytes = nbytes
        self.t = es.enter_context(nc.sbuf_tensor(name, [128, nbytes // 4], F32))
        self.blocks = [Res("%s_b%d" % (name, i)) for i in range((nbytes + BLK - 1) // BLK)]
        self.off = 0

    def reset(self, off=0):
        self.off = off

    def alloc(self, C, n, dtype):
        esz = 2 if dtype == BF16 else 4
        nb = C * n * esz
        nb_al = (nb + 63) // 64 * 64
        lo = self.off
        assert lo + nb_al <= self.nbytes, (lo, nb_al, self.nbytes)
        self.off += nb_al
        ap = self.t[:, lo // 4:(lo + nb) // 4]
        if dtype == BF16:
            ap = ap.bitcast(BF16)
        ap = ap.rearrange("p (c n) -> p c n", c=C)
        res = []
        for c in range(C):
            a = lo + c * n * esz
            b = a + n * esz
            res.append(MultiRes([self.blocks[i] for i in range(a // BLK, (b - 1) // BLK + 1)]))
        return Buf(ap, res)


class MultiRes:
    def __init__(self, blocks):
        self.blocks = blocks


def flat(rs):
    out = []
    for r in rs:
        if isinstance(r, MultiRes):
            out.extend(r.blocks)
        elif isinstance(r, (list, tuple)):
            out.extend(flat(r))
        else:
            out.append(r)
    seen = set()
    o2 = []
    for r in out:
        if id(r) not in seen:
            seen.add(id(r))
            o2.append(r)
    return o2


def build_nc():
    nc = bass.Bass("TRN2", target_bir_lowering=False)

    def din(name, shape):
        return nc.dram_tensor(name, list(shape), F32, kind="ExternalInput").ap()

    def dout(name, shape):
        return nc.dram_tensor(name, list(shape), F32, kind="ExternalOutput").ap()

    xp = din("xp", [SEQ, D]); xs = din("xs", [64, D])
    ppr = din("ppr", [SEQ, DPL]); psm = din("psm", [64, DPL])
    sca = din("sca", [480, D]); scb = din("scb", [48, DR]); sh = din("sh", [16, DR])
    w_in = din("w_in", [D, DIN]); w_pa = din("w_pa", [D, D]); w_pb = din("w_pb", [DR, D])
    w_out = din("w_out", [D, D]); w_ff1 = din("w_ff1", [D, DFF]); w_ff2 = din("w_ff2", [DFF, D])
    w_pg = din("w_pg", [D, D]); w_pp = din("w_pp", [DPL, D])
    rgw = din("rgw", [2, 128, 2 * NRGB * 128])
    colv = din("colv", [128, NCOL]); bcv = din("bcv", [128, 4 * D])
    yp = dout("yp", [SEQ, D]); ys = dout("ys", [64, D])
    ncap = dout("ncap", [30, D]); ncbp = dout("ncbp", [3, DR]); nhp = dout("nhp", [10, 128])
    ncas_new = dout("ncas_new", [64, D]); ncas_old = dout("ncas_old", [416, D])
    ncbs = dout("ncbs", [48, DR]); nhs = dout("nhs", [16, DR])
    dA = nc.dram_tensor("dA", [8, 128, KA * 128], BF16, kind="Internal").ap()
    dB = nc.dram_tensor("dB", [2, 128, 5 * KB * 128], BF16, kind="Internal").ap()
    dbg = {}
    if DEBUG:
        for nm, C in (("u2", 8), ("convo", 8), ("ca2", 8), ("cb", 10), ("hh", 10), ("aa", 10), ("bt", 10), ("hg2", 10), ("mixin", 8),
                      ("x1T", 8), ("hT", 32)):
            dbg[nm] = nc.dram_tensor("dbg_" + nm, [128, C, NT], F32, kind="ExternalOutput").ap()
        dbg["x1"] = nc.dram_tensor("dbg_x1", [128, 4, D], F32, kind="ExternalOutput").ap()

    es = ExitStack()
    S = Sched(nc, es)

    def sb(name, shape, dt):
        return es.enter_context(nc.sbuf_tensor(name, list(shape), dt))

    colv_t = sb("colv_t", [128, NCOL], F32); r_colv = Res("colv")
    dv = sb("dv", [128, NDV], F32); r_dv = Res("dv")
    wdah = sb("wdah", [128, 8 * KA], F32); r_wdah = Res("wdah")
    bc_t = sb("bc_t", [128, 4 * D], F32); r_bc = Res("bc")
    ident_f = sb("ident_f", [128, 128], F32); ident_b = sb("ident_b", [128, 128], BF16); r_id = Res("ident")
    ones_f = sb("ones_f", [128, 128], F32); r_ones = Res("ones")
    eps_t = sb("eps_t", [128, 1], F32); quarter_t = sb("quarter_t", [128, 1], F32); r_eps = Res("eps")
    cnh = sb("cnh", [128, 8], F32); r_cnh = Res("cnh")
    hcar = sb("hcar", [128, 10], F32); r_hcar = [Res("hcar%d" % i) for i in range(10)]
    ucar = sb("ucar", [128, 8, 30], BF16); r_ucar = Res("ucar")
    bcar = sb("bcar", [128, 10, 4], BF16); r_bcar = Res("bcar")
    NTMP = 12
    tmps = [sb("tmp%d" % i, [128, NT], F32) for i in range(NTMP)]
    r_tmps = [Res("tmp%d" % i) for i in range(NTMP)]
    tmp_ptr = [0]

    def tmp():
        i = tmp_ptr[0] % NTMP
        tmp_ptr[0] += 1
        return tmps[i], r_tmps[i]

    lnm = sb("lnm", [128, NT], F32); r_lnm = Res("lnm")
    lnr = sb("lnr", [128, NT], F32); r_lnr = Res("lnr")
    stat = sb("stat", [128, 64], F32)
    r_stat = [Res("stat%d" % i) for i in range(16)]
    stat_ptr = [0]

    def stat4():
        i = stat_ptr[0] % 16
        stat_ptr[0] += 1
        return stat[:, 4 * i:4 * i + 4], r_stat[i]

    ln2s = sb("ln2s", [128, 4, 4], F32); r_ln2s = Res("ln2s")
    bnst = sb("bnst", [128, 4, 12], F32)
    r_bnst = [Res("bnst%d" % i) for i in range(4)]
    bn_ptr = [0]

    WSLOT = 5120
    NW = 4
    wslots = [sb("wslot%d" % i, [128, WSLOT], BF16) for i in range(NW)]
    r_wslots = [Res("wslot%d" % i) for i in range(NW)]
    DSLOT = KA * 128
    ND = 2
    dslots = [sb("dslot%d" % i, [128, DSLOT], BF16) for i in range(ND)]
    r_dslots = [Res("dslot%d" % i) for i in range(ND)]

    pairs = [es.enter_context(nc.psum_tensor("pp%d" % i, [128, 2 * NT], F32)) for i in range(4)]
    r_banks = [Res("bank%d" % i, excl=True) for i in range(8)]
    bank_ptr = [0]

    def bank():
        i = bank_ptr[0] % 8
        bank_ptr[0] += 1
        return pairs[i // 2][:, (i % 2) * NT:(i % 2 + 1) * NT], r_banks[i]

    def bankpair():
        if bank_ptr[0] % 2:
            bank_ptr[0] += 1
        i = bank_ptr[0] % 8
        bank_ptr[0] += 2
        return pairs[i // 2], (r_banks[i], r_banks[i + 1])

    ARENA = 100 * 1024
    arena = Arena(nc, es, "arena", ARENA)

    def ACT(fn, reads, writes):
        return S.op("act", fn, flat(reads), flat(writes))

    def DVE(fn, reads, writes):
        return S.op("dve", fn, flat(reads), flat(writes))

    def POOL(fn, reads, writes):
        return S.op("pool", fn, flat(reads), flat(writes))

    def PE(fn, reads, writes, signal):
        return S.op("pe", fn, flat(reads), flat(writes), signal=signal)

    def act(out, in_, func, scale=1.0, bias=0.0):
        return lambda e: e.activation(out=out, in_=in_, func=func, scale=scale, bias=bias)

    def tt(out, in0, in1, op):
        return lambda e: e.tensor_tensor(out=out, in0=in0, in1=in1, op=op)

    def ts(out, in0, s1, s2, op0, op1):
        return lambda e: e.tensor_scalar(out=out, in0=in0, scalar1=s1, scalar2=s2, op0=op0, op1=op1)

    def ts1(out, in0, s1, op0):
        return lambda e: e.tensor_single_scalar(out=out, in_=in0, scalar=s1, op=op0)

    def stt(out, in0, scalar, in1, op0, op1):
        return lambda e: e.scalar_tensor_tensor(out=out, in0=in0, scalar=scalar, in1=in1, op0=op0, op1=op1)

    def mm(out, lhsT, rhs, start, stop):
        return lambda e: e.matmul(out, lhsT=lhsT, rhs=rhs, start=start, stop=stop)

    def tr(out, in_, ident):
        return lambda e: e.transpose(out, in_, ident)

    S.dma("sp", lambda e: e.dma_start(out=colv_t[:], in_=colv), "const0", [], [r_colv])
    S.dma("sp", lambda e: e.dma_start(out=bc_t[:], in_=bcv), "const1", [], [r_bc])
    POOL(lambda e: e.memset(ident_f[:], 0.0), [], [r_id])
    POOL(lambda e: e.affine_select(out=ident_f[:], in_=ident_f[:], pattern=[[-1, 128]], compare_op=ALU.not_equal,
                                   fill=1.0, base=0, channel_multiplier=1), [r_id], [r_id])
    POOL(lambda e: e.tensor_copy(out=ident_b[:], in_=ident_f[:]), [r_id], [r_id])
    POOL(lambda e: e.memset(ones_f[:], 1.0 / D), [], [r_ones])
    POOL(lambda e: e.memset(eps_t[:], EPS), [], [r_eps])
    POOL(lambda e: e.memset(quarter_t[:], 0.25), [r_eps], [r_eps])
    POOL(lambda e: e.memset(cnh[:], -0.5), [], [r_cnh])
    POOL(lambda e: e.memset(hcar[:], 0.0), [], r_hcar)
    ACT(act(dv[:, V_E:V_E + 10], colv_t[:, C_LAM:C_LAM + 10], AF.Exp, scale=-1.0), [r_colv], [r_dv])
    ACT(act(dv[:, V_SP:V_SP + 10], dv[:, V_E:V_E + 10], AF.Ln, bias=1.0), [r_dv], [r_dv])
    DVE(ts1(dv[:, V_C:V_C + 10], dv[:, V_SP:V_SP + 10], -8.0, ALU.mult), [r_dv], [r_dv])
    DVE(ts1(dv[:, V_HC:V_HC + 10], dv[:, V_SP:V_SP + 10], -4.0, ALU.mult), [r_dv], [r_dv])
    DVE(ts1(dv[:, V_HBRA:V_HBRA + 10], colv_t[:, C_BRA:C_BRA + 10], 0.5, ALU.mult), [r_colv], [r_dv])
    DVE(ts1(dv[:, V_HBRX:V_HBRX + 10], colv_t[:, C_BRX:C_BRX + 10], 0.5, ALU.mult), [r_colv], [r_dv])
    DVE(ts1(dv[:, V_HLAG:V_HLAG + 8], colv_t[:, C_LAG:C_LAG + 8], 0.5, ALU.mult), [r_colv], [r_dv])
    DVE(ts1(dv[:, V_HLAB:V_HLAB + 8], colv_t[:, C_LAB:C_LAB + 8], 0.5, ALU.mult), [r_colv], [r_dv])
    DVE(ts1(wdah[:], colv_t[:, C_WDA:C_WDA + 8 * KA], 0.5, ALU.mult), [r_colv], [r_wdah])
    r_dA = [Res("dA%d" % c) for c in range(8)]
    r_dB = [Res("dB%d" % h) for h in range(2)]

    def wsrc(w, K, c0, nc_):
        return w.rearrange("(k p) n -> p k n", p=128)[:, 0:K, c0:c0 + nc_]

    def tile_plan():
        pl = []
        for i in range(2):
            pl.append(("in_av", i, wsrc(w_in, 8, 512 * i, 512), 8, 512))
            pl.append(("in_ag", i, wsrc(w_in, 8, 1024 + 512 * i, 512), 8, 512))
        for h in range(2):
            pl.append(("in_bx", h, wsrc(w_in, 8, 2048 + 640 * h, 640), 8, 640))
        for h in range(2):
            pl.append(("rg", h, rgw[h].rearrange("p (k n) -> p k n", k=2 * NRGB), 2 * NRGB, 128))
            pl.append(("in_bg", h, wsrc(w_in, 8, 3328 + 640 * h, 640), 8, 640))
        for i in range(2):
            pl.append(("pb", i, wsrc(w_pb, 10, 512 * i, 512), 10, 512))
            pl.append(("in_gb", i, wsrc(w_in, 8, 5632 + 512 * i, 512), 8, 512))
        for i in range(2):
            pl.append(("pa", i, wsrc(w_pa, 8, 512 * i, 512), 8, 512))
            pl.append(("in_ga", i, wsrc(w_in, 8, 4608 + 512 * i, 512), 8, 512))
        for i in range(2):
            pl.append(("wo", i, wsrc(w_out, 8, 512 * i, 512), 8, 512))
        for i in range(8):
            pl.append(("ff1", i, wsrc(w_ff1, 8, 512 * i, 512), 8, 512))
        for i in range(2):
            pl.append(("pg", i, wsrc(w_pg, 8, 512 * i, 512), 8, 512))
        pl.append(("ppj", 0, wsrc(w_pp, 2, 0, 1024), 2, 1024))
        for hf in range(2):
            for kg in range(4):
                src = w_ff2.rearrange("(k p) n -> p k n", p=128)[:, 8 * kg:8 * kg + 8, 512 * hf:512 * hf + 512]
                pl.append(("ff2", hf * 4 + kg, src, 8, 512))
        return pl

    NTILES = 5
    wplan = []
    for t in range(NTILES):
        wplan.extend(tile_plan())
    wstate = {"loaded": 0, "cur": 0}

    NPIECE = len(tile_plan())
    wscr = nc.dram_tensor("wscr", [NPIECE, 128, WSLOT], BF16, kind="Internal").ap()
    r_wscr = [Res("wscr%d" % i) for i in range(NPIECE)]

    def w_advance():
        lim = min(len(wplan), wstate["cur"] + NW)
        while wstate["loaded"] < lim:
            j = wstate["loaded"]
            nm, idx, src, K, ncol = wplan[j]
            sl, rs = wslots[j % NW], r_wslots[j % NW]
            jl = j % NPIECE
            if (not W_SCRATCH) or j < NPIECE or (j < 2 * NPIECE and jl % 2 == 1):
                dst = sl[:, 0:K * ncol].rearrange("p (k n) -> p k n", k=K)
                S.dma("pool", (lambda e, dst=dst, src=src: e.dma_start(out=dst, in_=src)), "w%d" % (j % NW), [], [rs])
            else:
                S.dma("pool", (lambda e, sl=sl, jl=jl, n_=K * ncol: e.dma_start(out=sl[:, 0:n_], in_=wscr[jl][:, 0:n_])),
                      "w%d" % (j % NW), [r_wscr[jl]], [rs])
            wstate["loaded"] += 1

    def w_take(name, idx, n=1):
        w_advance()
        out = []
        for q in range(n):
            j = wstate["cur"] + q
            nm, ix, src, K, ncol = wplan[j]
            assert j < wstate["loaded"], (j, wstate)
            sl, rs = wslots[j % NW], r_wslots[j % NW]
            out.append((sl[:, 0:K * ncol].rearrange("p (k n) -> p k n", k=K), rs, nm, ix))
            jl_ = j % NPIECE
            if W_SCRATCH and ((j < NPIECE and jl_ % 2 == 0) or (NPIECE <= j < 2 * NPIECE and jl_ % 2 == 1)):
                S.dma("sp", (lambda e, sl=sl, jl=jl_, n_=K * ncol: e.dma_start(out=wscr[jl][:, 0:n_], in_=sl[:, 0:n_])),
                      "wb%d" % (j % NW), [rs], [r_wscr[jl_]])
        assert out[0][2] == name and out[0][3] == idx, (out[0][2:], name, idx)
        wstate["cur"] += n
        return out

    dstate = {"n": 0}

    def d_load(src, ncols, rsrc):
        j = dstate["n"]
        dstate["n"] += 1
        sl, rs = dslots[j % ND], r_dslots[j % ND]
        S.dma("sp", (lambda e, sl=sl, src=src, ncols=ncols: e.dma_start(out=sl[:, 0:ncols], in_=src)),
              "d%d" % (j % ND), [rsrc], [rs])
        return sl, rs

    out_toks = []

    def early_loads(kind, j):
        prompt = (kind == "p")
        N = NT if prompt else 64
        x_src = xp[j * NT:j * NT + N, :] if prompt else xs
        p_src = ppr[j * NT:j * NT + N, :] if prompt else psm
        arena.reset(0)
        mixin = arena.alloc(8, N, BF16)
        pT = arena.alloc(2, N, BF16)
        offAB = arena.off
        xbf = arena.alloc(4, D, BF16)
        pbf = arena.alloc(4, DPL, BF16)
        if prompt:
            S.dma("pool", lambda e: e.dma_start(out=xbf.t[:, :, :], in_=x_src.rearrange("(s p) d -> p s d", p=128)),
                  "xbf", [], flat(xbf.res))
            S.dma("pool", lambda e: e.dma_start(out=pbf.t[:, :, :], in_=p_src.rearrange("(s p) d -> p s d", p=128)),
                  "pbf", [], flat(pbf.res))
        else:
            S.dma("pool", lambda e: e.dma_start(out=xbf.t[0:64, 0, :], in_=x_src), "xbf", [], flat(xbf.res))
            S.dma("pool", lambda e: e.dma_start(out=pbf.t[0:64, 0, :], in_=p_src), "pbf", [], flat(pbf.res))
        return dict(mixin=mixin, pT=pT, offAB=offAB, xbf=xbf, pbf=pbf, off=arena.off)

    prefetched = {}
    dpre = {}

    def emit_tile(kind, j, nxt_tile=None):
        prompt = (kind == "p")
        N = NT if prompt else 64
        NS = 4 if prompt else 1
        P = 128 if prompt else 64
        first = prompt and j == 0
        last = prompt and j == 3
        tok0 = j * NT
        x_src = xp[tok0:tok0 + N, :] if prompt else xs
        p_src = ppr[tok0:tok0 + N, :] if prompt else psm
        y_dst = yp[tok0:tok0 + N, :] if prompt else ys
        LA = 30 + NT if prompt else 16 * 34
        LB = 3 + NT if prompt else 16 * 7

        pre = prefetched.pop((kind, j), None)
        if pre is None:
            pre = early_loads(kind, j)
        mixin, pT, offAB, xbf, pbf = pre["mixin"], pre["pT"], pre["offAB"], pre["xbf"], pre["pbf"]
        arena.reset(pre["off"])
        xT = arena.alloc(8, N, BF16)
        _off_u = arena.off
        uA = arena.alloc(8, 544, BF16)
        bxb = arena.alloc(10, 516 if prompt else 112, BF16)
        _save = arena.off
        arena.reset(_off_u)
        m_b = arena.alloc(8, N, F32)
        assert arena.off <= _save
        arena.reset(_save)
        if last or not prompt:
            ufp = arena.alloc(8, 64, F32)
            h0b = arena.alloc(10, 16, F32)
        if not prompt:
            sca_t = arena.alloc(4, D, BF16)
            scb_t = arena.alloc(1, DR, BF16)
            sh_t = arena.alloc(1, DR, F32)
        assert arena.off <= 50 * 1024, arena.off
        convo = arena.alloc(8, N, F32)
        _save = arena.off
        arena.reset(offAB)
        ca2 = arena.alloc(8, N, BF16)
        arena.reset(_save)
        cb = arena.alloc(5, N, F32)
        cbb = arena.alloc(5, N, BF16)
        hg2 = arena.alloc(10, N, BF16)
        if last or not prompt:
            bxf = arena.alloc(10, 64, F32)
            hl = arena.alloc(10, 16, F32)
            stg = arena.alloc(1, DR, F32)

        def uview(buf, c, L, ctx, a, b):
            if prompt:
                return buf.t[:, c, a:b]
            return buf.t[:, c, 16 * a:16 * b]

        def nview(ap2):
            return ap2

        def transpose_in(src_buf, nfc, dst_buf):
            for fc in range(nfc):
                bk, rb = bank()
                bkb = bk[:, 0:NT // 2].bitcast(BF16)
                for s in range(NS):
                    PE(tr(bkb[:, s * 128:s * 128 + P], src_buf.t[0:P, s, fc * 128:(fc + 1) * 128], ident_b[0:P, 0:P]),
                       [src_buf.res[s], r_id], [rb], signal=(s == NS - 1))
                ACT(act(dst_buf.t[:, fc, 0:N], bkb[:, 0:N], AF.Copy), [rb], [dst_buf.res[fc]])

        transpose_in(xbf, 8, xT)
        transpose_in(pbf, 2, pT)

        if STOP == 'T':
            return
        if prompt and first:
            DVE(lambda e: e.memset(uA.t[:, :, 0:30], 0.0), [], flat(uA.res))
            DVE(lambda e: e.memset(bxb.t[:, :, 0:3], 0.0), [], flat(bxb.res))
        elif prompt:
            DVE(lambda e: e.tensor_copy(out=uA.t[:, :, 0:30], in_=ucar[:, :, :]), [r_ucar], flat(uA.res))
            DVE(lambda e: e.tensor_copy(out=bxb.t[:, :, 0:3], in_=bcar[:, :, 0:3]), [r_bcar], flat(bxb.res))
        else:
            for s_ in range(4):
                S.dma("pool", (lambda e, s_=s_: e.dma_start(out=sca_t.t[0:120, s_, :], in_=sca[120 * s_:120 * s_ + 120, :])),
                      "sca%d" % s_, [], [sca_t.res[s_]])
            for fc in range(8):
                for s_ in range(4):
                    bk, rb = bank()
                    bkb = bk[:, 0:NT // 2].bitcast(BF16)
                    PE(tr(bkb[:, 0:120], sca_t.t[0:120, s_, fc * 128:(fc + 1) * 128], ident_b[0:120, 0:120]),
                       [sca_t.res[s_], r_id], [rb], signal=True)
                    ACT(act(uA.t[:, fc, 120 * s_:120 * s_ + 120], bkb[:, 0:120], AF.Copy, scale=2.0), [rb], [uA.res[fc]])
            S.dma("pool", lambda e: e.dma_start(out=scb_t.t[0:48, 0, :], in_=scb), "scb", [], flat(scb_t.res))
            S.dma("sp", lambda e: e.dma_start(out=sh_t.t[0:16, 0, :], in_=sh), "sh", [], flat(sh_t.res))
            for fc in range(10):
                bk, rb = bank()
                bkb = bk[:, 0:NT // 2].bitcast(BF16)
                PE(tr(bkb[:, 0:48], scb_t.t[0:48, 0, fc * 128:(fc + 1) * 128], ident_b[0:48, 0:48]),
                   [scb_t.res[0], r_id], [rb], signal=True)
                ACT(act(bxb.t[:, fc, 0:48], bkb[:, 0:48], AF.Copy), [rb], [bxb.res[fc]])
                bk, rb = bank()
                PE(tr(bk[:, 0:16], sh_t.t[0:16, 0, fc * 128:(fc + 1) * 128], ident_f[0:16, 0:16]),
                   [sh_t.res[0], r_id], [rb], signal=True)
                ACT(act(h0b.t[:, fc, :], bk[:, 0:16], AF.Copy), [rb], [h0b.res[fc]])

        need_state = last or not prompt

        build_list = [("B", 0), ("A", 0), ("A", 1), ("A", 2), ("A", 3), ("A", 4), ("B", 1), ("A", 5), ("A", 6), ("A", 7)]

        def build_next():
            if not build_list:
                return
            typ, q = build_list.pop(0)
            if typ == "A":
                src, ncols, rsrc, in1 = dA[q], KA * 128, r_dA[q], wdah[:, q * KA:(q + 1) * KA]
            else:
                src, ncols, rsrc, in1 = dB[q], 20 * 128, r_dB[q], colv_t[:, C_WDB + 20 * q:C_WDB + 20 * q + 20]
            jd = dstate["n"]
            dstate["n"] += 1
            sl, rs = dslots[jd % ND], r_dslots[jd % ND]
            K = ncols // 128
            o3 = sl[:, 0:ncols].rearrange("p (k n) -> p k n", k=K)
            i0 = ident_f[:].unsqueeze(1).to_broadcast([128, K, 128])
            i1 = in1.unsqueeze(2).to_broadcast([128, K, 128])
            DVE(tt(o3, i0, i1, ALU.mult), [r_id, r_wdah, r_colv], [rs])
            S.dma("sp", (lambda e, sl=sl, src=src, ncols=ncols: e.dma_start(out=src, in_=sl[:, 0:ncols])),
                  "db%d" % (jd % ND), [rs], [rsrc])

        for i in range(2):
            (wav, rav, _, _), (wag, rag, _, _) = w_take("in_av", i, 2)
            for m in range(4):
                c = 4 * i + m
                bv, rbv = bank()
                for kc in range(8):
                    PE(mm(bv[:, 0:N], wav[:, kc, m * 128:(m + 1) * 128], xT.t[:, kc, 0:N], kc == 0, kc == 7),
                       [rav, xT.res[kc]], [rbv], signal=(kc == 7))
                bg, rbg = bank()
                for kc in range(8):
                    PE(mm(bg[:, 0:N], wag[:, kc, m * 128:(m + 1) * 128], xT.t[:, kc, 0:N], kc == 0, kc == 7),
                       [rag, xT.res[kc]], [rbg], signal=(kc == 7))
                t1, rt1 = tmp()
                ACT(act(t1[:, 0:N], bg[:, 0:N], AF.Tanh, scale=0.5), [rbg], [rt1])
                DVE(stt(uview(uA, c, 34, 30, 30, 30 + N) if prompt else uview(uA, c, 34, 30, 30, 34),
                        nview(t1[:, 0:N]), 1.0, nview(bv[:, 0:N]), ALU.add, ALU.mult), [rt1, rbv], [uA.res[c]])
                if need_state:
                    n0 = N - 30 if prompt else 0
                    nn = 30 if prompt else 64
                    DVE(stt(ufp.t[:, c, 0:nn], t1[:, n0:n0 + nn], 1.0, bv[:, n0:n0 + nn], ALU.add, ALU.mult),
                        [rt1, rbv], [ufp.res[c]])
                    DVE(ts1(ufp.t[:, c, 0:nn], ufp.t[:, c, 0:nn], 0.5, ALU.mult), [ufp.res[c]], [ufp.res[c]])
                if first:
                    build_next()
        if first:
            while build_list:
                build_next()

        yield "head"
        if STOP == 'S1':
            return

        bx_state = {}

        def s2_group(c):
            h, m = divmod(c, 5)
            if m == 0:
                ((bx_state["w"], bx_state["r"], _, _),) = w_take("in_bx", h, 1)
            wbx, rbx = bx_state["w"], bx_state["r"]
            bk, rb = bank()
            for kc in range(8):
                PE(mm(bk[:, 0:N], wbx[:, kc, m * 128:(m + 1) * 128], xT.t[:, kc, 0:N], kc == 0, kc == 7),
                   [rbx, xT.res[kc]], [rb], signal=(kc == 7))
            ACT(act(uview(bxb, c, 7, 3, 3, 3 + N) if prompt else uview(bxb, c, 7, 3, 3, 7),
                    bk[:, 0:N], AF.Copy), [rb], [bxb.res[c]])
            if need_state:
                n0 = N - 3 if prompt else 0
                nn = 3 if prompt else 64
                ACT(act(bxf.t[:, c, 0:nn], bk[:, n0:n0 + nn], AF.Copy), [rb], [bxf.res[c]])

        def get_diag(src, ncols, rsrc, build_in1):
            pk = dpre.pop((kind, j, id(rsrc)), None)
            if pk is not None:
                return pk
            return d_load(src, ncols, rsrc)

        def conva_chunk(c):
            sl, rs = get_diag(dA[c], KA * 128, r_dA[c], wdah[:, c * KA:(c + 1) * KA])
            dg = sl[:, 0:KA * 128].rearrange("p (k n) -> p k n", k=KA)
            bk, rb = bank()
            for k in range(KA):
                rhs = uview(uA, c, 34, 30, k, k + N) if prompt else uview(uA, c, 34, 30, k, k + 4)
                PE(mm(bk[:, 0:N], dg[:, k, :], rhs, k == 0, k == KA - 1), [rs, uA.res[c]], [rb],
                   signal=(k == KA - 1))
            ACT(act(convo.t[:, c, 0:N], bk[:, 0:N], AF.Identity, bias=colv_t[:, C_BDA + c:C_BDA + c + 1]),
                [rb, r_colv], [convo.res[c]])

        mean_t, rmean = lnm, r_lnm
        rstd_t, rrstd = lnr, r_lnr

        def lna_stats():
            bmean, rbmean = bank()
            for c in range(8):
                PE(mm(bmean[:, 0:N], ones_f[:], convo.t[:, c, 0:N], c == 0, c == 7), [r_ones, convo.res[c]], [rbmean],
                   signal=(c == 7))
            bex2, rbex2 = bank()
            for c in range(8):
                t1, rt1 = tmp()
                ACT(act(t1[:, 0:N], convo.t[:, c, 0:N], AF.Square), [convo.res[c]], [rt1])
                PE(mm(bex2[:, 0:N], ones_f[:], t1[:, 0:N], c == 0, c == 7), [r_ones, rt1], [rbex2], signal=(c == 7))
            ACT(act(mean_t[:, 0:N], bmean[:, 0:N], AF.Copy), [rbmean], [rmean])
            DVE(tt(rstd_t[:, 0:N], mean_t[:, 0:N], mean_t[:, 0:N], ALU.mult), [rmean], [rrstd])
            DVE(tt(rstd_t[:, 0:N], bex2[:, 0:N], rstd_t[:, 0:N], ALU.subtract), [rbex2, rrstd], [rrstd])
            ACT(act(rstd_t[:, 0:N], rstd_t[:, 0:N], AF.Sqrt, bias=eps_t[:, 0:1]), [rrstd, r_eps], [rrstd])
            DVE(lambda e: e.reciprocal(out=rstd_t[:, 0:N], in_=rstd_t[:, 0:N]), [rrstd], [rrstd])

        def lna_norm(c):
            d1, rd1 = tmp()
            EW = POOL if POOL_OFF else DVE
            EW(tt(d1[:, 0:N], convo.t[:, c, 0:N], mean_t[:, 0:N], ALU.subtract), [convo.res[c], rmean], [rd1])
            EW(tt(d1[:, 0:N], d1[:, 0:N], rstd_t[:, 0:N], ALU.mult), [rd1, rrstd], [rd1])
            ACT(act(ca2.t[:, c, 0:N], d1[:, 0:N], AF.Silu, scale=colv_t[:, C_LAG + c:C_LAG + c + 1],
                    bias=colv_t[:, C_LAB + c:C_LAB + c + 1]), [rd1, r_colv], [ca2.res[c]])

        rg_state = {}

        def convb_half(h):
            sl, rs = get_diag(dB[h], 20 * 128, r_dB[h], colv_t[:, C_WDB + 20 * h:C_WDB + 20 * h + 20])
            dg = sl[:, 0:20 * 128].rearrange("p (k n) -> p k n", k=20)
            for m in range(5):
                c = 5 * h + m
                bk, rb = bank()
                for k in range(KB):
                    rhs = uview(bxb, c, 7, 3, k, k + N) if prompt else uview(bxb, c, 7, 3, k, k + 4)
                    PE(mm(bk[:, 0:N], dg[:, m * KB + k, :], rhs, k == 0, k == KB - 1), [rs, bxb.res[c]], [rb],
                       signal=(k == KB - 1))
                ACT(act(cbb.t[:, m, 0:N], bk[:, 0:N], AF.Identity, bias=colv_t[:, C_BDB + c:C_BDB + c + 1]),
                    [rb, r_colv], [cbb.res[m]])
                ACT(act(cb.t[:, m, 0:N], bk[:, 0:N], AF.Identity, bias=colv_t[:, C_BDB + c:C_BDB + c + 1]),
                    [rb, r_colv], [cb.res[m]])
            (wrg, rrg, _, _), (wbg, rbgw, _, _) = w_take("rg", h, 2)
            rg_state.update(wrg=wrg, rrg=rrg, wbg=wbg, rbgw=rbgw)

        blk_idx = {}
        blk = 0
        for g in range(2):
            for m in range(5):
                for kk in RG_BLOCKS[m]:
                    blk_idx[(g, m, kk)] = blk
                    blk += 1

        def rg_stage1(h, m):
            wrg, rrg, wbg, rbgw = rg_state["wrg"], rg_state["rrg"], rg_state["wbg"], rg_state["rbgw"]
            c = 5 * h + m
            kks = RG_BLOCKS[m]
            br, rbr = bank()
            for q, kk in enumerate(kks):
                PE(mm(br[:, 0:N], wrg[:, blk_idx[(0, m, kk)], :], cbb.t[:, kk, 0:N], q == 0, q == len(kks) - 1),
                   [rrg, cbb.res[kk]], [rbr], signal=(q == len(kks) - 1))
            bi, rbi = bank()
            for q, kk in enumerate(kks):
                PE(mm(bi[:, 0:N], wrg[:, blk_idx[(1, m, kk)], :], cbb.t[:, kk, 0:N], q == 0, q == len(kks) - 1),
                   [rrg, cbb.res[kk]], [rbi], signal=(q == len(kks) - 1))
            bgt, rbgt = bank()
            for kc in range(8):
                PE(mm(bgt[:, 0:N], wbg[:, kc, m * 128:(m + 1) * 128], xT.t[:, kc, 0:N], kc == 0, kc == 7),
                   [rbgw, xT.res[kc]], [rbgt], signal=(kc == 7))
            tr_, rtr = tmp()
            ACT(act(tr_[:, 0:N], br[:, 0:N], AF.Tanh, scale=0.5, bias=dv[:, V_HBRA + c:V_HBRA + c + 1]),
                [rbr, r_dv], [rtr])
            ti_, rti = tmp()
            ACT(act(ti_[:, 0:N], bi[:, 0:N], AF.Tanh, scale=0.5, bias=dv[:, V_HBRX + c:V_HBRX + c + 1]),
                [rbi, r_dv], [rti])
            a_, ra = tmp()
            ACT(act(a_[:, 0:N], tr_[:, 0:N], AF.Exp, scale=dv[:, V_HC + c:V_HC + c + 1],
                    bias=dv[:, V_HC + c:V_HC + c + 1]), [rtr, r_dv], [ra])
            a2_, ra2 = tr_, rtr
            if POOL_OFF:
                POOL(tt(a2_[:, 0:N], a_[:, 0:N], a_[:, 0:N], ALU.mult), [ra, rtr], [ra2])
            else:
                ACT(act(a2_[:, 0:N], tr_[:, 0:N], AF.Exp, scale=dv[:, V_C + c:V_C + c + 1],
                        bias=dv[:, V_C + c:V_C + c + 1]), [rtr, r_dv], [ra2])
            sq_, rsq = tmp()
            DVE(stt(ti_[:, 0:N], ti_[:, 0:N], 1.0, cb.t[:, m, 0:N], ALU.add, ALU.mult), [rti, cb.res[m]], [rti])
            return dict(c=c, m=m, a_=a_, ra=ra, a2_=a2_, ra2=ra2, ti_=ti_, rti=rti, sq_=sq_, rsq=rsq, bgt=bgt, rbgt=rbgt)

        def rg_stage1b(st):
            ACT(act(st["sq_"][:, 0:N], st["bgt"][:, 0:N], AF.Gelu_apprx_tanh), [st["rbgt"]], [st["rsq"]])

        def rg_stage2(st):
            c, m, a_, ra, a2_, ra2, ti_, rti, sq_, rsq = (st[k] for k in ("c", "m", "a_", "ra", "a2_", "ra2", "ti_", "rti",
                                                                         "sq_", "rsq"))
            ACT(act(a2_[:, 0:N], a2_[:, 0:N], AF.Sqrt, scale=-0.25, bias=quarter_t[:, 0:1]), [ra2, r_eps], [ra2])
            DVE(tt(a2_[:, 0:N], a2_[:, 0:N], ti_[:, 0:N], ALU.mult), [ra2, rti], [ra2])
            hh, rhh = tmp()
            if DEBUG and first:
                dbg_dump1("aa", c, a_[:, 0:N], ra)
                dbg_dump1("bt", c, a2_[:, 0:N], ra2)
            if prompt:
                if first:
                    DVE(ts1(a2_[:, 0:1], ti_[:, 0:1], 0.5, ALU.mult), [rti, ra2], [ra2])
                DVE((lambda e, hh=hh, a_=a_, a2_=a2_, c=c: e.tensor_tensor_scan(
                    out=hh[:, 0:N], data0=a_[:, 0:N], data1=a2_[:, 0:N],
                    initial=(0.0 if first else hcar[:, c:c + 1]), op0=ALU.mult, op1=ALU.add)),
                    [ra, ra2, r_hcar[c]], [rhh])
                DVE(lambda e, hh=hh, c=c: e.tensor_copy(out=hcar[:, c:c + 1], in_=hh[:, N - 1:N]), [rhh], [r_hcar[c]])
            else:
                for t_ in range(4):
                    prev = h0b.t[:, c, :] if t_ == 0 else hh[:, 16 * (t_ - 1):16 * t_]
                    rprev = [h0b.res[c]] if t_ == 0 else [rhh]
                    DVE(tt(hh[:, 16 * t_:16 * t_ + 16], a_[:, 16 * t_:16 * t_ + 16], prev, ALU.mult), [ra] + rprev, [rhh])
                    DVE(tt(hh[:, 16 * t_:16 * t_ + 16], hh[:, 16 * t_:16 * t_ + 16], a2_[:, 16 * t_:16 * t_ + 16], ALU.add),
                        [rhh, ra2], [rhh])
                DVE(lambda e, hh=hh, c=c: e.tensor_copy(out=hl.t[:, c, :], in_=hh[:, 48:64]), [rhh], [hl.res[c]])
            DVE(stt(hg2.t[:, c, 0:N], sq_[:, 0:N], 2.0, hh[:, 0:N], ALU.mult, ALU.mult), [rsq, rhh], [hg2.res[c]])
            if DEBUG and first:
                dbg_dump1("cb", c, cb.t[:, m, 0:N], cb.res[m])
                dbg_dump1("hh", c, hh[:, 0:N], rhh)

        for c in range(10):
            s2_group(c)
        if prompt and not last:
            DVE(lambda e: e.tensor_copy(out=bcar[:, :, 0:3], in_=bxb.t[:, :, NT:NT + 3]), flat(bxb.res), [r_bcar])
        if STOP == 'S3':
            return
        ca_next = [0]

        def filler(n):
            for _ in range(n):
                if ca_next[0] < 8:
                    conva_chunk(ca_next[0])
                    ca_next[0] += 1

        def rg_half(h, fill):
            for grp, nf in zip(((0, 1), (2, 3), (4,)), fill):
                sts = [rg_stage1(h, m) for m in grp]
                for st in sts:
                    rg_stage1b(st)
                filler(nf)
                for st in sts:
                    rg_stage2(st)

        convb_half(0)
        filler(1)
        rg_half(0, (2, 1, 1))
        convb_half(1)
        filler(1)
        rg_half(1, (1, 1, 0))
        filler(8)
        if prompt and not last:
            DVE(lambda e: e.tensor_copy(out=ucar[:, :, :], in_=uA.t[:, :, NT:NT + 30]), flat(uA.res), [r_ucar])
        lna_stats()
        if STOP == 'S4':
            return
        for i in range(2):
            (wpb, rpb, _, _), (wgb, rgb, _, _) = w_take("pb", i, 2)
            for m in range(4):
                c = 4 * i + m
                by, rby = bank()
                for kc in range(10):
                    PE(mm(by[:, 0:N], wpb[:, kc, m * 128:(m + 1) * 128], hg2.t[:, kc, 0:N], kc == 0, kc == 9),
                       [rpb, hg2.res[kc]], [rby], signal=(kc == 9))
                bg, rbg = bank()
                for kc in range(8):
                    PE(mm(bg[:, 0:N], wgb[:, kc, m * 128:(m + 1) * 128], xT.t[:, kc, 0:N], kc == 0, kc == 7),
                       [rgb, xT.res[kc]], [rbg], signal=(kc == 7))
                t1, rt1 = tmp()
                ACT(act(t1[:, 0:N], bg[:, 0:N], AF.Tanh, scale=0.5), [rbg], [rt1])
                DVE(stt(m_b.t[:, c, 0:N], t1[:, 0:N], 1.0, by[:, 0:N], ALU.add, ALU.mult), [rt1, rby], [m_b.res[c]])
                lna_norm(c)
        for i in range(2):
            (wpa, rpa, _, _), (wga, rga, _, _) = w_take("pa", i, 2)
            for m in range(4):
                c = 4 * i + m
                by, rby = bank()
                for kc in range(8):
                    PE(mm(by[:, 0:N], wpa[:, kc, m * 128:(m + 1) * 128], ca2.t[:, kc, 0:N], kc == 0, kc == 7),
                       [rpa, ca2.res[kc]], [rby], signal=(kc == 7))
                bg, rbg = bank()
                for kc in range(8):
                    PE(mm(bg[:, 0:N], wga[:, kc, m * 128:(m + 1) * 128], xT.t[:, kc, 0:N], kc == 0, kc == 7),
                       [rga, xT.res[kc]], [rbg], signal=(kc == 7))
                t1, rt1 = tmp()
                ACT(act(t1[:, 0:N], bg[:, 0:N], AF.Tanh, scale=0.5), [rbg], [rt1])
                DVE(stt(t1[:, 0:N], t1[:, 0:N], 1.0, by[:, 0:N], ALU.add, ALU.mult), [rt1, rby], [rt1])
                DVE(stt(mixin.t[:, c, 0:N], t1[:, 0:N], 2.0, m_b.t[:, c, 0:N], ALU.mult, ALU.add), [rt1, m_b.res[c]], [mixin.res[c]])
        if DEBUG and first:
            dbg_dump("u2", uA, 8, 30, "bf16")
            dbg_dump("convo", convo, 8, 0, "f32")
            dbg_dump("ca2", ca2, 8, 0, "bf16")
            dbg_dump("hg2", hg2, 10, 0, "bf16")
            dbg_dump("mixin", mixin, 8, 0, "bf16")

        if STOP == 'S6':
            return
        def fm_to_rows(srcbuf, nchunks, ncols, dst_rows_fn):
            for c in range(nchunks):
                bk, rb = bank()
                PE(tr(bk[0:ncols, 0:128], srcbuf.t[:, c, 0:ncols], ident_f[:, :]), [srcbuf.res[c], r_id], [rb], signal=True)
                ACT(act(stg.t[0:ncols, 0, c * 128:(c + 1) * 128], bk[0:ncols, 0:128], AF.Copy), [rb], [stg.res[0]])
            dst_rows_fn()

        if last:
            fm_to_rows(ufp, 8, 30, lambda: out_toks.append(
                S.dma("sp", lambda e: e.dma_start(out=ncap, in_=stg.t[0:30, 0, 0:D]), "ost", flat(stg.res), [])))
            fm_to_rows(bxf, 10, 3, lambda: out_toks.append(
                S.dma("sp", lambda e: e.dma_start(out=ncbp, in_=stg.t[0:3, 0, 0:DR]), "ost", flat(stg.res), [])))
            bk, rb = bank()
            PE(tr(bk[0:10, 0:128], hcar[:, 0:10], ident_f[:, :]), r_hcar + [r_id], [rb], signal=True)
            ACT(act(stg.t[0:10, 0, 0:128], bk[0:10, 0:128], AF.Copy), [rb], [stg.res[0]])
            out_toks.append(S.dma("sp", lambda e: e.dma_start(out=nhp, in_=stg.t[0:10, 0, 0:128]), "ost", flat(stg.res), []))
        if not prompt:
            fm_to_rows(ufp, 8, 64, lambda: out_toks.append(
                S.dma("sp", lambda e: e.dma_start(out=ncas_new, in_=stg.t[0:64, 0, 0:D]), "ost", flat(stg.res), [])))
            fm_to_rows(bxf, 10, 64, lambda: out_toks.append(
                S.dma("sp", lambda e: e.dma_start(out=ncbs, in_=stg.t[16:64, 0, 0:DR]), "ost", flat(stg.res), [])))
            fm_to_rows(hl, 10, 16, lambda: out_toks.append(
                S.dma("sp", lambda e: e.dma_start(out=nhs, in_=stg.t[0:16, 0, 0:DR]), "ost", flat(stg.res), [])))
            out_toks.append(S.dma("sp", lambda e: e.dma_start(out=ncas_old, in_=sca[64:480, :]), "ost2", [], []))

        if STOP == 'state':
            return
        arena.reset(offAB)
        hT = arena.alloc(32, N, BF16)
        x1T = arena.alloc(8, N, BF16)
        x1 = arena.alloc(4, D, F32)
        xfp = arena.alloc(2, D, F32)
        tg = arena.alloc(2, D, F32)
        yb = arena.alloc(2, D, F32)

        (wo0, rwo0, _, _), (wo1, rwo1, _, _) = w_take("wo", 0, 2)
        wos = ((wo0, rwo0), (wo1, rwo1))
        def wo_mm(s):
            S.dma("sp", (lambda e, s=s: e.dma_start(out=xfp.t[0:P, s % 2, :], in_=x_src[s * 128:s * 128 + P, :])),
                  "xfp%d" % (s % 2), [], [xfp.res[s % 2]])
            pr, (rb0, rb1) = bankpair()
            rbs = (rb0, rb1)
            for hf in range(2):
                for kc in range(8):
                    PE(mm(pr[0:P, hf * NT:(hf + 1) * NT], mixin.t[:, kc, s * 128:s * 128 + P], wos[hf][0][:, kc, :],
                          kc == 0, kc == 7), [wos[hf][1], mixin.res[kc]], [rbs[hf]], signal=(kc == 7))
            b = s % 2
            DVE(stt(x1.t[0:P, s, :], xfp.t[0:P, b, :], 4.0 * ALPHA, pr[0:P, :], ALU.mult, ALU.add),
                [xfp.res[b], rb0, rb1], [x1.res[s]])
            bi_ = bn_ptr[0] % 4
            bn_ptr[0] += 1
            for hf in range(2):
                DVE((lambda e, s=s, hf=hf, bi_=bi_: e.bn_stats(out=bnst[0:P, bi_, 6 * hf:6 * hf + 6],
                                                               in_=x1.t[0:P, s, hf * NT:(hf + 1) * NT])),
                    [x1.res[s]], [r_bnst[bi_]])
            mv, rmv = stat4()
            DVE(lambda e, mv=mv, bi_=bi_: e.bn_aggr(out=mv[0:P, 0:2], in_=bnst[0:P, bi_, :]), [r_bnst[bi_]], [rmv])
            DVE(ts1(mv[0:P, 2:3], mv[0:P, 1:2], 16.0 * EPS, ALU.add), [rmv], [rmv])
            POOL(tt(mv[0:P, 2:3], mv[0:P, 2:3], cnh[0:P, 0:1], ALU.pow), [rmv, r_cnh], [rmv])
            DVE(ts(x1.t[0:P, s, :], x1.t[0:P, s, :], mv[0:P, 0:1], mv[0:P, 2:3], ALU.subtract, ALU.mult),
                [x1.res[s], rmv], [x1.res[s]])

        def ln1_tr(s):
            for fc in range(8):
                bk, rb = bank()
                PE(tr(bk[:, 0:P], x1.t[0:P, s, fc * 128:(fc + 1) * 128], ident_f[0:P, 0:P]), [x1.res[s], r_id], [rb],
                   signal=True)
                ACT(act(x1T.t[:, fc, s * 128:s * 128 + P], bk[:, 0:P], AF.Identity,
                        scale=colv_t[:, C_L1G + fc:C_L1G + fc + 1], bias=colv_t[:, C_L1B + fc:C_L1B + fc + 1]),
                    [rb, r_colv], [x1T.res[fc]])
            EW = POOL if POOL_OFF else DVE
            EW(tt(x1.t[0:P, s, :], x1.t[0:P, s, :], bc_t[0:P, 0:D], ALU.mult), [x1.res[s], r_bc], [x1.res[s]])
            EW(tt(x1.t[0:P, s, :], x1.t[0:P, s, :], bc_t[0:P, D:2 * D], ALU.add), [x1.res[s], r_bc], [x1.res[s]])

        wo_mm(0)
        for s in range(1, NS):
            wo_mm(s)
            ln1_tr(s - 1)
        ln1_tr(NS - 1)
        if DEBUG and first:
            dbg_dump("x1T", x1T, 8, 0, "bf16")
            S.dma("sp", lambda e: e.dma_start(out=dbg["x1"], in_=x1.t[:, :, :]), "odbgx", flat(x1.res), [])

        if STOP == 'S7':
            return
        for i in range(8):
            ((wf, rwf, _, _),) = w_take("ff1", i, 1)
            for m in range(4):
                c = 4 * i + m
                bk, rb = bank()
                for kc in range(8):
                    PE(mm(bk[:, 0:N], wf[:, kc, m * 128:(m + 1) * 128], x1T.t[:, kc, 0:N], kc == 0, kc == 7),
                       [rwf, x1T.res[kc]], [rb], signal=(kc == 7))
                t1, rt1 = tmp()
                ACT(act(t1[:, 0:N], bk[:, 0:N], AF.Relu), [rb], [rt1])
                if c % 2 == 0:
                    ACT(act(hT.t[:, c, 0:N], t1[:, 0:N], AF.Square), [rt1], [hT.res[c]])
                else:
                    DVE(tt(hT.t[:, c, 0:N], t1[:, 0:N], t1[:, 0:N], ALU.mult), [rt1], [hT.res[c]])
        if DEBUG and first:
            dbg_dump("hT", hT, 32, 0, "bf16")

        if STOP == 'S8':
            return
        (wg0, rwg0, _, _), (wg1, rwg1, _, _), (wpj, rwpj, _, _) = w_take("pg", 0, 3)
        wgs = ((wg0, rwg0), (wg1, rwg1))
        ple = []
        for s in range(NS):
            b = s % 2
            pr, (rb0, rb1) = bankpair()
            rbs = (rb0, rb1)
            for hf in range(2):
                for kc in range(8):
                    PE(mm(pr[0:P, hf * NT:(hf + 1) * NT], x1T.t[:, kc, s * 128:s * 128 + P], wgs[hf][0][:, kc, :],
                          kc == 0, kc == 7), [wgs[hf][1], x1T.res[kc]], [rbs[hf]], signal=(kc == 7))
            ACT(act(tg.t[0:P, b, :], pr[0:P, :], AF.Tanh, scale=0.5), [rb0, rb1], [tg.res[b]])
            pr2, (rc0, rc1) = bankpair()
            rcs = (rc0, rc1)
            for hf in range(2):
                for kc in range(2):
                    PE(mm(pr2[0:P, hf * NT:(hf + 1) * NT], pT.t[:, kc, s * 128:s * 128 + P],
                          wpj[:, kc, hf * NT:(hf + 1) * NT], kc == 0, kc == 1), [rwpj, pT.res[kc]], [rcs[hf]],
                       signal=(kc == 1))
            DVE(stt(tg.t[0:P, b, :], tg.t[0:P, b, :], 1.0, pr2[0:P, :], ALU.add, ALU.mult), [tg.res[b], rc0, rc1],
                [tg.res[b]])
            DVE(ts1(x1.t[0:P, s, :], x1.t[0:P, s, :], ALPHA, ALU.mult), [x1.res[s]], [x1.res[s]])
            DVE(stt(x1.t[0:P, s, :], tg.t[0:P, b, :], 0.5, x1.t[0:P, s, :], ALU.mult, ALU.add), [tg.res[b], x1.res[s]],
                [x1.res[s]])
        if STOP == 'S9':
            return
        for hf in range(2):
            prs = []
            for s in range(NS):
                bk, rb = bank()
                prs.append((bk, rb))
            for kg in range(4):
                ((wf2, rwf2, _, _),) = w_take("ff2", hf * 4 + kg, 1)
                for s in range(NS):
                    bk, rb = prs[s]
                    for kc in range(8):
                        kglob = 8 * kg + kc
                        PE(mm(bk[0:P, :], hT.t[:, kglob, s * 128:s * 128 + P], wf2[:, kc, :], kglob == 0, kglob == 31),
                           [rwf2, hT.res[kglob]], [rb], signal=(kc == 7))
                if hf == 1 and kg == 1 and nxt_tile is not None and PREFETCH_X:
                    prefetched[nxt_tile] = early_loads(*nxt_tile)
            for s in range(NS):
                bk, rb = prs[s]
                DVE(tt(x1.t[0:P, s, hf * NT:(hf + 1) * NT], x1.t[0:P, s, hf * NT:(hf + 1) * NT], bk[0:P, :], ALU.add),
                    [x1.res[s], rb], [x1.res[s]])
        yield "ff2"
        mv4 = ln2s
        for s in range(NS):
            bi_ = bn_ptr[0] % 4
            bn_ptr[0] += 1
            for hf in range(2):
                DVE((lambda e, s=s, hf=hf, bi_=bi_: e.bn_stats(out=bnst[0:P, bi_, 6 * hf:6 * hf + 6],
                                                               in_=x1.t[0:P, s, hf * NT:(hf + 1) * NT])),
                    [x1.res[s]], [r_bnst[bi_]])
            DVE(lambda e, s=s, bi_=bi_: e.bn_aggr(out=mv4[0:P, s, 0:2], in_=bnst[0:P, bi_, :]), [r_bnst[bi_]], [r_ln2s])
        ACT(act(mv4[0:P, 0:NS, 2], mv4[0:P, 0:NS, 1], AF.Sqrt, bias=eps_t[0:P, 0:1]), [r_ln2s, r_eps], [r_ln2s])
        DVE(lambda e: e.reciprocal(out=mv4[0:P, 0:NS, 2], in_=mv4[0:P, 0:NS, 2]), [r_ln2s], [r_ln2s])
        for s in range(NS):
            b = s % 2
            DVE(ts(yb.t[0:P, b, :], x1.t[0:P, s, :], mv4[0:P, s, 0:1], mv4[0:P, s, 2:3], ALU.subtract, ALU.mult),
                [x1.res[s], r_ln2s], [yb.res[b]])
            DVE(tt(yb.t[0:P, b, :], yb.t[0:P, b, :], bc_t[0:P, 2 * D:3 * D], ALU.mult), [yb.res[b], r_bc], [yb.res[b]])
            DVE(tt(yb.t[0:P, b, :], yb.t[0:P, b, :], bc_t[0:P, 3 * D:4 * D], ALU.add), [yb.res[b], r_bc], [yb.res[b]])
            out_toks.append(S.dma("sp", (lambda e, s=s, b=b: e.dma_start(out=y_dst[s * 128:s * 128 + P, :],
                                                                         in_=yb.t[0:P, b, :])),
                                  "oy%d" % b, [yb.res[b]], []))

    dbg_tmp = {}

    def dbg_dump(name, buf, C, off, kind):
        for c in range(C):
            dbg_dump1(name, c, buf.t[:, c, off:off + NT], buf.res[c])

    def dbg_dump1(name, c, ap, res):
        ti = tmp_ptr[0] % NTMP
        t1, rt1 = tmp()
        DVE(lambda e: e.tensor_copy(out=t1[:, 0:NT], in_=ap), [res], [rt1])
        S.dma("sp", lambda e: e.dma_start(out=dbg[name][:, c, :], in_=t1[:, 0:NT]), "odbg%d" % ti, [rt1], [])

    if STOP != "setup":
        tl = (TILES if TILES is not None else [("p", 0), ("p", 1), ("p", 2), ("p", 3), ("s", 0)])
        gens = [emit_tile(kind, j, tl[q + 1] if q + 1 < len(tl) else None) for q, (kind, j) in enumerate(tl)]

        def run_to(g, tag):
            for t_ in g:
                if t_ == tag:
                    return True
            return False

        run_to(gens[0], "head")
        for q in range(len(gens)):
            alive = run_to(gens[q], "ff2")
            if q + 1 < len(gens) and HEAD_OVERLAP:
                run_to(gens[q + 1], "head")
                nk, nj = tl[q + 1]
                dpre[(nk, nj, id(r_dB[0]))] = d_load(dB[0], 20 * 128, r_dB[0])
                dpre[(nk, nj, id(r_dA[0]))] = d_load(dA[0], KA * 128, r_dA[0])
            if alive:
                run_to(gens[q], None)
            if q + 1 < len(gens) and not HEAD_OVERLAP:
                run_to(gens[q + 1], "head")

    final = list(out_toks)
    for nm, (sem, cnt) in S.dsem.items():
        if nm.startswith("o"):
            final.append(("dma", nm, cnt, sem))
    S.wait_all("sp", final)

    with nc.Block() as block:
        @block.tensor
        def _(e):
            S.replay("pe", e)

        @block.scalar
        def _(e):
            S.replay("act", e)

        @block.vector
        def _(e):
            S.replay("dve", e)

        @block.gpsimd
        def _(e):
            S.replay("pool", e)

        @block.sync
        def _(e):
            S.replay("sp", e)
    es.close()
    return nc


def _host_prep(inp):
    f = lambda a: np.ascontiguousarray(np.asarray(a, dtype=np.float32))
    colv = np.zeros((128, NCOL), np.float32)

    def put(col0, vec):
        v = np.asarray(vec, np.float32).reshape(-1, 128)
        colv[:, col0:col0 + v.shape[0]] = v.T

    put(C_BDA, inp["b_dw_a"][0]); put(C_LAG, inp["ln_a_g"][0]); put(C_LAB, inp["ln_a_b"][0])
    put(C_BDB, inp["b_dw_b"][0]); put(C_BRA, inp["b_rg_a"][0].reshape(-1)); put(C_BRX, inp["b_rg_x"][0].reshape(-1))
    put(C_LAM, inp["rg_lam"][0]); put(C_L1G, inp["ln1_g"][0]); put(C_L1B, inp["ln1_b"][0])
    wda = np.asarray(inp["w_dw_a"][0], np.float32)
    for c in range(8):
        colv[:, C_WDA + c * KA:C_WDA + (c + 1) * KA] = wda[:, c * 128:(c + 1) * 128].T
    wdb = np.asarray(inp["w_dw_b"][0], np.float32)
    for c in range(10):
        colv[:, C_WDB + c * KB:C_WDB + (c + 1) * KB] = wdb[:, c * 128:(c + 1) * 128].T
    bcv = np.zeros((128, 4 * D), np.float32)
    for i, k in enumerate(("ln1_g", "ln1_b", "ln2_g", "ln2_b")):
        bcv[:, i * D:(i + 1) * D] = np.asarray(inp[k][0], np.float32)[None, :]
    rgw = np.zeros((2, 128, 2 * NRGB, 128), np.float32)
    for g, key in enumerate(("w_rg_a", "w_rg_x")):
        wfull = np.zeros((DR, DR), np.float32)
        w = np.asarray(inp[key][0], np.float32)
        for hd in range(16):
            wfull[80 * hd:80 * hd + 80, 80 * hd:80 * hd + 80] = w[hd]
        for h in range(2):
            blk = g * NRGB
            for m in range(5):
                for kk in RG_BLOCKS[m]:
                    r0 = 640 * h + 128 * kk
                    c0 = 640 * h + 128 * m
                    rgw[h, :, blk, :] = wfull[r0:r0 + 128, c0:c0 + 128]
                    blk += 1
    rgw = rgw.reshape(2, 128, 2 * NRGB * 128)
    common = {
        "w_in": f(inp["w_in"][0]), "w_pa": f(inp["w_proj_a"][0]), "w_pb": f(inp["w_proj_b"][0]),
        "w_out": f(inp["w_out"][0]), "w_ff1": f(inp["w_ff1"][0]), "w_ff2": f(inp["w_ff2"][0]),
        "w_pg": f(inp["w_ple_gate"][0]), "w_pp": f(inp["w_ple_proj"][0]),
        "rgw": np.ascontiguousarray(rgw), "colv": colv, "bcv": bcv,
    }
    maps = []
    for i in range(NCORES):
        m = dict(common)
        m["xp"] = f(inp["x_prompt"][i])
        m["xs"] = f(np.asarray(inp["x_sample"])[16 * i:16 * i + 16].transpose(1, 0, 2).reshape(64, D))
        m["ppr"] = f(inp["p_prompt"][0][i])
        m["psm"] = f(np.asarray(inp["p_sample"][0])[16 * i:16 * i + 16].transpose(1, 0, 2).reshape(64, DPL))
        m["sca"] = f(np.asarray(inp["state_conv_a"][0])[16 * i:16 * i + 16].transpose(1, 0, 2).reshape(480, D))
        m["scb"] = f(np.asarray(inp["state_conv_b"][0])[16 * i:16 * i + 16].transpose(1, 0, 2).reshape(48, DR))
        m["sh"] = f(np.asarray(inp["state_h"][0])[16 * i:16 * i + 16])
        maps.append(m)
    return maps


_NC_CACHE = {}


def kernel(**inputs):
    maps = _host_prep(inputs)
    if "nc" not in _NC_CACHE:
        _NC_CACHE["nc"] = build_nc()
    nc = _NC_CACHE["nc"]
    res = run_bass_kernel_spmd(nc, maps, core_ids=list(range(NCORES)))
    R = res.results
    y_p = np.stack([R[i]["yp"] for i in range(NCORES)], 0).astype(np.float32)
    y_s = np.concatenate([R[i]["ys"].reshape(4, 16, D).transpose(1, 0, 2) for i in range(NCORES)], 0).astype(np.float32)
    ca_p = np.stack([R[i]["ncap"] for i in range(NCORES)], 0)[None].astype(np.float32)
    cb_p = np.stack([R[i]["ncbp"] for i in range(NCORES)], 0)[None].astype(np.float32)
    h_p = np.stack([R[i]["nhp"].reshape(DR) for i in range(NCORES)], 0)[None].astype(np.float32)
    ca_s = np.concatenate([np.concatenate([R[i]["ncas_old"].reshape(26, 16, D), R[i]["ncas_new"].reshape(4, 16, D)], 0)
                           .transpose(1, 0, 2) for i in range(NCORES)], 0)[None].astype(np.float32)
    cb_s = np.concatenate([R[i]["ncbs"].reshape(3, 16, DR).transpose(1, 0, 2) for i in range(NCORES)], 0)[None].astype(np.float32)
    h_s = np.concatenate([R[i]["nhs"] for i in range(NCORES)], 0)[None].astype(np.float32)
    if DEBUG:
        kernel.debug = R
    return (y_p, y_s, ca_p, cb_p, h_p, ca_s, cb_s, h_s)
```
